# Optimizing a Trainium2 kernel written in Bass

```python
import math
import jax, jax.numpy as jnp
from jax import lax
import numpy as np

D_MODEL = 1024
BATCH = 16
SEQ = 2048
DEPTH = 4
DEC_BATCH = 16
DEC_SEQ = 4096
PAST_LEN = 128

GLA_HEADS = 4
GLA_DK = 64
GLA_DV = 128
GLA_RANK = 16
GLA_GATE_NORM = 16.0
DIFF_HEADS = 4
DIFF_DH = 64
DIFF_DV = 2 * DIFF_DH
Q_BLOCK = 128
ROPE_THETA = 500000.0
ROPE_DIM = DIFF_DH // 4
SSD_HEADS = 8
SSD_P = 64
SSD_GROUPS = 2
SSD_E = SSD_HEADS // SSD_GROUPS
SSD_N = 64
SSD_CONV = 5
CHUNK = 64
D_FF = 4 * D_MODEL
EPS = 1e-6

GLA_QK = GLA_HEADS * GLA_DK
GLA_V = GLA_HEADS * GLA_DV
DIFF_QK = DIFF_HEADS * 2 * DIFF_DH
DIFF_V = DIFF_HEADS * DIFF_DV
SSD_INNER = SSD_HEADS * SSD_P
SSD_BC = SSD_GROUPS * SSD_N
SSD_XBC = SSD_INNER + 2 * SSD_BC
MIX_WIDTH = GLA_V + DIFF_V + SSD_INNER
IN_SIZES = (GLA_QK, GLA_QK, GLA_V, GLA_V, 2 * GLA_RANK,
            DIFF_QK, DIFF_QK, DIFF_V,
            SSD_INNER, SSD_XBC, 2 * SSD_HEADS)
IN_WIDTH = sum(IN_SIZES)
IN_OFFSETS = tuple(int(v) for v in np.cumsum(IN_SIZES)[:-1])

kernel_name = "hybrid_bidir_gla_diff_ssd_encoder"


def rmsnorm(x, g):
    xf = x.astype(jnp.float32)
    y = xf * lax.rsqrt(jnp.mean(xf * xf, axis=-1, keepdims=True) + EPS)
    return (y * g.astype(jnp.float32)).astype(x.dtype)


def flip(t):
    return jnp.flip(t, axis=1)


def to_chunks(t):
    b, s = t.shape[:2]
    return t.reshape(b, s // CHUNK, CHUNK, *t.shape[2:]).swapaxes(0, 1)


def from_chunks(t):
    nc, b, c = t.shape[:3]
    return t.swapaxes(0, 1).reshape(b, nc * c, *t.shape[3:])


def rope_tables(s):
    inv = ROPE_THETA ** (-jnp.arange(0, ROPE_DIM, 2, dtype=jnp.float32) / ROPE_DIM)
    ang = jnp.arange(s, dtype=jnp.float32)[:, None] * inv[None, :]
    ang = jnp.concatenate([ang, ang], axis=-1)
    return jnp.cos(ang), jnp.sin(ang)


def apply_partial_rope(t, cos, sin):
    c = cos[:, None, None, :].astype(t.dtype)
    s = sin[:, None, None, :].astype(t.dtype)
    tr, tp = t[..., :ROPE_DIM], t[..., ROPE_DIM:]
    t1, t2 = tr[..., :ROPE_DIM // 2], tr[..., ROPE_DIM // 2:]
    rot = jnp.concatenate([-t2, t1], axis=-1)
    return jnp.concatenate([tr * c + rot * s, tp], axis=-1)


def gla_scan(q, k, v, logg):
    q, k, v, logg = (t.astype(jnp.float32) for t in (q, k, v, logg))
    b_, _, h, dk = q.shape
    dv = v.shape[-1]
    mask = jnp.tril(jnp.ones((CHUNK, CHUNK), dtype=bool))

    def step(state, inp):
        qi, ki, vi, gi = inp
        b = jnp.cumsum(gi, axis=1)
        diff = b[:, :, None] - b[:, None, :]
        decay = jnp.exp(jnp.where(mask[:, :, None, None], diff, -jnp.inf))
        attn = jnp.einsum('bihd,bjhd,bijhd->bijh', qi, ki, decay)
        o = (jnp.einsum('bijh,bjhe->bihe', attn, vi)
             + jnp.einsum('bihd,bhde->bihe', qi * jnp.exp(b), state))
        b_last = b[:, -1]
        state = (state * jnp.exp(b_last)[..., None]
                 + jnp.einsum('bjhd,bjhe->bhde', ki * jnp.exp(b_last[:, None] - b), vi))
        return state, o

    state0 = jnp.zeros((b_, h, dk, dv), jnp.float32)
    _, o = lax.scan(step, state0, (to_chunks(q), to_chunks(k), to_chunks(v), to_chunks(logg)))
    return from_chunks(o)


def gla_mixer(gq, gk, gv, gg, glr, wg_f, bg_f, wg_b, bg_b, norm_g):
    b_, s, _ = gq.shape
    q = gq.reshape(b_, s, GLA_HEADS, GLA_DK) * (GLA_DK ** -0.5)
    k = gk.reshape(b_, s, GLA_HEADS, GLA_DK)
    v = gv.reshape(b_, s, GLA_HEADS, GLA_DV)
    r_f, r_b = glr[..., :GLA_RANK], glr[..., GLA_RANK:]
    logg_f = (jax.nn.log_sigmoid((r_f @ wg_f + bg_f).astype(jnp.float32)) / GLA_GATE_NORM
              ).reshape(b_, s, GLA_HEADS, GLA_DK)
    logg_b = (jax.nn.log_sigmoid((r_b @ wg_b + bg_b).astype(jnp.float32)) / GLA_GATE_NORM
              ).reshape(b_, s, GLA_HEADS, GLA_DK)
    o = gla_scan(q, k, v, logg_f) + flip(gla_scan(flip(q), flip(k), flip(v), flip(logg_b)))
    o = rmsnorm(o.astype(gv.dtype), norm_g)
    o = o * jax.nn.silu(gg.reshape(b_, s, GLA_HEADS, GLA_DV))
    return o.reshape(b_, s, GLA_V)


def diff_mixer(dq, dk, dvv, qn, kn, lq1, lk1, lq2, lk2, subln, lambda_init, cos, sin):
    b_, s, _ = dq.shape
    q = rmsnorm(dq.reshape(b_, s, DIFF_HEADS, 2, DIFF_DH), qn)
    k = rmsnorm(dk.reshape(b_, s, DIFF_HEADS, 2, DIFF_DH), kn)
    q = apply_partial_rope(q, cos, sin)
    k = apply_partial_rope(k, cos, sin)
    v = dvv.reshape(b_, s, DIFF_HEADS, DIFF_DV)
    lam = (jnp.exp(jnp.sum(lq1.astype(jnp.float32) * lk1.astype(jnp.float32)))
           - jnp.exp(jnp.sum(lq2.astype(jnp.float32) * lk2.astype(jnp.float32)))
           + lambda_init)
    scale = DIFF_DH ** -0.5
    nq = s // Q_BLOCK
    qb = q.reshape(b_, nq, Q_BLOCK, DIFF_HEADS, 2, DIFF_DH).transpose(1, 0, 3, 4, 2, 5)
    kt = k.transpose(0, 2, 3, 1, 4)
    vt = v.transpose(0, 2, 1, 3)

    def block(qblk):
        sc = jnp.einsum('bhcqd,bhckd->bhcqk', qblk, kt).astype(jnp.float32) * scale
        p = jax.nn.softmax(sc, axis=-1)
        w = p[:, :, 0] - lam * p[:, :, 1]
        return jnp.einsum('bhqk,bhkv->bhqv', w.astype(vt.dtype), vt)

    o = lax.map(block, qb)
    o = o.transpose(1, 0, 3, 2, 4).reshape(b_, s, DIFF_HEADS, DIFF_DV)
    o = rmsnorm(o, subln) * (1.0 - lambda_init)
    return o.reshape(b_, s, DIFF_V)


def ssd_scan(x, dt, a_neg, bm, cm):
    x, dt, bm, cm = (t.astype(jnp.float32) for t in (x, dt, bm, cm))
    la = dt * a_neg.astype(jnp.float32)
    b_ = x.shape[0]
    mask = jnp.tril(jnp.ones((CHUNK, CHUNK), dtype=bool))

    def step(state, inp):
        xi, dti, lai, bi, ci = inp
        acum = jnp.cumsum(lai, axis=1)
        seg = acum[:, :, None] - acum[:, None, :]
        lmat = jnp.exp(jnp.where(mask[:, :, None, None], seg, -jnp.inf))
        cb = jnp.einsum('bign,bjgn->bijg', ci, bi)
        m = cb[..., None] * lmat * dti[:, None]
        y = (jnp.einsum('bijge,bjgep->bigep', m, xi)
             + jnp.einsum('bign,bgepn->bigep', ci, state) * jnp.exp(acum)[..., None])
        w = jnp.exp(acum[:, -1:] - acum) * dti
        state = (state * jnp.exp(acum[:, -1])[..., None, None]
                 + jnp.einsum('bjge,bjgep,bjgn->bgepn', w, xi, bi))
        return state, y

    state0 = jnp.zeros((b_, SSD_GROUPS, SSD_E, SSD_P, SSD_N), jnp.float32)
    _, y = lax.scan(step, state0, (to_chunks(x), to_chunks(dt), to_chunks(la),
                                   to_chunks(bm), to_chunks(cm)))
    return from_chunks(y)


def ssd_mixer(sz, sxbc, sdt, conv_w, conv_b, dtb_f, dtb_b, alog_f, alog_b, d_skip, norm_g):
    b_, s, _ = sz.shape
    pad = SSD_CONV // 2
    xbc = lax.conv_general_dilated(sxbc, conv_w[:, None, :], window_strides=(1,),
                                   padding=[(pad, pad)],
                                   dimension_numbers=('NWC', 'WIO', 'NWC'),
                                   feature_group_count=SSD_XBC)
    xbc = jax.nn.silu(xbc + conv_b)
    xs = xbc[..., :SSD_INNER].reshape(b_, s, SSD_GROUPS, SSD_E, SSD_P)
    bm = xbc[..., SSD_INNER:SSD_INNER + SSD_BC].reshape(b_, s, SSD_GROUPS, SSD_N)
    cm = xbc[..., SSD_INNER + SSD_BC:].reshape(b_, s, SSD_GROUPS, SSD_N)
    dt_f = jax.nn.softplus(sdt[..., :SSD_HEADS] + dtb_f).reshape(b_, s, SSD_GROUPS, SSD_E)
    dt_b = jax.nn.softplus(sdt[..., SSD_HEADS:] + dtb_b).reshape(b_, s, SSD_GROUPS, SSD_E)
    a_f = -jnp.exp(alog_f).reshape(SSD_GROUPS, SSD_E)
    a_b = -jnp.exp(alog_b).reshape(SSD_GROUPS, SSD_E)
    y = (ssd_scan(xs, dt_f, a_f, bm, cm)
         + flip(ssd_scan(flip(xs), flip(dt_b), a_b, flip(bm), flip(cm)))).astype(xs.dtype)
    y = y + d_skip.reshape(SSD_GROUPS, SSD_E)[:, :, None] * xs
    y = y.reshape(b_, s, SSD_INNER)
    return rmsnorm(y * jax.nn.silu(sz), norm_g)


def trunk(x, norm1, w_in, gla_wg_f, gla_bg_f, gla_wg_b, gla_bg_b, gla_norm,
          diff_qnorm, diff_knorm, diff_lq1, diff_lk1, diff_lq2, diff_lk2, diff_subln,
          ssd_conv_w, ssd_conv_b, ssd_dt_bias_f, ssd_dt_bias_b, ssd_A_log_f, ssd_A_log_b,
          ssd_D, ssd_norm, w_out, norm2, w_mlp1, w_mlp2):
    s = x.shape[1]
    cos, sin = rope_tables(s)
    for l in range(DEPTH):
        h = rmsnorm(x, norm1[l])
        (gq, gk, gv, gg, glr, dq, dk, dvv, sz, sxbc, sdt) = jnp.split(h @ w_in[l], IN_OFFSETS, axis=-1)
        o_gla = gla_mixer(gq, gk, gv, gg, glr, gla_wg_f[l], gla_bg_f[l], gla_wg_b[l], gla_bg_b[l],
                          gla_norm[l])
        lambda_init = 0.8 - 0.6 * math.exp(-0.3 * l)
        o_diff = diff_mixer(dq, dk, dvv, diff_qnorm[l], diff_knorm[l], diff_lq1[l], diff_lk1[l],
                            diff_lq2[l], diff_lk2[l], diff_subln[l], lambda_init, cos, sin)
        o_ssd = ssd_mixer(sz, sxbc, sdt, ssd_conv_w[l], ssd_conv_b[l], ssd_dt_bias_f[l],
                          ssd_dt_bias_b[l], ssd_A_log_f[l], ssd_A_log_b[l], ssd_D[l], ssd_norm[l])
        x = x + jnp.concatenate([o_gla, o_diff, o_ssd], axis=-1) @ w_out[l]
        h = rmsnorm(x, norm2[l])
        x = x + jnp.square(jax.nn.relu(h @ w_mlp1[l])) @ w_mlp2[l]
    return x


def setup_inputs(seed: int = 0) -> dict:
    key = jax.random.key(seed)
    ks = jax.random.split(key, 32)
    f32 = jnp.float32
    L = DEPTH

    def nrm(k, shape, scale):
        return jax.random.normal(k, shape, f32) * scale

    def gain(k, shape):
        return 1.0 + nrm(k, shape, 0.02)

    def dt_bias(k):
        dt = jnp.exp(jax.random.uniform(k, (L, SSD_HEADS), f32, math.log(1e-3), math.log(1e-1)))
        return dt + jnp.log(-jnp.expm1(-dt))

    def a_log(k):
        return jnp.log(jax.random.uniform(k, (L, SSD_HEADS), f32, 1.0, 16.0))

    return {
        "x_prompt": jax.random.normal(ks[0], (BATCH, SEQ, D_MODEL), f32),
        "x_sample": jax.random.normal(ks[1], (DEC_BATCH, DEC_SEQ, D_MODEL), f32),
        "norm1": gain(ks[2], (L, D_MODEL)),
        "w_in": nrm(ks[3], (L, D_MODEL, IN_WIDTH), D_MODEL ** -0.5),
        "gla_wg_f": nrm(ks[4], (L, GLA_RANK, GLA_QK), GLA_RANK ** -0.5),
        "gla_bg_f": nrm(ks[5], (L, GLA_QK), 0.1),
        "gla_wg_b": nrm(ks[6], (L, GLA_RANK, GLA_QK), GLA_RANK ** -0.5),
        "gla_bg_b": nrm(ks[7], (L, GLA_QK), 0.1),
        "gla_norm": gain(ks[8], (L, GLA_DV)),
        "diff_qnorm": gain(ks[9], (L, DIFF_DH)),
        "diff_knorm": gain(ks[10], (L, DIFF_DH)),
        "diff_lq1": nrm(ks[11], (L, DIFF_DH), 0.1),
        "diff_lk1": nrm(ks[12], (L, DIFF_DH), 0.1),
        "diff_lq2": nrm(ks[13], (L, DIFF_DH), 0.1),
        "diff_lk2": nrm(ks[14], (L, DIFF_DH), 0.1),
        "diff_subln": gain(ks[15], (L, DIFF_DV)),
        "ssd_conv_w": nrm(ks[16], (L, SSD_CONV, SSD_XBC), SSD_CONV ** -0.5),
        "ssd_conv_b": nrm(ks[17], (L, SSD_XBC), 0.02),
        "ssd_dt_bias_f": dt_bias(ks[18]),
        "ssd_dt_bias_b": dt_bias(ks[19]),
        "ssd_A_log_f": a_log(ks[20]),
        "ssd_A_log_b": a_log(ks[21]),
        "ssd_D": gain(ks[22], (L, SSD_HEADS)),
        "ssd_norm": gain(ks[23], (L, SSD_INNER)),
        "w_out": nrm(ks[24], (L, MIX_WIDTH, D_MODEL), MIX_WIDTH ** -0.5),
        "norm2": gain(ks[25], (L, D_MODEL)),
        "w_mlp1": nrm(ks[26], (L, D_MODEL, D_FF), D_MODEL ** -0.5),
        "w_mlp2": nrm(ks[27], (L, D_FF, D_MODEL), D_FF ** -0.5),
    }


def reference(x_prompt, x_sample, norm1, w_in, gla_wg_f, gla_bg_f, gla_wg_b, gla_bg_b, gla_norm,
              diff_qnorm, diff_knorm, diff_lq1, diff_lk1, diff_lq2, diff_lk2, diff_subln,
              ssd_conv_w, ssd_conv_b, ssd_dt_bias_f, ssd_dt_bias_b, ssd_A_log_f, ssd_A_log_b,
              ssd_D, ssd_norm, w_out, norm2, w_mlp1, w_mlp2):
    weights = (norm1, w_in, gla_wg_f, gla_bg_f, gla_wg_b, gla_bg_b, gla_norm,
               diff_qnorm, diff_knorm, diff_lq1, diff_lk1, diff_lq2, diff_lk2, diff_subln,
               ssd_conv_w, ssd_conv_b, ssd_dt_bias_f, ssd_dt_bias_b, ssd_A_log_f, ssd_A_log_b,
               ssd_D, ssd_norm, w_out, norm2, w_mlp1, w_mlp2)
    y_prompt = trunk(x_prompt, *weights)
    y_sample = trunk(x_sample, *weights)
    return (y_prompt, y_sample)
```

```python
import math
from contextlib import ExitStack

import numpy as np
import concourse.bass as bass
import concourse.mybir as mybir
from concourse.bass_utils import run_bass_kernel_spmd

F32 = mybir.dt.float32
BF16 = mybir.dt.bfloat16
AF = mybir.ActivationFunctionType
ALU = mybir.AluOpType
AX = mybir.AxisListType

D = 1024
DEPTH = 4
INW = 4400
MIXW = 1536
DFF = 4096
EPS = 1e-6
O_GQ, O_GK, O_GV, O_GG, O_LR = 0, 256, 512, 1024, 1536
O_DQ, O_DK, O_DV = 1568, 2080, 2592
O_SZ, O_SX, O_SDT = 3104, 3616, 4384
CH = 64


class Buf:
    __slots__ = ("name", "w", "rs")

    def __init__(self, name=""):
        self.name = name
        self.w = None
        self.rs = []


class Sched:
    CE = ("pe", "dve", "act", "pool")

    def __init__(self, nc, dma_ring=8):
        self.nc = nc
        self.streams = {e: [] for e in ("pe", "dve", "act", "pool", "sp")}
        self.cnt = {e: 0 for e in self.CE}
        self.waited = {e: {} for e in self.streams}
        self.dma_ring = dma_ring
        self.dma_i = {"sp": 0, "pool": 0, "act": 0}
        self.ninst = 0

    def _need(self, eng, tok):
        key, val = tok
        if self.waited[eng].get(key, 0) >= val:
            return None
        self.waited[eng][key] = val
        return tok

    def _track(self, tok, reads, writes):
        for b in writes:
            b.w = tok
            b.rs = []
        for b in reads:
            if b not in writes:
                b.rs.append(tok)
                if len(b.rs) > 24:
                    m = {}
                    for k, v in b.rs:
                        m[k] = max(m.get(k, 0), v)
                    b.rs = list(m.items())

    def _deps(self, reads, writes, extra):
        deps = list(extra)
        for b in reads:
            if b.w is not None:
                deps.append(b.w)
        for b in writes:
            if b.w is not None:
                deps.append(b.w)
            deps.extend(b.rs)
        return deps

    def emit(self, eng, fn, reads=(), writes=(), signal=True, extra=()):
        wm = {}
        for t in self._deps(reads, writes, extra):
            if t[0] == eng and (eng == "pe" or t[1] > self.cnt[eng]):
                continue
            t = self._need(eng, t)
            if t is not None:
                wm[t[0]] = max(wm.get(t[0], 0), t[1])
        if signal:
            self.cnt[eng] += 1
            tok = (eng, self.cnt[eng])
        else:
            tok = (eng, self.cnt[eng] + 1)
        self.streams[eng].append((tuple(wm.items()), fn, tok if signal else None))
        self.ninst += 1
        self._track(tok, reads, writes)
        return tok

    def dma(self, q, fn, reads=(), writes=(), extra=()):
        i = self.dma_i[q]
        self.dma_i[q] += 1
        key = f"dma_{q}_{i % self.dma_ring}"
        val = 16 * (i // self.dma_ring + 1)
        deps = self._deps(reads, writes, extra)
        if i >= self.dma_ring:
            deps.append((key, val - 16))
        wm = {}
        for t in deps:
            t = self._need(q, t)
            if t is not None:
                wm[t[0]] = max(wm.get(t[0], 0), t[1])
        tok = (key, val)
        self.streams[q].append((tuple(wm.items()), fn, tok))
        self.ninst += 1
        self._track(tok, reads, writes)
        return tok

    def finish(self, bufs):
        wm = {}
        for b in bufs:
            for t in ([b.w] if b.w is not None else []) + list(b.rs):
                t = self._need("sp", t)
                if t is not None:
                    wm[t[0]] = max(wm.get(t[0], 0), t[1])
        self.streams["sp"].append((tuple(wm.items()), lambda e: e.nop(), None))

    def replay(self):
        nc = self.nc
        keys = set()
        for st in self.streams.values():
            for waits, fn, tok in st:
                for k, v in waits:
                    keys.add(k)
                if tok is not None:
                    keys.add(tok[0])
        with ExitStack() as es:
            sems = {k: es.enter_context(nc.semaphore(k)) for k in sorted(keys)}
            block = es.enter_context(nc.Block())

            def run(engobj, st):
                for waits, fn, tok in st:
                    for k, v in waits:
                        engobj.wait_ge(sems[k], v)
                    ins = fn(engobj)
                    if tok is not None:
                        ins.then_inc(sems[tok[0]], 16 if tok[0].startswith("dma_") else 1)

            @block.sync
            def _(e):
                run(e, self.streams["sp"])

            @block.tensor
            def _(e):
                run(e, self.streams["pe"])

            @block.vector
            def _(e):
                run(e, self.streams["dve"])

            @block.scalar
            def _(e):
                run(e, self.streams["act"])

            @block.gpsimd
            def _(e):
                run(e, self.streams["pool"])


class T:
    def __init__(self, nc, name, shape, dtype, psum=False):
        self.h = (nc.alloc_psum_tensor if psum else nc.alloc_sbuf_tensor)(name, list(shape), dtype)
        self.b = Buf(name)

    def __getitem__(self, k):
        return self.h[k]


class V(T):
    def __init__(self, ap, b):
        self.h = ap
        self.b = b


def lambda_init(l):
    return 0.8 - 0.6 * math.exp(-0.3 * l)


WNAMES = ["w_in", "w_out", "w_mlp1", "w_mlp2"]
WSHAPES = {"w_in": (D, INW), "w_out": (MIXW, D), "w_mlp1": (D, DFF), "w_mlp2": (DFF, D)}
SMALL = {"norm1": (D,), "gla_wg_f": (16, 256), "gla_bg_f": (256,), "gla_wg_b": (16, 256), "gla_bg_b": (256,),
         "gla_norm": (128,), "diff_qnorm": (64,), "diff_knorm": (64,), "diff_lq1": (64,), "diff_lk1": (64,),
         "diff_lq2": (64,), "diff_lk2": (64,), "diff_subln": (128,), "ssd_conv_w": (5, 768), "ssd_conv_b": (768,),
         "ssd_dt_bias_f": (8,), "ssd_dt_bias_b": (8,), "ssd_A_log_f": (8,), "ssd_A_log_b": (8,), "ssd_D": (8,),
         "ssd_norm": (512,), "norm2": (D,)}


def host_consts(smax):
    ident = np.eye(128, dtype=np.float32)
    bones = np.zeros((128, 128), np.float32)
    bones[:64, :64] = 1.0 / 64
    bones[64:, 64:] = 1.0 / 64
    rmat = np.zeros((128, 128), np.float32)
    for blk in (0, 64):
        for d in range(8):
            rmat[blk + d + 8, blk + d] = -1.0
            rmat[blk + d, blk + d + 8] = 1.0
    inv = (500000.0 ** (-np.arange(0, 16, 2, dtype=np.float32) / np.float32(16))).astype(np.float32)
    ang = (np.arange(smax, dtype=np.float32)[:, None] * inv[None, :]).astype(np.float32)
    cos = np.ones((128, smax), np.float32)
    sin = np.zeros((128, smax), np.float32)
    for blk in (0, 64):
        for d in range(16):
            cos[blk + d] = np.cos(ang[:, d % 8])
            sin[blk + d] = np.sin(ang[:, d % 8])
    tri = np.zeros((64, 2, 64), np.float32)
    jj, ii = np.meshgrid(np.arange(64), np.arange(64), indexing="ij")
    tri[:, 0, :] = (jj <= ii)
    tri[:, 1, :] = (jj >= ii)
    hsel = np.zeros((16, 2), np.float32)
    hsel[:8, 0] = 1.0
    hsel[8:, 1] = 1.0
    return {"c_ident": ident, "c_bones": bones, "c_rmat": rmat, "c_cos": cos, "c_sin": sin, "c_tri": tri,
            "c_hsel": hsel}


def pack_smalls(p, L):
    cols = {}
    parts = []
    off = 0

    def add(name, arr):
        nonlocal off
        arr = np.ascontiguousarray(arr, dtype=np.float32)
        a = np.zeros((128, int(np.prod(arr.shape[1:]))), np.float32)
        a[:arr.shape[0]] = arr.reshape(arr.shape[0], -1)
        cols[name] = (off, a.shape[1])
        parts.append(a)
        off += a.shape[1]

    add("g1", p["norm1"].reshape(L, 8, 128).transpose(2, 0, 1))
    add("g2", p["norm2"].reshape(L, 8, 128).transpose(2, 0, 1))
    add("wg", np.stack([p["gla_wg_f"], p["gla_wg_b"]], 1).transpose(2, 0, 1, 3))
    add("bg", np.stack([p["gla_bg_f"], p["gla_bg_b"]], 1).reshape(L, 2, 2, 128).transpose(3, 0, 1, 2))
    add("qn", np.tile(p["diff_qnorm"], (1, 2)).T)
    add("kn", np.tile(p["diff_knorm"], (1, 2)).T)
    lql = np.stack([p["diff_lq1"], p["diff_lk1"], p["diff_lq2"], p["diff_lk2"]], 1)
    add("lql", np.broadcast_to(lql[None], (128, L, 4, 64)))
    add("dtb", np.concatenate([p["ssd_dt_bias_f"], p["ssd_dt_bias_b"]], 1).T)
    add("alog", np.concatenate([p["ssd_A_log_f"], p["ssd_A_log_b"]], 1).T)
    add("convw", p["ssd_conv_w"].reshape(L, 5, 6, 128).transpose(3, 0, 2, 1))
    add("convb", p["ssd_conv_b"].reshape(L, 6, 128).transpose(2, 0, 1))
    add("glan", np.broadcast_to(p["gla_norm"][None], (128, L, 128)))
    add("subln", np.broadcast_to(p["diff_subln"][None], (128, L, 128)))
    add("ssdn", np.broadcast_to(p["ssd_norm"][None], (128, L, 512)))
    add("ssdD", np.broadcast_to(p["ssd_D"][None], (128, L, 8)))
    return np.concatenate(parts, 1), cols


def smalls_cols(L):
    dummy = {k: np.zeros((L,) + v, np.float32) for k, v in SMALL.items()}
    return pack_smalls(dummy, L)[1]


class Prog:
    def __init__(self, seq_groups, depth=DEPTH, TT=256, dbg=False, mixers=("gla", "diff", "ssd")):
        self.L = depth
        self.TT = TT
        self.dbg = dbg
        self.mixers = mixers
        self.groups = seq_groups
        self.smax = max(g[3] for g in seq_groups)
        nc = self.nc = bass.Bass("TRN2", target_bir_lowering=False)
        self.S = Sched(nc)
        L = depth
        skind = "ExternalOutput" if dbg else "Internal"
        self.xin, self.yout = {}, {}
        for (iname, oname, n, S) in seq_groups:
            self.xin[iname] = nc.dram_tensor(iname, [n, S, D], F32, kind="ExternalInput").ap()
            self.yout[oname] = nc.dram_tensor(oname, [n, S, D], F32, kind="ExternalOutput").ap()
        self.wf = {k: nc.dram_tensor(k, [L] + list(WSHAPES[k]), F32, kind="ExternalInput").ap() for k in WNAMES}
        self.wb = {k: nc.dram_tensor("b_" + k, [L] + list(WSHAPES[k]), BF16, kind="Internal").ap() for k in WNAMES}
        self.wb_buf = {k: [Buf() for _ in range(L)] for k in WNAMES}
        self.scols = smalls_cols(L)
        nsm = sum(v[1] for v in self.scols.values())
        self.smalls_d = nc.dram_tensor("smalls", [128, nsm], F32, kind="ExternalInput").ap()
        self.cd = {k: nc.dram_tensor(k, [128, 128], F32, kind="ExternalInput").ap()
                   for k in ("c_ident", "c_bones", "c_rmat")}
        self.cd["c_tri"] = nc.dram_tensor("c_tri", [64, 2, 64], F32, kind="ExternalInput").ap()
        self.cd["c_hsel"] = nc.dram_tensor("c_hsel", [16, 2], F32, kind="ExternalInput").ap()
        self.cd["c_cos"] = nc.dram_tensor("c_cos", [128, self.smax], F32, kind="ExternalInput").ap()
        self.cd["c_sin"] = nc.dram_tensor("c_sin", [128, self.smax], F32, kind="ExternalInput").ap()
        sm = self.smax
        def scr(name, shape, dt):
            return nc.dram_tensor(name, list(shape), dt, kind=skind).ap(), Buf(name)
        self.s_dq, self.b_dq = scr("s_dq", [4, 128, sm], BF16)
        self.s_dk, self.b_dk = scr("s_dk", [4, 128, sm], BF16)
        self.s_dv, self.b_dv = scr("s_dv", [sm, 512], BF16)
        self.s_gq, self.b_gq = scr("s_gq", [2, 128, sm], BF16)
        self.s_gk, self.b_gk = scr("s_gk", [2, 128, sm], BF16)
        self.s_gv, self.b_gv = scr("s_gv", [sm, 512], BF16)
        self.s_gg, self.b_gg = scr("s_gg", [sm, 512], BF16)
        self.s_gb, self.b_gb = scr("s_gb", [2, 2, 128, sm], F32)
        self.s_sx, self.b_sx = scr("s_sx", [6, 128, sm], BF16)
        self.s_sz, self.b_sz = scr("s_sz", [sm, 512], BF16)
        self.s_dt, self.b_dt = scr("s_dt", [2, 16, sm], F32)
        self.s_o, self.b_o = scr("s_o", [sm, MIXW], BF16)
        self.s_gof, self.b_gof = scr("s_gof", [sm, 512], F32)
        self.s_sxt, self.b_sxt = scr("s_sxt", [sm, 640], BF16)
        self.s_sbc, self.b_sbc = scr("s_sbc", [2, 128, sm], BF16)
        self.s_acd, self.b_acd = scr("s_acd", [16, sm], F32)
        self.s_yfd, self.b_yfd = scr("s_yfd", [sm, 512], F32)
        self._alloc()

    def op(self, eng, name, reads, writes, *args, signal=True, **kw):
        return self.S.emit(eng, lambda e: getattr(e, name)(*args, **kw),
                           reads=[t.b if isinstance(t, T) else t for t in reads],
                           writes=[t.b if isinstance(t, T) else t for t in writes], signal=signal)

    def dma(self, q, out, in_, reads, writes, **kw):
        return self.S.dma(q, lambda e: e.dma_start(out=out, in_=in_, **kw),
                          reads=[t.b if isinstance(t, T) else t for t in reads],
                          writes=[t.b if isinstance(t, T) else t for t in writes])

    def sm(self, name):
        o, n = self.scols[name]
        return self.smalls[:, o:o + n]

    def _alloc(self):
        nc, L, TT = self.nc, self.L, self.TT
        NS = TT // 128
        self.NS = NS
        t = lambda name, shape, dt: T(nc, name, shape, dt)
        nsm = sum(v[1] for v in self.scols.values())
        self.smalls = t("smalls_sb", [128, nsm], F32)
        self.ident = t("ident", [128, 128], BF16)
        self.bones = t("bones", [128, 128], BF16)
        self.rmat = t("rmat", [128, 128], BF16)
        self.cst_f = t("cst_f", [128, 3, 128], F32)
        self.wgb = t("wgb", [16, L * 2 * 256], BF16)
        self.nbg = t("nbg", [128, L * 4], F32)
        self.qg = t("qg", [128, L], F32)
        self.aneg = t("aneg", [16, L], F32)
        self.lam = t("lam", [128, L], F32)
        self.lamtmp = t("lamtmp", [128, 4 * 64], F32)
        self.lam2 = t("lam2", [128, 4], F32)
        self.x_sb = t("x_sb", [128, NS, D], F32)
        self.xn = t("xn", [128, NS, D], BF16)
        self.junk = t("junk", [128, D], BF16)
        self.ss = t("ss", [128, NS], F32)
        self.rstd = t("rstd", [128, NS], F32)
        self.hT = t("hT", [128, 8, TT], BF16)
        self.h2T = t("h2T", [128, 8, TT], BF16)
        self.o_sb = t("o_sb", [128, NS, MIXW], BF16)
        self.oT = t("oT", [128, 12, TT], BF16)
        self.hid = t("hid", [128, 32, TT], BF16)
        self.rtmp = [t(f"rtmp{i}", [128, TT], F32) for i in range(2)]
        self.wbuf = [t(f"wbuf{i}", [128, 6144], BF16) for i in range(2)]
        self.wi = 0
        self.stg = [t(f"stg{i}", [128, TT], BF16) for i in range(4)]
        self.stgi = 0
        self.stgf = [t(f"stgf{i}", [128, TT], F32) for i in range(2)]
        self.stgfi = 0
        self.stgt = [t(f"stgt{i}", [128, NS, 512], BF16) for i in range(2)]
        self.stgti = 0
        self.fa = [t(f"fa{i}", [128, TT], F32) for i in range(4)]
        self.sqb = t("sqb", [128, TT], BF16)
        self.qnb = t("qnb", [128, TT], BF16)
        self.cos_sb = t("cos_sb", [128, TT], F32)
        self.sin_sb = t("sin_sb", [128, TT], F32)
        self.lr_sb = [t(f"lr_sb{i}", [16, TT], BF16) for i in range(2)]
        self.dt_sb = [t(f"dt_sb{i}", [16, TT], F32) for i in range(3)]
        self.ps = [T(nc, f"ps{i}", [128, 512], F32, psum=True) for i in range(6)]
        self.pstt = [T(nc, f"pst{i}", [128, 1024], BF16, psum=True) for i in range(2)]
        self.psx = V(self.pstt[0][:, :].bitcast(F32), self.pstt[0].b)
        sm = self.smax
        self.QB = min(512, sm)
        hflat = self.hid[:, :, :].rearrange("p c t -> p (c t)")
        if 2 * sm <= 32 * TT:
            self.dqT = [V(hflat[:, 0:sm], self.hid.b)]
            self.dkT = [V(hflat[:, sm:2 * sm], self.hid.b)]
        else:
            self.dqT = [t("dqT0", [128, sm], BF16)]
            self.dkT = [t("dkT0", [128, sm], BF16)]
        self.dva = [t(f"dva{i}", [128, sm // 128, 132], BF16) for i in range(1)]
        self.pt = [t(f"pt{i}", [128, self.QB], BF16) for i in range(3)]
        self.pti = 0
        self.dfin = t("dfin", [128, 8], F32)
        self.dft = [t(f"dft{i}", [128, 128], F32) for i in range(2)]
        self.djunk = t("djunk", [128, 128], BF16)
        self.dost = [t(f"dost{i}", [128, self.QB // 128, 128], BF16) for i in range(2)]
        self.dosti = 0
        self.sublnS = t("sublnS", [128, L * 128], F32)
        GT = self.GT = min(256, sm)
        NCH = GT // CH
        self.g_q = t("g_q", [128, GT], BF16)
        self.g_k = t("g_k", [128, GT], BF16)
        self.g_sp = t("g_sp", [128, GT], F32)
        self.g_cs = t("g_cs", [128, GT], F32)
        self.g_eb = t("g_eb", [128, GT], F32)
        self.g_enb = t("g_enb", [128, GT], F32)
        self.g_qt = t("g_qt", [128, GT], BF16)
        self.g_kt = t("g_kt", [128, GT], BF16)
        self.g_kh = t("g_kh", [128, GT], BF16)
        self.g_ktok = t("g_ktok", [64, NCH, 128], BF16)
        self.g_v = t("g_v", [64, NCH, 256], BF16)
        self.g_g = t("g_g", [64, NCH, 256], BF16)
        self.g_of = t("g_of", [64, NCH, 256], F32)
        self.g_ost = t("g_ost", [64, NCH, 256], BF16)
        self.g_am = [t(f"g_am{i}", [64, 64], BF16) for i in range(2)]
        self.g_Sf = t("g_Sf", [128, 128], F32)
        self.g_Sb = t("g_Sb", [128, 128], BF16)
        self.g_o = [t(f"g_o{i}", [64, 128], F32) for i in range(2)]
        self.g_oi = [t(f"g_oi{i}", [64, 128], F32) for i in range(2)]
        self.g_fin = t("g_fin", [64, 8], F32)
        self.g_junk = t("g_junk", [64, 128], BF16)
        self.g_mask = t("g_mask", [128, GT], F32)
        self.tri = t("tri", [64, 2, 64], F32)
        self.c_in = t("c_in", [128, GT + 4], BF16)
        self.c_acc = t("c_acc", [128, GT], F32)
        self.c_cv = t("c_cv", [128, GT], BF16)
        self.c_xt = t("c_xt", [64, NCH, 128], BF16)
        self.s_la = t("s_la", [16, GT], F32)
        self.s_dtt = t("s_dtt", [16, GT], F32)
        self.s_cs = t("s_cs", [16, GT], F32)
        self.s_rcs = t("s_rcs", [16, GT], F32)
        self.s_ac = t("s_ac", [16, GT], F32)
        self.s_seg = t("s_seg", [128, 8, CH], F32)
        self.s_t1 = t("s_t1", [64, 8, CH], F32)
        self.s_t2 = t("s_t2", [64, 8, CH], F32)
        self.s_mt = t("s_mt", [64, 8, CH], BF16)
        self.s_tok = t("s_tok", [64, NCH, 32], F32)
        self.s_sm = t("s_sm", [128, 64], F32)
        self.s_xt = t("s_xt", [64, NCH, 640], BF16)
        self.s_xw = t("s_xw", [64, 512], BF16)
        self.s_bt = t("s_bt", [128, GT], BF16)
        self.s_ct = t("s_ct", [128, GT], BF16)
        self.s_Sf = t("s_Sf", [128, 256], F32)
        self.s_Sb = t("s_Sb", [128, 256], BF16)
        self.s_yi = t("s_yi", [64, 512], F32)
        self.s_yf = V(self.x_sb[0:64, :, :].rearrange("p s d -> p (s d)")[:, 0:NCH * 512].rearrange("p (c f) -> p c f", f=512),
                      self.x_sb.b)
        self.s_z = V(self.o_sb[0:64, :, :].rearrange("p s d -> p (s d)")[:, 0:NCH * 512].rearrange("p (c f) -> p c f", f=512),
                     self.o_sb.b)
        self.s_y = t("s_y", [64, 512], F32)
        self.s_y2 = t("s_y2", [64, 512], F32)
        self.s_ost = V(self.oT[0:64, :, :].rearrange("p s d -> p (s d)")[:, 0:NCH * 512].rearrange("p (c f) -> p c f", f=512),
                       self.oT.b)
        self.s_junk = t("s_junk", [64, 512], BF16)
        self.mbias = t("mbias", [64, 2, CH], F32)
        self.hsel = t("hsel", [16, 2], F32)
        self.cvt_in = [V(self.wbuf[i][:, 0:2200].bitcast(F32), self.wbuf[i].b) for i in range(2)]
        self.cvt_out = [V(self.wbuf[i][:, 2200:3300], self.wbuf[i].b) for i in range(2)]

    def next_w(self):
        w = self.wbuf[self.wi % len(self.wbuf)]
        self.wi += 1
        return w

    def next_stg(self):
        w = self.stg[self.stgi % len(self.stg)]
        self.stgi += 1
        return w

    def prologue(self):
        L = self.L
        self.dma("sp", self.smalls[:], self.smalls_d, [], [self.smalls])
        for i, k in enumerate(("c_ident", "c_bones", "c_rmat")):
            self.dma("sp", self.cst_f[:, i, :], self.cd[k], [], [self.cst_f])
        for i, tt in enumerate((self.ident, self.bones, self.rmat)):
            self.op("dve", "tensor_copy", [self.cst_f], [tt], tt[:], self.cst_f[:, i, :])
        self.op("dve", "tensor_copy", [self.smalls], [self.wgb], self.wgb[:], self.sm("wg")[0:16, :])
        self.op("dve", "tensor_scalar", [self.smalls], [self.nbg], self.nbg[:], self.sm("bg"), -1.0, None,
                op0=ALU.mult)
        self.op("dve", "tensor_scalar", [self.smalls], [self.qg], self.qg[:], self.sm("qn"), 0.125, None,
                op0=ALU.mult)
        self.op("act", "activation", [self.smalls], [self.aneg], self.aneg[:], self.sm("alog")[0:16, :], AF.Exp)
        self.op("dve", "tensor_scalar", [self.aneg], [self.aneg], self.aneg[:], self.aneg[:], -1.0, None,
                op0=ALU.mult)
        lql = self.sm("lql")
        for l in range(L):
            for j in range(2):
                a = lql[:, (l * 4 + 2 * j) * 64:(l * 4 + 2 * j + 1) * 64]
                b = lql[:, (l * 4 + 2 * j + 1) * 64:(l * 4 + 2 * j + 2) * 64]
                self.op("dve", "tensor_tensor", [self.smalls], [self.lamtmp], self.lamtmp[:, j * 64:(j + 1) * 64],
                        a, b, ALU.mult)
                self.op("dve", "tensor_reduce", [self.lamtmp], [self.lam2], self.lam2[:, j:j + 1],
                        self.lamtmp[:, j * 64:(j + 1) * 64], AX.X, ALU.add)
            self.op("act", "activation", [self.lam2], [self.lam2], self.lam2[:, 2:4], self.lam2[:, 0:2], AF.Exp)
            self.op("dve", "tensor_tensor", [self.lam2], [self.lam2], self.lam2[:, 0:1], self.lam2[:, 2:3],
                    self.lam2[:, 3:4], ALU.subtract)
            self.op("dve", "tensor_scalar", [self.lam2], [self.lam], self.lam[:, l:l + 1], self.lam2[:, 0:1],
                    float(lambda_init(l)), None, op0=ALU.add)
        for l in range(L):
            self.op("dve", "tensor_scalar", [self.smalls], [self.sublnS], self.sublnS[:, l * 128:(l + 1) * 128],
                    self.sm("subln")[:, l * 128:(l + 1) * 128], float(1.0 - lambda_init(l)), None, op0=ALU.mult)
        for i in range(1):
            self.op("pool", "memset", [], [self.dva[i]], self.dva[i][:, :, 128:129], 1.0)
        if self.mixers and len(self.mixers) < 3:
            self.op("dve", "memset", [], [self.o_sb], self.o_sb[:, :, :], 0.0)
            for t0 in range(0, self.smax, self.TT):
                self.dma("sp", self.s_o[t0:t0 + self.TT, :].rearrange("(s p) d -> p s d", p=128), self.o_sb[:, :, :],
                         [self.o_sb], [self.b_o])
        self.op("pool", "memset", [], [self.g_mask], self.g_mask[:, :], 1.0)
        self.op("pool", "memset", [self.g_mask], [self.g_mask],
                self.g_mask[:, :].rearrange("p (c t) -> p c t", t=CH)[:, :, 0:1], 0.0)
        self.dma("sp", self.tri[:, :, :], self.cd["c_tri"], [], [self.tri])
        self.op("dve", "tensor_scalar", [self.tri], [self.mbias], self.mbias[:, :, :], self.tri[:, :, :], 30000.0, -30000.0,
                op0=ALU.mult, op1=ALU.add)
        self.dma("sp", self.hsel[:, :], self.cd["c_hsel"], [], [self.hsel])
        i = 0
        for l in range(L):
            for k in WNAMES:
                r, c = WSHAPES[k]
                per = r * c // 128
                src = self.wf[k][l].rearrange("(p a) c -> p (a c)", p=128)
                dst = self.wb[k][l].rearrange("(p a) c -> p (a c)", p=128)
                cw = 1100 if k == "w_in" else 1024
                for j in range(per // cw):
                    ci, co = self.cvt_in[i % 2], self.cvt_out[i % 2]
                    self.dma("sp", ci[:, 0:cw], src[:, j * cw:(j + 1) * cw], [], [ci])
                    eng = ("dve", "act", "pool")[i % 3]
                    if eng == "act":
                        self.op("act", "copy", [ci], [co], co[:, 0:cw], ci[:, 0:cw])
                    else:
                        self.op(eng, "tensor_copy", [ci], [co], co[:, 0:cw], ci[:, 0:cw])
                    self.dma("sp", dst[:, j * cw:(j + 1) * cw], co[:, 0:cw], [co], [self.wb_buf[k][l]])
                    i += 1

    def norm_T(self, gname, l, dst):
        NS, TT = self.NS, self.TT
        for s in range(NS):
            self.op("act", "activation", [self.x_sb], [self.junk, self.ss], self.junk[:], self.x_sb[:, s, :],
                    AF.Square, accum_out=self.ss[:, s:s + 1])
        self.op("dve", "tensor_scalar", [self.ss], [self.rstd], self.rstd[:], self.ss[:], 1.0 / D, EPS,
                op0=ALU.mult, op1=ALU.add)
        self.op("act", "activation", [self.rstd], [self.rstd], self.rstd[:], self.rstd[:], AF.Ln)
        self.op("act", "activation", [self.rstd], [self.rstd], self.rstd[:], self.rstd[:], AF.Exp, scale=-0.5)
        for s in range(NS):
            if s % 2 == 0:
                self.op("dve", "tensor_scalar", [self.x_sb, self.rstd], [self.xn], self.xn[:, s, :],
                        self.x_sb[:, s, :], self.rstd[:, s:s + 1], None, op0=ALU.mult)
            else:
                self.op("act", "activation", [self.x_sb, self.rstd], [self.xn], self.xn[:, s, :],
                        self.x_sb[:, s, :], AF.Copy, scale=self.rstd[:, s:s + 1])
        g = self.sm(gname)
        for kc in range(8):
            pb = self.pstt[kc % 2]
            for s in range(NS):
                self.op("pe", "transpose", [self.xn, self.ident], [pb],
                        pb[:, s * 128: (s + 1) * 128], self.xn[:, s, kc * 128:(kc + 1) * 128],
                        self.ident[:], signal=(s == NS - 1))
            gk = g[:, l * 8 + kc: l * 8 + kc + 1]
            if kc % 2 == 0:
                self.op("dve", "tensor_scalar", [pb, self.smalls], [dst], dst[:, kc, :],
                        pb[:, 0:TT], gk, None, op0=ALU.mult)
            else:
                self.op("act", "activation", [pb, self.smalls], [dst], dst[:, kc, :],
                        pb[:, 0:TT], AF.Copy, scale=gk)

    def load_w(self, k, l, c0, ncols, r0=0, nrows=None):
        rows = WSHAPES[k][0] if nrows is None else nrows
        nch = rows // 128
        w = self.next_w()
        src = self.wb[k][l][r0:r0 + rows, c0:c0 + ncols].rearrange("(kc p) n -> p kc n", p=128)
        view = w[:, 0:nch * ncols].rearrange("p (kc n) -> p kc n", kc=nch)
        self.dma("sp", view, src, [self.wb_buf[k][l]], [w])
        return w, view

    def phase_a(self, l, t0):
        TT, NS = self.TT, self.NS
        hT = self.hT
        fmi = [0]

        def fm_acc(w, wv, j, m=128):
            ps = self.ps[4 + fmi[0] % 2]
            fmi[0] += 1
            for kc in range(8):
                self.op("pe", "matmul", [w, hT], [ps], ps[0:m, 0:TT], wv[:, kc, j * 128:j * 128 + m], hT[:, kc, :],
                        start=(kc == 0), stop=(kc == 7), signal=(kc == 7))
            return ps

        evi = [0]

        def evac_copy(ps, dst_t, dst_ap, src_ap, scale=None):
            evi[0] += 1
            if evi[0] % 2 == 0:
                self.op("dve", "tensor_scalar", [ps], [dst_t], dst_ap, src_ap, 1.0 if scale is None else scale, None,
                        op0=ALU.mult)
            else:
                self.op("act", "activation", [ps], [dst_t], dst_ap, src_ap, AF.Copy,
                        scale=1.0 if scale is None else scale)

        if "gla" in self.mixers:
            w, wv = self.load_w("w_in", l, O_GQ, 512)
            for j in range(4):
                ps = fm_acc(w, wv, j)
                st = self.next_stg()
                evac_copy(ps, st, st[:, 0:TT], ps[:, 0:TT], scale=(0.125 if j < 2 else None))
                dst = (self.s_gq if j < 2 else self.s_gk)[j % 2][:, t0:t0 + TT]
                self.dma("pool", dst, st[:, 0:TT], [st], [self.b_gq if j < 2 else self.b_gk])
        w = self.next_w()
        wv = w[:, 0:8 * 48].rearrange("p (kc n) -> p kc n", kc=8)
        self.dma("sp", wv[:, :, 0:32], self.wb["w_in"][l][:, O_LR:O_LR + 32].rearrange("(kc p) n -> p kc n", p=128),
                 [self.wb_buf["w_in"][l]], [w])
        self.dma("sp", wv[:, :, 32:48], self.wb["w_in"][l][:, O_SDT:O_SDT + 16].rearrange("(kc p) n -> p kc n", p=128),
                 [self.wb_buf["w_in"][l]], [w])
        if "gla" in self.mixers:
            for d in range(2):
                ps = self.ps[4 + fmi[0] % 2]
                fmi[0] += 1
                for kc in range(8):
                    self.op("pe", "matmul", [w, hT], [ps], ps[0:16, 0:TT], wv[:, kc, d * 16:(d + 1) * 16], hT[:, kc, :],
                            start=(kc == 0), stop=(kc == 7), signal=(kc == 7))
                self.op("dve", "tensor_copy", [ps], [self.lr_sb[d]], self.lr_sb[d][:, 0:TT], ps[0:16, 0:TT])
            for d in range(2):
                for m in range(2):
                    ps = self.ps[4 + fmi[0] % 2]
                    fmi[0] += 1
                    wgo = ((l * 2 + d) * 256) + m * 128
                    self.op("pe", "matmul", [self.wgb, self.lr_sb[d]], [ps], ps[:, 0:TT], self.wgb[:, wgo:wgo + 128],
                            self.lr_sb[d][:, 0:TT], start=True, stop=True)
                    f = self.stgf[self.stgfi % 2]
                    self.stgfi += 1
                    bi = (l * 2 + d) * 2 + m
                    self.op("act", "activation", [ps, self.nbg], [f], f[:, 0:TT], ps[:, 0:TT], AF.Exp,
                            bias=self.nbg[:, bi:bi + 1], scale=-1.0)
                    self.op("act", "activation", [f], [f], f[:, 0:TT], f[:, 0:TT], AF.Ln, bias=1.0)
                    self.dma("pool", self.s_gb[d, m][:, t0:t0 + TT], f[:, 0:TT], [f], [self.b_gb])
        if "ssd" in self.mixers:
            ps = self.ps[4 + fmi[0] % 2]
            fmi[0] += 1
            for kc in range(8):
                self.op("pe", "matmul", [w, hT], [ps], ps[0:16, 0:TT], wv[:, kc, 32:48], hT[:, kc, :],
                        start=(kc == 0), stop=(kc == 7), signal=(kc == 7))
            d0, d1, d2 = self.dt_sb
            dtb = self.sm("dtb")
            self.op("act", "activation", [ps, self.smalls], [d0], d0[:, 0:TT], ps[0:16, 0:TT], AF.Exp,
                    bias=dtb[0:16, l:l + 1], scale=1.0)
            self.op("act", "activation", [d0], [d1], d1[:, 0:TT], d0[:, 0:TT], AF.Ln, bias=1.0)
            self.op("dve", "tensor_scalar", [d1, self.aneg], [d2], d2[:, 0:TT], d1[:, 0:TT], self.aneg[:, l:l + 1],
                    None, op0=ALU.mult)
            self.dma("pool", self.s_dt[0][:, t0:t0 + TT], d1[:, 0:TT], [d1], [self.b_dt])
            self.dma("pool", self.s_dt[1][:, t0:t0 + TT], d2[:, 0:TT], [d2], [self.b_dt])
            for (c0, nj, jb) in ((O_SX, 4, 0), (O_SX + 512, 2, 4)):
                w2, wv2 = self.load_w("w_in", l, c0, nj * 128)
                for j in range(nj):
                    ps = fm_acc(w2, wv2, j)
                    st = self.next_stg()
                    evac_copy(ps, st, st[:, 0:TT], ps[:, 0:TT])
                    self.dma("pool", self.s_sx[jb + j][:, t0:t0 + TT], st[:, 0:TT], [st], [self.b_sx])
        if "diff" in self.mixers:
            self.dma("sp", self.cos_sb[:, 0:TT], self.cd["c_cos"][:, t0:t0 + TT], [], [self.cos_sb])
            self.dma("sp", self.sin_sb[:, 0:TT], self.cd["c_sin"][:, t0:t0 + TT], [], [self.sin_sb])
            for qk in range(2):
                w2, wv2 = self.load_w("w_in", l, O_DQ if qk == 0 else O_DK, 512)
                gain = self.qg[:, l:l + 1] if qk == 0 else self.sm("kn")[:, l:l + 1]
                gain_t = self.qg if qk == 0 else self.smalls
                for j in range(4):
                    ps = fm_acc(w2, wv2, j)
                    p6 = self.ps[0]
                    f0, f1, f2, f3 = self.fa
                    self.op("act", "activation", [ps], [self.sqb], self.sqb[:, 0:TT], ps[:, 0:TT], AF.Square)
                    self.op("pe", "matmul", [self.bones, self.sqb], [p6], p6[:, 0:TT], self.bones[:], self.sqb[:, 0:TT],
                            start=True, stop=True)
                    self.op("dve", "tensor_scalar", [p6], [f0], f0[:, 0:TT], p6[:, 0:TT], EPS, None,
                            op0=ALU.add)
                    self.op("act", "activation", [f0], [f0], f0[:, 0:TT], f0[:, 0:TT], AF.Ln)
                    self.op("act", "activation", [f0], [f0], f0[:, 0:TT], f0[:, 0:TT], AF.Exp, scale=-0.5)
                    self.op("dve", "scalar_tensor_tensor", [ps, gain_t, f0], [f1], f1[:, 0:TT], ps[:, 0:TT], gain,
                            f0[:, 0:TT], op0=ALU.mult, op1=ALU.mult)
                    self.op("act", "copy", [f1], [self.qnb], self.qnb[:, 0:TT], f1[:, 0:TT])
                    self.op("pe", "matmul", [self.rmat, self.qnb], [p6], p6[:, 0:TT], self.rmat[:], self.qnb[:, 0:TT],
                            start=True, stop=True)
                    self.op("pool", "tensor_tensor", [f1, self.cos_sb], [f2], f2[:, 0:TT], f1[:, 0:TT],
                            self.cos_sb[:, 0:TT], ALU.mult)
                    self.op("dve", "tensor_tensor", [p6, self.sin_sb], [f3], f3[:, 0:TT], p6[:, 0:TT],
                            self.sin_sb[:, 0:TT], ALU.mult)
                    st = self.next_stg()
                    self.op("pool", "tensor_tensor", [f2, f3], [st], st[:, 0:TT], f2[:, 0:TT], f3[:, 0:TT], ALU.add)
                    self.dma("pool", (self.s_dq if qk == 0 else self.s_dk)[j][:, t0:t0 + TT], st[:, 0:TT], [st],
                             [self.b_dq if qk == 0 else self.b_dk])
        tmg = []
        if "gla" in self.mixers:
            tmg += [(O_GV, self.s_gv, self.b_gv, False), (O_GG, self.s_gg, self.b_gg, True)]
        if "diff" in self.mixers:
            tmg += [(O_DV, self.s_dv, self.b_dv, False)]
        if "ssd" in self.mixers:
            tmg += [(O_SZ, self.s_sz, self.b_sz, True)]
        for (c0, dst, dbuf, silu) in tmg:
            w2, wv2 = self.load_w("w_in", l, c0, 512)
            sg = self.stgt[self.stgti % 2]
            self.stgti += 1
            for s in range(NS):
                ps = self.ps[s % 4]
                for kc in range(8):
                    self.op("pe", "matmul", [w2, hT], [ps], ps[:, :], hT[:, kc, s * 128:(s + 1) * 128], wv2[:, kc, :],
                            start=(kc == 0), stop=(kc == 7), signal=(kc == 7))
                if silu:
                    self.op("act", "activation", [ps], [sg], sg[:, s, :], ps[:, :], AF.Silu)
                else:
                    evac_copy(ps, sg, sg[:, s, :], ps[:, :])
            self.dma("pool", dst[t0:t0 + TT, :].rearrange("(s p) c -> p s c", p=128), sg[:, :, :], [sg], [dbuf])

    def phase_c(self, l, xsrc, ydst, t0):
        TT, NS = self.TT, self.NS
        x_sb = self.x_sb
        self.dma("sp", x_sb[:, :, :], xsrc[t0:t0 + TT, :].rearrange("(s p) d -> p s d", p=128), [self.b_y], [x_sb])
        if self.mixers:
            o_sb, oT = self.o_sb, self.oT
            self.dma("sp", o_sb[:, :, :], self.s_o[t0:t0 + TT, :].rearrange("(s p) d -> p s d", p=128),
                     [self.b_o], [o_sb])
            for fc in range(12):
                pb = self.pstt[fc % 2]
                for s in range(NS):
                    self.op("pe", "transpose", [o_sb, self.ident], [pb],
                            pb[:, s * 128: (s + 1) * 128], o_sb[:, s, fc * 128:(fc + 1) * 128],
                            self.ident[:], signal=(s == NS - 1))
                if fc % 2 == 0:
                    self.op("dve", "tensor_copy", [pb], [oT], oT[:, fc, :], pb[:, 0:TT])
                else:
                    self.op("act", "copy", [pb], [oT], oT[:, fc, :], pb[:, 0:TT])
            for n in range(2):
                w, wv = self.load_w("w_out", l, n * 512, 512)
                for s in range(NS):
                    ps = self.ps[s % 4]
                    for fc in range(12):
                        self.op("pe", "matmul", [w, oT], [ps], ps[:, :], oT[:, fc, s * 128:(s + 1) * 128], wv[:, fc, :],
                                start=(fc == 0), stop=(fc == 11), signal=(fc == 11))
                    self.op("dve", "tensor_tensor", [ps, x_sb], [x_sb], x_sb[:, s, n * 512:(n + 1) * 512], ps[:, :],
                            x_sb[:, s, n * 512:(n + 1) * 512], ALU.add)
        self.norm_T("g2", l, self.h2T)
        h2T, hid = self.h2T, self.hid
        for fg in range(8):
            w, wv = self.load_w("w_mlp1", l, fg * 512, 512)
            for j in range(4):
                c = fg * 4 + j
                ps = self.ps[4 + c % 2]
                for kc in range(8):
                    self.op("pe", "matmul", [w, h2T], [ps], ps[:, 0:TT], wv[:, kc, j * 128:(j + 1) * 128], h2T[:, kc, :],
                            start=(kc == 0), stop=(kc == 7), signal=(kc == 7))
                r = self.rtmp[c % 2]
                self.op("act", "activation", [ps], [r], r[:, 0:TT], ps[:, 0:TT], AF.Relu)
                self.op("pool", "tensor_tensor", [r], [hid], hid[:, c, :], r[:, 0:TT], r[:, 0:TT], ALU.mult)
        for n in range(2):
            for g in range(4):
                w, wv = self.load_w("w_mlp2", l, n * 512, 512, r0=g * 1024, nrows=1024)
                for s in range(NS):
                    ps = self.ps[s % 4]
                    for j in range(8):
                        c = g * 8 + j
                        self.op("pe", "matmul", [w, hid], [ps], ps[:, :], hid[:, c, s * 128:(s + 1) * 128], wv[:, j, :],
                                start=(c == 0), stop=(c == 31), signal=(j == 7 and (s == NS - 1 or c == 31)))
            for s in range(NS):
                ps = self.ps[s % 4]
                self.op("dve", "tensor_tensor", [ps, x_sb], [x_sb], x_sb[:, s, n * 512:(n + 1) * 512], ps[:, :],
                        x_sb[:, s, n * 512:(n + 1) * 512], ALU.add)
        self.dma("pool", ydst[t0:t0 + TT, :].rearrange("(s p) d -> p s d", p=128), x_sb[:, :, :], [x_sb], [self.b_y])

    def build(self):
        L, TT = self.L, self.TT
        self.prologue()
        outs = []
        for (iname, oname, n, S) in self.groups:
            for i in range(n):
                xin = self.xin[iname][i]
                y = self.yout[oname][i]
                self.b_y = Buf("y")
                outs.append(self.b_y)
                for t0 in range(0, S, TT):
                    self.dma("sp", self.x_sb[:, :, :], xin[t0:t0 + TT, :].rearrange("(s p) d -> p s d", p=128),
                             [], [self.x_sb])
                    self.norm_T("g1", 0, self.hT)
                    self.phase_a(0, t0)
                for l in range(L):
                    self.mixers_phase(l, S)
                    for t0 in range(0, S, TT):
                        self.phase_c(l, xin if l == 0 else y, y, t0)
                        if l + 1 < L:
                            self.norm_T("g1", l + 1, self.hT)
                            self.phase_a(l + 1, t0)
        self.S.finish(outs + [self.b_o, self.b_dq, self.b_dk, self.b_dv, self.b_gq, self.b_gk, self.b_gv, self.b_gg,
                              self.b_gb, self.b_sx, self.b_sz, self.b_dt])
        self.S.replay()
        return self.nc

    def mixers_phase(self, l, S):
        import os
        if os.environ.get("K_SKIP_MIX"):
            return
        if "diff" in self.mixers:
            self.diff_mixer(l, S)
        if "gla" in self.mixers:
            self.gla_mixer(l, S)
        if "ssd" in self.mixers:
            self.ssd_mixer(l, S)

    def diff_mixer(self, l, S):
        QB = min(self.QB, S)
        NQ = QB // 128
        NK = S // 128
        for h in range(4):
            qT, kT, va = self.dqT[0], self.dkT[0], self.dva[0]
            self.dma("sp", qT[:, 0:S], self.s_dq[h][:, 0:S], [self.b_dq], [qT])
            self.dma("sp", kT[:, 0:S], self.s_dk[h][:, 0:S], [self.b_dk], [kT])
            self.dma("sp", va[:, 0:NK, 0:128],
                     self.s_dv[0:S, h * 128:(h + 1) * 128].rearrange("(kt p) c -> p kt c", p=128), [self.b_dv], [va])
            for q0 in range(0, S, QB):
                for qs in range(NQ):
                    self.op("dve", "memset", [], [self.ps[qs]], self.ps[qs][:, :], 0.0)
                for kt in range(NK):
                    for c in range(2):
                        sc = self.ps[4 + (kt * 2 + c) % 2]
                        self.op("pe", "matmul", [kT, qT], [sc], sc[:, 0:QB], kT[c * 64:(c + 1) * 64, kt * 128:(kt + 1) * 128],
                                qT[c * 64:(c + 1) * 64, q0:q0 + QB], start=True, stop=True)
                        pt = self.pt[self.pti % 3]
                        self.pti += 1
                        self.op("act", "activation", [sc], [pt], pt[:, 0:QB], sc[:, 0:QB], AF.Exp)
                        for qs in range(NQ):
                            acc = self.ps[qs]
                            self.op("pe", "matmul", [pt, va], [acc], acc[:, c * 256:c * 256 + 129],
                                    pt[:, qs * 128:(qs + 1) * 128], va[:, kt, 0:129],
                                    start=False, stop=(kt == NK - 1), signal=(qs == NQ - 1), skip_group_check=True)
                ost = self.dost[self.dosti % 2]
                self.dosti += 1
                for qs in range(NQ):
                    acc = self.ps[qs]
                    fin = self.dfin
                    t0_, t1_ = self.dft
                    self.op("dve", "reciprocal", [acc], [fin], fin[:, 0:1], acc[:, 128:129])
                    self.op("dve", "reciprocal", [acc], [fin], fin[:, 1:2], acc[:, 384:385])
                    self.op("dve", "tensor_scalar", [fin, self.lam], [fin], fin[:, 2:3], fin[:, 1:2],
                            self.lam[:, l:l + 1], -1.0, op0=ALU.mult, op1=ALU.mult)
                    self.op("dve", "tensor_scalar", [acc, fin], [t0_], t0_[:], acc[:, 256:384], fin[:, 2:3], None,
                            op0=ALU.mult)
                    self.op("dve", "scalar_tensor_tensor", [acc, fin, t0_], [t1_], t1_[:], acc[:, 0:128], fin[:, 0:1],
                            t0_[:], op0=ALU.mult, op1=ALU.add)
                    self.op("act", "activation", [t1_], [self.djunk, fin], self.djunk[:], t1_[:], AF.Square,
                            accum_out=fin[:, 3:4])
                    self.op("dve", "tensor_scalar", [fin], [fin], fin[:, 4:5], fin[:, 3:4], 1.0 / 128, EPS,
                            op0=ALU.mult, op1=ALU.add)
                    self.op("act", "activation", [fin], [fin], fin[:, 6:7], fin[:, 4:5], AF.Ln)
                    self.op("act", "activation", [fin], [fin], fin[:, 5:6], fin[:, 6:7], AF.Exp, scale=-0.5)
                    self.op("dve", "scalar_tensor_tensor", [t1_, fin, self.sublnS], [ost], ost[:, qs, :], t1_[:],
                            fin[:, 5:6], self.sublnS[:, l * 128:(l + 1) * 128], op0=ALU.mult, op1=ALU.mult)
                self.dma("pool", self.s_o[q0:q0 + QB, 512 + h * 128:512 + (h + 1) * 128].rearrange("(s p) c -> p s c", p=128),
                         ost[:, 0:NQ, :], [ost], [self.b_o])

    def gla_mixer(self, l, S):
        GT = min(self.GT, S)
        NCH = GT // CH
        glan = self.sm("glan")
        import os
        ndir = int(os.environ.get("K_GLA_DIRS", "2"))
        nochunk = os.environ.get("K_GLA_NOCHUNK")
        for m in range(2):
            for d in range(ndir):
                self.op("dve", "memset", [], [self.g_Sf], self.g_Sf[:, :], 0.0)
                self.op("pool", "memset", [], [self.g_Sb], self.g_Sb[:, :], 0.0)
                tiles = list(range(0, S, GT))
                if d == 1:
                    tiles = tiles[::-1]
                for t0 in tiles:
                    q, k, sp, cs, eb, enb = self.g_q, self.g_k, self.g_sp, self.g_cs, self.g_eb, self.g_enb
                    qt, kt, kh, ktok, v = self.g_qt, self.g_kt, self.g_kh, self.g_ktok, self.g_v
                    self.dma("sp", q[:, 0:GT], self.s_gq[m][:, t0:t0 + GT], [self.b_gq], [q])
                    self.dma("sp", k[:, 0:GT], self.s_gk[m][:, t0:t0 + GT], [self.b_gk], [k])
                    self.dma("sp", sp[:, 0:GT], self.s_gb[d, m][:, t0:t0 + GT], [self.b_gb], [sp])
                    self.dma("sp", v[:, 0:NCH, :],
                             self.s_gv[t0:t0 + GT, m * 256:(m + 1) * 256].rearrange("(c p) f -> p c f", p=CH),
                             [self.b_gv], [v])
                    self.op("dve", "tensor_tensor_scan", [self.g_mask, sp], [cs], cs[:, 0:GT], self.g_mask[:, 0:GT],
                            sp[:, 0:GT], 0.0, ALU.mult, ALU.add)
                    cs3 = cs[:, 0:GT].rearrange("p (c t) -> p c t", t=CH)
                    if d == 1:
                        self.op("dve", "tensor_tensor", [sp, cs], [sp], sp[:, 0:GT], sp[:, 0:GT], cs[:, 0:GT],
                                ALU.subtract)
                        self.op("dve", "tensor_tensor", [sp, cs], [cs], cs3,
                                sp[:, 0:GT].rearrange("p (c t) -> p c t", t=CH),
                                cs3[:, :, CH - 1:CH].to_broadcast([128, NCH, CH]), ALU.add)
                    self.op("act", "activation", [cs], [eb], eb[:, 0:GT], cs[:, 0:GT], AF.Exp, scale=-1.0 / 16)
                    self.op("act", "activation", [cs], [enb], enb[:, 0:GT], cs[:, 0:GT], AF.Exp, scale=1.0 / 16)
                    self.op("dve", "tensor_tensor", [q, eb], [qt], qt[:, 0:GT], q[:, 0:GT], eb[:, 0:GT], ALU.mult)
                    self.op("pool", "tensor_tensor", [k, enb], [kt], kt[:, 0:GT], k[:, 0:GT], enb[:, 0:GT], ALU.mult)
                    eb3 = eb[:, 0:GT].rearrange("p (c t) -> p c t", t=CH)
                    edge = CH - 1 if d == 0 else 0
                    self.op("dve", "tensor_tensor", [kt, eb], [kh], kh[:, 0:GT].rearrange("p (c t) -> p c t", t=CH),
                            kt[:, 0:GT].rearrange("p (c t) -> p c t", t=CH),
                            eb3[:, :, edge:edge + 1].to_broadcast([128, NCH, CH]), ALU.mult)
                    pb = self.pstt[0]
                    for c in range(NCH):
                        self.op("pe", "transpose", [kh, self.ident], [pb], pb[0:64, c * 128:(c + 1) * 128],
                                kh[:, c * CH:(c + 1) * CH], self.ident[:], signal=(c == NCH - 1))
                    self.op("dve", "tensor_copy", [pb], [ktok], ktok[:, 0:NCH, :],
                            pb[0:64, 0:NCH * 128].rearrange("p (c f) -> p c f", f=128))
                    if d == 1:
                        self.dma("sp", self.g_of[:, 0:NCH, :],
                                 self.s_gof[t0:t0 + GT, m * 256:(m + 1) * 256].rearrange("(c p) f -> p c f", p=CH),
                                 [self.b_gof], [self.g_of])
                        self.dma("sp", self.g_g[:, 0:NCH, :],
                                 self.s_gg[t0:t0 + GT, m * 256:(m + 1) * 256].rearrange("(c p) f -> p c f", p=CH),
                                 [self.b_gg], [self.g_g])
                    chunks = list(range(NCH))
                    if d == 1:
                        chunks = chunks[::-1]
                    if nochunk:
                        chunks = []
                    for c in chunks:
                        cs_ = slice(c * CH, (c + 1) * CH)
                        for hh in range(int(os.environ.get("K_GLA_HH", "2"))):
                            p0 = hh * 64
                            pa = self.ps[4 + hh]
                            pav = pa[0:64, 0:64]
                            self.op("pe", "matmul", [kt, qt], [pa], pav, kt[p0:p0 + 64, cs_], qt[p0:p0 + 64, cs_],
                                    start=True, stop=True)
                            am = self.g_am[hh]
                            self.op("dve", "tensor_tensor", [pa, self.tri], [am], am[:, :], pav,
                                    self.tri[:, d, :], ALU.mult)
                            po = self.ps[hh]
                            vv = v[:, c, hh * 128:(hh + 1) * 128]
                            self.op("pe", "matmul", [am, v], [po], po[0:64, 0:128], am[:, :], vv, start=True, stop=True)
                            pin = self.ps[4 + hh]
                            pinv = pin[0:64, 128:256]
                            self.op("pe", "matmul", [qt, self.g_Sb], [pin], pinv, qt[p0:p0 + 64, cs_],
                                    self.g_Sb[p0:p0 + 64, :], start=True, stop=True)
                            oi = self.g_oi[hh]
                            self.op("act", "copy", [pin], [oi], oi[:, :], pinv)
                            pu = self.ps[2 + hh]
                            self.op("pe", "matmul", [ktok, v], [pu], pu[:, 0:128], ktok[:, c, :], vv, start=True, stop=True)
                            ecol = c * CH + edge
                            self.op("dve", "scalar_tensor_tensor", [self.g_Sf, eb, pu], [self.g_Sf],
                                    self.g_Sf[p0:p0 + 64, :], self.g_Sf[p0:p0 + 64, :], eb[p0:p0 + 64, ecol:ecol + 1],
                                    pu[p0:p0 + 64, 0:128], op0=ALU.mult, op1=ALU.add)
                            if d == 0:
                                self.op("dve", "tensor_tensor", [po, oi], [self.g_of], self.g_of[:, c, hh * 128:(hh + 1) * 128],
                                        po[0:64, 0:128], oi[:, :], ALU.add)
                            else:
                                o = self.g_o[hh]
                                fin = self.g_fin
                                f0 = hh * 4
                                self.op("dve", "tensor_tensor", [po, oi], [o], o[:, :], po[0:64, 0:128], oi[:, :], ALU.add)
                                self.op("pool", "tensor_tensor", [o, self.g_of], [o], o[:, :], o[:, :],
                                        self.g_of[:, c, hh * 128:(hh + 1) * 128], ALU.add)
                                self.op("act", "activation", [o], [self.g_junk, fin], self.g_junk[:, :], o[:, :], AF.Square,
                                        accum_out=fin[:, f0:f0 + 1])
                                self.op("dve", "tensor_scalar", [fin], [fin], fin[:, f0 + 1:f0 + 2], fin[:, f0:f0 + 1],
                                        1.0 / 128, EPS, op0=ALU.mult, op1=ALU.add)
                                self.op("act", "activation", [fin], [fin], fin[:, f0 + 2:f0 + 3], fin[:, f0 + 1:f0 + 2], AF.Ln)
                                self.op("act", "activation", [fin], [fin], fin[:, f0 + 3:f0 + 4], fin[:, f0 + 2:f0 + 3], AF.Exp,
                                        scale=-0.5)
                                self.op("dve", "scalar_tensor_tensor", [o, fin, self.smalls], [o], o[:, :], o[:, :],
                                        fin[:, f0 + 3:f0 + 4], glan[0:64, l * 128:(l + 1) * 128], op0=ALU.mult, op1=ALU.mult)
                                self.op("pool", "tensor_tensor", [o, self.g_g], [self.g_ost],
                                        self.g_ost[:, c, hh * 128:(hh + 1) * 128], o[:, :],
                                        self.g_g[:, c, hh * 128:(hh + 1) * 128], ALU.mult)
                        self.op("act", "copy", [self.g_Sf], [self.g_Sb], self.g_Sb[:, :], self.g_Sf[:, :])
                    if d == 0:
                        self.dma("pool", self.s_gof[t0:t0 + GT, m * 256:(m + 1) * 256].rearrange("(c p) f -> p c f", p=CH),
                                 self.g_of[:, 0:NCH, :], [self.g_of], [self.b_gof])
                    else:
                        self.dma("pool", self.s_o[t0:t0 + GT, m * 256:(m + 1) * 256].rearrange("(c p) f -> p c f", p=CH),
                                 self.g_ost[:, 0:NCH, :], [self.g_ost], [self.b_o])

    def ssd_mixer(self, l, S):
        GT = min(self.GT, S)
        NCH = GT // CH
        cw, cb = self.sm("convw"), self.sm("convb")
        for t0 in range(0, S, GT):
            lo, hi = max(t0 - 2, 0), min(t0 + GT + 2, S)
            for fc in range(6):
                ci, acc, cv = self.c_in, self.c_acc, self.c_cv
                self.op("pool", "memset", [], [ci], ci[:, 0:2], 0.0)
                self.op("pool", "memset", [], [ci], ci[:, GT + 2:GT + 4], 0.0)
                self.dma("sp", ci[:, lo - (t0 - 2):hi - (t0 - 2)], self.s_sx[fc][:, lo:hi], [self.b_sx], [ci])
                wo = (l * 6 + fc) * 5
                self.op("dve", "tensor_scalar", [ci, self.smalls], [acc], acc[:, 0:GT], ci[:, 0:GT], cw[:, wo:wo + 1], None,
                        op0=ALU.mult)
                for k in range(1, 5):
                    self.op("dve", "scalar_tensor_tensor", [ci, self.smalls, acc], [acc], acc[:, 0:GT], ci[:, k:k + GT],
                            cw[:, wo + k:wo + k + 1], acc[:, 0:GT], op0=ALU.mult, op1=ALU.add)
                self.op("act", "activation", [acc, self.smalls], [cv], cv[:, 0:GT], acc[:, 0:GT], AF.Silu,
                        bias=cb[:, l * 6 + fc:l * 6 + fc + 1])
                if fc >= 4:
                    self.dma("pool", self.s_sbc[fc - 4][:, t0:t0 + GT], cv[:, 0:GT], [cv], [self.b_sbc])
                if fc <= 4:
                    pb = self.pstt[1]
                    for c in range(NCH):
                        self.op("pe", "transpose", [cv, self.ident], [pb], pb[0:64, c * 128:(c + 1) * 128],
                                cv[:, c * CH:(c + 1) * CH], self.ident[:], signal=(c == NCH - 1))
                    self.op("dve", "tensor_copy", [pb], [self.c_xt], self.c_xt[:, 0:NCH, :],
                            pb[0:64, 0:NCH * 128].rearrange("p (c f) -> p c f", f=128))
                    self.dma("pool", self.s_sxt[t0:t0 + GT, fc * 128:(fc + 1) * 128].rearrange("(c p) f -> p c f", p=CH),
                             self.c_xt[:, 0:NCH, :], [self.c_xt], [self.b_sxt])
        ssdD, ssdn = self.sm("ssdD"), self.sm("ssdn")
        idf = self.cst_f
        for d in range(2):
            r0 = d * 8
            self.op("dve", "memset", [], [self.s_Sf], self.s_Sf[:, :], 0.0)
            self.op("pool", "memset", [], [self.s_Sb], self.s_Sb[:, :], 0.0)
            tiles = list(range(0, S, GT))
            if d == 1:
                tiles = tiles[::-1]
            edge = CH - 1 if d == 0 else 0
            for t0 in tiles:
                la, dtt, cs, rcs, ac = self.s_la, self.s_dtt, self.s_cs, self.s_rcs, self.s_ac
                self.dma("sp", dtt[:, 0:GT], self.s_dt[0][:, t0:t0 + GT], [self.b_dt], [dtt])
                self.dma("sp", la[:, 0:GT], self.s_dt[1][:, t0:t0 + GT], [self.b_dt], [la])
                self.dma("sp", self.s_xt[:, 0:NCH, :], self.s_sxt[t0:t0 + GT, :].rearrange("(c p) f -> p c f", p=CH),
                         [self.b_sxt], [self.s_xt])
                self.dma("sp", self.s_bt[:, 0:GT], self.s_sbc[0][:, t0:t0 + GT], [self.b_sbc], [self.s_bt])
                self.dma("sp", self.s_ct[:, 0:GT], self.s_sbc[1][:, t0:t0 + GT], [self.b_sbc], [self.s_ct])
                self.op("dve", "tensor_tensor_scan", [self.g_mask, la], [cs], cs[:, 0:GT], self.g_mask[0:16, 0:GT],
                        la[:, 0:GT], 0.0, ALU.mult, ALU.add)
                cs3 = cs[:, 0:GT].rearrange("p (c t) -> p c t", t=CH)
                self.op("dve", "tensor_tensor", [la, cs], [rcs], rcs[:, 0:GT], la[:, 0:GT], cs[:, 0:GT], ALU.subtract)
                self.op("dve", "tensor_tensor", [rcs, cs], [rcs], rcs[:, 0:GT].rearrange("p (c t) -> p c t", t=CH),
                        rcs[:, 0:GT].rearrange("p (c t) -> p c t", t=CH),
                        cs3[:, :, CH - 1:CH].to_broadcast([16, NCH, CH]), ALU.add)
                self.op("dve", "tensor_scalar", [cs, self.hsel], [ac], ac[:, 0:GT], cs[:, 0:GT], self.hsel[:, 0:1], None,
                        op0=ALU.mult)
                self.op("dve", "scalar_tensor_tensor", [rcs, self.hsel, ac], [ac], ac[:, 0:GT], rcs[:, 0:GT],
                        self.hsel[:, 1:2], ac[:, 0:GT], op0=ALU.mult, op1=ALU.add)
                self.dma("pool", self.s_acd[:, t0:t0 + GT], ac[:, 0:GT], [ac], [self.b_acd])
                pq = self.ps[5]
                for c in range(NCH):
                    self.op("pe", "transpose", [ac, idf], [pq], pq[0:64, c * 32:c * 32 + 16], ac[:, c * CH:(c + 1) * CH],
                            idf[0:16, 0, 0:16], signal=False)
                    self.op("pe", "transpose", [dtt, idf], [pq], pq[0:64, c * 32 + 16:c * 32 + 32], dtt[:, c * CH:(c + 1) * CH],
                            idf[0:16, 0, 0:16], signal=(c == NCH - 1))
                self.op("dve", "tensor_copy", [pq], [self.s_tok], self.s_tok[:, 0:NCH, :],
                        pq[0:64, 0:NCH * 32].rearrange("p (c f) -> p c f", f=32))
                if d == 1:
                    self.dma("sp", self.s_yf[:, 0:NCH, :], self.s_yfd[t0:t0 + GT, :].rearrange("(c p) f -> p c f", p=CH),
                             [self.b_yfd], [self.s_yf])
                    self.dma("sp", self.s_z[:, 0:NCH, :], self.s_sz[t0:t0 + GT, :].rearrange("(c p) f -> p c f", p=CH),
                             [self.b_sz], [self.s_z])
                chunks = list(range(NCH))
                if d == 1:
                    chunks = chunks[::-1]
                for c in chunks:
                    cs_ = slice(c * CH, (c + 1) * CH)
                    seg, t1, t2, mt, sm_ = self.s_seg, self.s_t1, self.s_t2, self.s_mt, self.s_sm
                    self.dma("sp", seg[:, :, :], self.s_acd[r0:r0 + 8, t0 + c * CH:t0 + (c + 1) * CH].partition_broadcast(128),
                             [self.b_acd], [seg])
                    actok = self.s_tok[:, c, r0:r0 + 8]
                    dttok = self.s_tok[:, c, 16 + r0:16 + r0 + 8]
                    self.op("dve", "tensor_tensor", [seg, self.s_tok], [t1], t1[:, :, :], seg[0:64, :, :],
                            actok.unsqueeze(2).to_broadcast([64, 8, CH]), ALU.subtract)
                    self.op("pool", "tensor_tensor", [t1, self.mbias], [t2], t2[:, :, :], t1[:, :, :],
                            self.mbias[:, d:d + 1, :].to_broadcast([64, 8, CH]), ALU.add)
                    self.op("act", "activation", [t2], [t1], t1[:, :, :], t2[:, :, :], AF.Exp)
                    self.op("dve", "tensor_tensor", [t1, self.s_tok], [t2], t2[:, :, :], t1[:, :, :],
                            dttok.unsqueeze(2).to_broadcast([64, 8, CH]), ALU.mult)
                    for g in range(2):
                        pcb = self.ps[4 + g]
                        self.op("pe", "matmul", [self.s_bt, self.s_ct], [pcb], pcb[0:64, 0:64],
                                self.s_bt[g * 64:(g + 1) * 64, cs_], self.s_ct[g * 64:(g + 1) * 64, cs_], start=True, stop=True)
                    for g in range(2):
                        pcb = self.ps[4 + g]
                        self.op("dve", "tensor_tensor", [t2, pcb], [mt], mt[:, g * 4:(g + 1) * 4, :], t2[:, g * 4:(g + 1) * 4, :],
                                pcb[0:64, 0:64].unsqueeze(1).to_broadcast([64, 4, CH]), ALU.mult)
                    self.op("act", "activation", [self.s_tok], [sm_], sm_[0:64, 0:8], actok, AF.Exp)
                    self.op("dve", "tensor_tensor", [seg, self.s_tok], [sm_], sm_[0:64, 8:16], seg[0:64, :, edge], actok,
                            ALU.subtract)
                    self.op("act", "activation", [sm_], [sm_], sm_[0:64, 16:24], sm_[0:64, 8:16], AF.Exp)
                    self.op("dve", "tensor_tensor", [sm_, self.s_tok], [sm_], sm_[0:64, 24:32], sm_[0:64, 16:24], dttok, ALU.mult)
                    self.op("act", "activation", [seg], [sm_], sm_[:, 32:40], seg[:, :, edge], AF.Exp)
                    xt = self.s_xt
                    py = self.ps[0]
                    for h in range(8):
                        self.op("pe", "matmul", [mt, xt], [py], py[0:64, h * 64:(h + 1) * 64], mt[:, h, :],
                                xt[:, c, h * 64:(h + 1) * 64], start=True, stop=True, signal=(h == 7))
                    yi = self.s_yi
                    for g in range(2):
                        pi = self.ps[1] if g == 0 else self.psx
                        self.op("pe", "matmul", [self.s_ct, self.s_Sb], [pi], pi[0:64, 0:256],
                                self.s_ct[g * 64:(g + 1) * 64, cs_], self.s_Sb[g * 64:(g + 1) * 64, :], start=True, stop=True)
                    for g in range(2):
                        pi = self.ps[1] if g == 0 else self.psx
                        self.op("dve", "tensor_tensor", [pi, sm_], [yi],
                                yi[:, g * 256:(g + 1) * 256].rearrange("p (h q) -> p h q", q=64),
                                pi[0:64, 0:256].rearrange("p (h q) -> p h q", q=64),
                                sm_[0:64, g * 4:(g + 1) * 4].unsqueeze(2).to_broadcast([64, 4, 64]), ALU.mult)
                    self.op("dve", "tensor_tensor", [xt, sm_], [self.s_xw], self.s_xw[:, :].rearrange("p (h q) -> p h q", q=64),
                            xt[:, c, 0:512].rearrange("p (h q) -> p h q", q=64),
                            sm_[0:64, 24:32].unsqueeze(2).to_broadcast([64, 8, 64]), ALU.mult)
                    pu = self.ps[2]
                    for g in range(2):
                        self.op("pe", "matmul", [xt, self.s_xw], [self.ps[2 + g]], self.ps[2 + g][:, 0:256], xt[:, c, 512:640],
                                self.s_xw[:, g * 256:(g + 1) * 256], start=True, stop=True)
                    for g in range(2):
                        p0 = g * 64
                        Sg = self.s_Sf[p0:p0 + 64, :].rearrange("p (h q) -> p h q", q=64)
                        self.op("dve", "tensor_tensor", [self.s_Sf, sm_], [self.s_Sf], Sg, Sg,
                                sm_[p0:p0 + 64, 32 + g * 4:32 + g * 4 + 4].unsqueeze(2).to_broadcast([64, 4, 64]), ALU.mult)
                        self.op("dve", "tensor_tensor", [self.s_Sf, self.ps[2 + g]], [self.s_Sf], self.s_Sf[p0:p0 + 64, :],
                                self.s_Sf[p0:p0 + 64, :], self.ps[2 + g][p0:p0 + 64, 0:256], ALU.add)
                    self.op("act", "copy", [self.s_Sf], [self.s_Sb], self.s_Sb[:, :], self.s_Sf[:, :])
                    if d == 0:
                        self.op("dve", "tensor_tensor", [py, yi], [self.s_yf], self.s_yf[:, c, :], py[0:64, :], yi[:, :], ALU.add)
                    else:
                        y, y2, fin = self.s_y, self.s_y2, self.g_fin
                        self.op("dve", "tensor_tensor", [py, yi], [y], y[:, :], py[0:64, :], yi[:, :], ALU.add)
                        self.op("pool", "tensor_tensor", [y, self.s_yf], [y], y[:, :], y[:, :], self.s_yf[:, c, :], ALU.add)
                        self.op("dve", "tensor_tensor", [xt, self.smalls], [y2], y2[:, :].rearrange("p (h q) -> p h q", q=64),
                                xt[:, c, 0:512].rearrange("p (h q) -> p h q", q=64),
                                ssdD[0:64, l * 8:(l + 1) * 8].unsqueeze(2).to_broadcast([64, 8, 64]), ALU.mult)
                        self.op("pool", "tensor_tensor", [y, y2], [y], y[:, :], y[:, :], y2[:, :], ALU.add)
                        self.op("pool", "tensor_tensor", [y, self.s_z], [y], y[:, :], y[:, :], self.s_z[:, c, :], ALU.mult)
                        self.op("act", "activation", [y], [self.s_junk, fin], self.s_junk[:, :], y[:, :], AF.Square,
                                accum_out=fin[:, 0:1])
                        self.op("dve", "tensor_scalar", [fin], [fin], fin[:, 1:2], fin[:, 0:1], 1.0 / 512, EPS,
                                op0=ALU.mult, op1=ALU.add)
                        self.op("act", "activation", [fin], [fin], fin[:, 2:3], fin[:, 1:2], AF.Ln)
                        self.op("act", "activation", [fin], [fin], fin[:, 3:4], fin[:, 2:3], AF.Exp, scale=-0.5)
                        self.op("dve", "scalar_tensor_tensor", [y, fin, self.smalls], [self.s_ost], self.s_ost[:, c, :], y[:, :],
                                fin[:, 3:4], ssdn[0:64, l * 512:(l + 1) * 512], op0=ALU.mult, op1=ALU.mult)
                if d == 0:
                    self.dma("pool", self.s_yfd[t0:t0 + GT, :].rearrange("(c p) f -> p c f", p=CH), self.s_yf[:, 0:NCH, :],
                             [self.s_yf], [self.b_yfd])
                else:
                    self.dma("pool", self.s_o[t0:t0 + GT, 1024:1536].rearrange("(c p) f -> p c f", p=CH),
                             self.s_ost[:, 0:NCH, :], [self.s_ost], [self.b_o])


_PROG = {}


def kernel(**inputs):
    inp = {k: np.asarray(v) for k, v in inputs.items()}
    xp, xs = inp["x_prompt"], inp["x_sample"]
    NCORE = 8
    npp, nps = xp.shape[0] // NCORE, xs.shape[0] // NCORE
    SP, SS = xp.shape[1], xs.shape[1]
    L = inp["w_in"].shape[0]
    key = (npp, SP, nps, SS, L)
    if key not in _PROG:
        prog = Prog([("x_p", "y_p", npp, SP), ("x_s", "y_s", nps, SS)], depth=L)
        _PROG[key] = prog.build()
    nc = _PROG[key]
    small, _ = pack_smalls({k: inp[k] for k in SMALL}, L)
    consts = host_consts(max(SP, SS))
    in_maps = []
    for c in range(NCORE):
        m = {"x_p": np.ascontiguousarray(xp[c * npp:(c + 1) * npp]),
             "x_s": np.ascontiguousarray(xs[c * nps:(c + 1) * nps]), "smalls": small}
        for k in WNAMES:
            m[k] = inp[k]
        m.update(consts)
        in_maps.append(m)
    res = run_bass_kernel_spmd(nc, in_maps, core_ids=list(range(NCORE)))
    yp = np.concatenate([r["y_p"] for r in res.results], 0).astype(np.float32)
    ys = np.concatenate([r["y_s"] for r in res.results], 0).astype(np.float32)
    return (yp, ys)
```

```python
import math
from contextlib import ExitStack

import numpy as np
import concourse.bass as bass
import concourse.mybir as mybir
from concourse.bass_utils import run_bass_kernel_spmd

F32 = mybir.dt.float32
BF16 = mybir.dt.bfloat16
AF = mybir.ActivationFunctionType
ALU = mybir.AluOpType
AX = mybir.AxisListType

D = 1024
DEPTH = 4
INW = 4400
MIXW = 1536
DFF = 4096
EPS = 1e-6
O_GQ, O_GK, O_GV, O_GG, O_LR = 0, 256, 512, 1024, 1536
O_DQ, O_DK, O_DV = 1568, 2080, 2592
O_SZ, O_SX, O_SDT = 3104, 3616, 4384
CH = 64


ALL_BUFS = []


class Buf:
    __slots__ = ("name", "w", "rs")

    def __init__(self, name=""):
        self.name = name
        self.w = None
        self.rs = []
        ALL_BUFS.append(self)


class Sched:
    CE = ("pe", "dve", "act", "pool")

    def __init__(self, nc, dma_ring=8):
        self.nc = nc
        self.streams = {e: [] for e in ("pe", "dve", "act", "pool", "sp")}
        self.cnt = {e: 0 for e in self.CE}
        self.waited = {e: {} for e in self.streams}
        self.dma_ring = dma_ring
        self.dma_i = {"sp": 0, "pool": 0, "act": 0}
        self.dma_last = {}
        self.ninst = 0
        self.label = ""
        self.labels = {e: [] for e in self.streams}
        self.annotate = False

    def _need(self, eng, tok):
        key, val = tok
        if self.waited[eng].get(key, 0) >= val:
            return None
        self.waited[eng][key] = val
        return tok

    def _track(self, tok, reads, writes):
        for b in writes:
            b.w = tok
            b.rs = []
        for b in reads:
            if b not in writes:
                b.rs.append(tok)
                if len(b.rs) > 24:
                    m = {}
                    for k, v in b.rs:
                        m[k] = max(m.get(k, 0), v)
                    b.rs = list(m.items())

    def _deps(self, reads, writes, extra):
        deps = list(extra)
        for b in reads:
            if b.w is not None:
                deps.append(b.w)
        for b in writes:
            if b.w is not None:
                deps.append(b.w)
            deps.extend(b.rs)
        return deps

    def emit(self, eng, fn, reads=(), writes=(), signal=True, extra=()):
        wm = {}
        for t in self._deps(reads, writes, extra):
            if t[0] == eng and (eng == "pe" or t[1] > self.cnt[eng]):
                continue
            t = self._need(eng, t)
            if t is not None:
                wm[t[0]] = max(wm.get(t[0], 0), t[1])
        if signal:
            self.cnt[eng] += 1
            tok = (eng, self.cnt[eng])
        else:
            tok = (eng, self.cnt[eng] + 1)
        self.streams[eng].append((tuple(wm.items()), fn, tok if signal else None))
        if self.annotate:
            self.labels[eng].append(self.label)
        self.ninst += 1
        self._track(tok, reads, writes)
        return tok

    def dma(self, q, fn, reads=(), writes=(), extra=()):
        i = self.dma_i[q]
        self.dma_i[q] += 1
        key = f"dma_{q}_{i % self.dma_ring}"
        val = 16 * (i // self.dma_ring + 1)
        deps = self._deps(reads, writes, extra)
        if i >= self.dma_ring:
            deps.append((key, val - 16))
        wm = {}
        for t in deps:
            t = self._need(q, t)
            if t is not None:
                wm[t[0]] = max(wm.get(t[0], 0), t[1])
        tok = (key, val)
        self.dma_last[key] = val
        self.streams[q].append((tuple(wm.items()), fn, tok))
        if self.annotate:
            self.labels[q].append(self.label)
        self.ninst += 1
        self._track(tok, reads, writes)
        return tok

    def barrier(self):
        toks = [(e, self.cnt[e]) for e in self.CE if self.cnt[e] > 0] + list(self.dma_last.items())
        for eng in self.streams:
            wm = {}
            for t in toks:
                if t[0] == eng:
                    continue
                t = self._need(eng, t)
                if t is not None:
                    wm[t[0]] = max(wm.get(t[0], 0), t[1])
            if wm:
                self.labels[eng].append("barrier")
                if eng == "sp":
                    self.streams[eng].append((tuple(wm.items()), lambda e: e.nop(), None))
                else:
                    self.cnt[eng] += 1
                    self.streams[eng].append((tuple(wm.items()), lambda e: e.nop(), (eng, self.cnt[eng])))
        for b in ALL_BUFS:
            b.w = None
            b.rs = []

    def finish(self, bufs):
        wm = {}
        for b in bufs:
            for t in ([b.w] if b.w is not None else []) + list(b.rs):
                t = self._need("sp", t)
                if t is not None:
                    wm[t[0]] = max(wm.get(t[0], 0), t[1])
        self.labels["sp"].append("finish")
        self.streams["sp"].append((tuple(wm.items()), lambda e: e.nop(), None))

    def replay(self):
        nc = self.nc
        keys = set()
        for st in self.streams.values():
            for waits, fn, tok in st:
                for k, v in waits:
                    keys.add(k)
                if tok is not None:
                    keys.add(tok[0])
        with ExitStack() as es:
            sems = {k: es.enter_context(nc.semaphore(k)) for k in sorted(keys)}
            block = es.enter_context(nc.Block())

            def run(engobj, st, lab=None):
                for i, (waits, fn, tok) in enumerate(st):
                    for k, v in waits:
                        engobj.wait_ge(sems[k], v)
                    ins = fn(engobj)
                    if lab is not None and i < len(lab):
                        ins.annotate(lab[i])
                    if tok is not None:
                        ins.then_inc(sems[tok[0]], 16 if tok[0].startswith("dma_") else 1)

            @block.sync
            def _(e):
                run(e, self.streams["sp"], self.labels["sp"] if self.annotate else None)

            @block.tensor
            def _(e):
                run(e, self.streams["pe"], self.labels["pe"] if self.annotate else None)

            @block.vector
            def _(e):
                run(e, self.streams["dve"], self.labels["dve"] if self.annotate else None)

            @block.scalar
            def _(e):
                run(e, self.streams["act"], self.labels["act"] if self.annotate else None)

            @block.gpsimd
            def _(e):
                run(e, self.streams["pool"], self.labels["pool"] if self.annotate else None)


class T:
    def __init__(self, nc, name, shape, dtype, psum=False):
        self.h = (nc.alloc_psum_tensor if psum else nc.alloc_sbuf_tensor)(name, list(shape), dtype)
        self.b = Buf(name)

    def __getitem__(self, k):
        return self.h[k]


class V(T):
    def __init__(self, ap, b):
        self.h = ap
        self.b = b


class Arena:
    def __init__(self, nc, name, nbytes):
        self.nc = nc
        self.t = nc.alloc_sbuf_tensor(name, [128, nbytes // 2], BF16)
        self.base = nc.lookup_mloc(self.t).addr
        self.size = nbytes
        self.off = 0
        self.peak = 0

    def reset(self):
        self.off = 0

    def alloc(self, name, shape, dtype):
        esz = 4 if dtype == F32 else 2
        n = int(np.prod(shape[1:]))
        nb = (n * esz + 63) // 64 * 64
        assert self.off + nb <= self.size, (name, self.off, nb, self.size)
        h = self.nc.alloc_sbuf_tensor_at(name, list(shape), dtype, offset=self.base + self.off)
        self.off += nb
        self.peak = max(self.peak, self.off)
        return V(h, Buf(name))


def lambda_init(l):
    return 0.8 - 0.6 * math.exp(-0.3 * l)


WNAMES = ["w_in", "w_out", "w_mlp1", "w_mlp2"]
WSHAPES = {"w_in": (D, INW), "w_out": (MIXW, D), "w_mlp1": (D, DFF), "w_mlp2": (DFF, D)}
SMALL = {"norm1": (D,), "gla_wg_f": (16, 256), "gla_bg_f": (256,), "gla_wg_b": (16, 256), "gla_bg_b": (256,),
         "gla_norm": (128,), "diff_qnorm": (64,), "diff_knorm": (64,), "diff_lq1": (64,), "diff_lk1": (64,),
         "diff_lq2": (64,), "diff_lk2": (64,), "diff_subln": (128,), "ssd_conv_w": (5, 768), "ssd_conv_b": (768,),
         "ssd_dt_bias_f": (8,), "ssd_dt_bias_b": (8,), "ssd_A_log_f": (8,), "ssd_A_log_b": (8,), "ssd_D": (8,),
         "ssd_norm": (512,), "norm2": (D,)}


def host_consts(smax):
    ident = np.eye(128, dtype=np.float32)
    bones = np.zeros((128, 128), np.float32)
    bones[:64, :64] = 1.0 / 64
    bones[64:, 64:] = 1.0 / 64
    rmat = np.zeros((128, 128), np.float32)
    for blk in (0, 64):
        for d in range(8):
            rmat[blk + d + 8, blk + d] = -1.0
            rmat[blk + d, blk + d + 8] = 1.0
    inv = (500000.0 ** (-np.arange(0, 16, 2, dtype=np.float32) / np.float32(16))).astype(np.float32)
    ang = (np.arange(smax, dtype=np.float32)[:, None] * inv[None, :]).astype(np.float32)
    cos = np.ones((128, smax), np.float32)
    sin = np.zeros((128, smax), np.float32)
    for blk in (0, 64):
        for d in range(16):
            cos[blk + d] = np.cos(ang[:, d % 8])
            sin[blk + d] = np.sin(ang[:, d % 8])
    tri = np.zeros((64, 2, 64), np.float32)
    jj, ii = np.meshgrid(np.arange(64), np.arange(64), indexing="ij")
    tri[:, 0, :] = (jj <= ii)
    tri[:, 1, :] = (jj >= ii)
    hsel = np.zeros((16, 2), np.float32)
    hsel[:8, 0] = 1.0
    hsel[8:, 1] = 1.0
    return {"c_ident": ident, "c_bones": bones, "c_rmat": rmat, "c_cos": cos, "c_sin": sin, "c_tri": tri,
            "c_hsel": hsel}


def pack_smalls(p, L):
    cols = {}
    parts = []
    off = 0

    def add(name, arr):
        nonlocal off
        arr = np.ascontiguousarray(arr, dtype=np.float32)
        a = np.zeros((128, int(np.prod(arr.shape[1:]))), np.float32)
        a[:arr.shape[0]] = arr.reshape(arr.shape[0], -1)
        cols[name] = (off, a.shape[1])
        parts.append(a)
        off += a.shape[1]

    add("g1", p["norm1"].reshape(L, 8, 128).transpose(2, 0, 1))
    add("g2", p["norm2"].reshape(L, 8, 128).transpose(2, 0, 1))
    add("wg", np.stack([p["gla_wg_f"], p["gla_wg_b"]], 1).transpose(2, 0, 1, 3))
    add("bg", np.stack([p["gla_bg_f"], p["gla_bg_b"]], 1).reshape(L, 2, 2, 128).transpose(3, 0, 1, 2))
    add("qn", np.tile(p["diff_qnorm"], (1, 2)).T)
    add("kn", np.tile(p["diff_knorm"], (1, 2)).T)
    lql = np.stack([p["diff_lq1"], p["diff_lk1"], p["diff_lq2"], p["diff_lk2"]], 1)
    add("lql", np.broadcast_to(lql[None], (128, L, 4, 64)))
    add("dtb", np.concatenate([p["ssd_dt_bias_f"], p["ssd_dt_bias_b"]], 1).T)
    add("alog", np.concatenate([p["ssd_A_log_f"], p["ssd_A_log_b"]], 1).T)
    add("convw", p["ssd_conv_w"].reshape(L, 5, 6, 128).transpose(3, 0, 2, 1))
    add("convb", p["ssd_conv_b"].reshape(L, 6, 128).transpose(2, 0, 1))
    add("glan", np.broadcast_to(p["gla_norm"][None], (128, L, 128)))
    add("subln", np.broadcast_to(p["diff_subln"][None], (128, L, 128)))
    add("ssdn", np.broadcast_to(p["ssd_norm"][None], (128, L, 512)))
    add("ssdD", np.broadcast_to(p["ssd_D"][None], (128, L, 8)))
    return np.concatenate(parts, 1), cols


def smalls_cols(L):
    dummy = {k: np.zeros((L,) + v, np.float32) for k, v in SMALL.items()}
    return pack_smalls(dummy, L)[1]


class Prog:
    def __init__(self, seq_groups, depth=DEPTH, TT=512, dbg=False, mixers=("gla", "diff", "ssd")):
        self.L = depth
        self.TT = TT
        self.dbg = dbg
        self.mixers = mixers
        self.groups = seq_groups
        self.smax = max(g[3] for g in seq_groups)
        nc = self.nc = bass.Bass("TRN2", target_bir_lowering=False)
        self.S = Sched(nc)
        import os
        self.S.annotate = bool(os.environ.get("K_ANNOTATE"))
        L = depth
        skind = "ExternalOutput" if dbg else "Internal"
        self.xin, self.yout = {}, {}
        for (iname, oname, n, S) in seq_groups:
            self.xin[iname] = nc.dram_tensor(iname, [n, S, D], F32, kind="ExternalInput").ap()
            self.yout[oname] = nc.dram_tensor(oname, [n, S, D], F32, kind="ExternalOutput").ap()
        self.wf = {k: nc.dram_tensor(k, [L] + list(WSHAPES[k]), F32, kind="ExternalInput").ap() for k in WNAMES}
        self.wb = {k: nc.dram_tensor("b_" + k, [L] + list(WSHAPES[k]), BF16, kind="Internal").ap() for k in WNAMES}
        self.wb_buf = {k: [Buf() for _ in range(L)] for k in WNAMES}
        self.scols = smalls_cols(L)
        nsm = sum(v[1] for v in self.scols.values())
        self.smalls_d = nc.dram_tensor("smalls", [128, nsm], F32, kind="ExternalInput").ap()
        self.cd = {k: nc.dram_tensor(k, [128, 128], F32, kind="ExternalInput").ap()
                   for k in ("c_ident", "c_bones", "c_rmat")}
        self.cd["c_tri"] = nc.dram_tensor("c_tri", [64, 2, 64], F32, kind="ExternalInput").ap()
        self.cd["c_hsel"] = nc.dram_tensor("c_hsel", [16, 2], F32, kind="ExternalInput").ap()
        self.cd["c_cos"] = nc.dram_tensor("c_cos", [128, self.smax], F32, kind="ExternalInput").ap()
        self.cd["c_sin"] = nc.dram_tensor("c_sin", [128, self.smax], F32, kind="ExternalInput").ap()
        sm = self.smax
        def scr(name, shape, dt):
            return nc.dram_tensor(name, list(shape), dt, kind=skind).ap(), Buf(name)
        self.s_dq, self.b_dq = scr("s_dq", [4, 128, sm], BF16)
        self.s_dk, self.b_dk = scr("s_dk", [4, 128, sm], BF16)
        self.s_dv, self.b_dv = scr("s_dv", [sm, 512], BF16)
        self.s_gq, self.b_gq = scr("s_gq", [2, 128, sm], BF16)
        self.s_gk, self.b_gk = scr("s_gk", [2, 128, sm], BF16)
        self.s_gv, self.b_gv = scr("s_gv", [sm, 512], BF16)
        self.s_gg, self.b_gg = scr("s_gg", [sm, 512], BF16)
        self.s_gb, self.b_gb = scr("s_gb", [2, 2, 128, sm], F32)
        self.s_sx, self.b_sx = scr("s_sx", [6, 128, sm], BF16)
        self.s_sz, self.b_sz = scr("s_sz", [sm, 512], BF16)
        self.s_dt, self.b_dt = scr("s_dt", [2, 16, sm], F32)
        self.s_o, self.b_o = scr("s_o", [sm, MIXW], BF16)
        self.s_gof, self.b_gof = scr("s_gof", [sm, 512], F32)
        self.s_sxt, self.b_sxt = scr("s_sxt", [sm, 640], BF16)
        self.s_sbc, self.b_sbc = scr("s_sbc", [2, 128, sm], BF16)
        self.s_acd, self.b_acd = scr("s_acd", [16, sm], F32)
        self.s_yfd, self.b_yfd = scr("s_yfd", [sm, 512], F32)
        self._alloc()

    def op(self, eng, name, reads, writes, *args, signal=True, **kw):
        return self.S.emit(eng, lambda e: getattr(e, name)(*args, **kw),
                           reads=[t.b if isinstance(t, T) else t for t in reads],
                           writes=[t.b if isinstance(t, T) else t for t in writes], signal=signal)

    def dma(self, q, out, in_, reads, writes, **kw):
        return self.S.dma(q, lambda e: e.dma_start(out=out, in_=in_, **kw),
                          reads=[t.b if isinstance(t, T) else t for t in reads],
                          writes=[t.b if isinstance(t, T) else t for t in writes])

    def sm(self, name):
        o, n = self.scols[name]
        return self.smalls[:, o:o + n]

    def _alloc(self):
        nc, L, TT = self.nc, self.L, self.TT
        NS = TT // 128
        self.NS = NS
        t = lambda name, shape, dt: T(nc, name, shape, dt)
        nsm = sum(v[1] for v in self.scols.values())
        self.smalls = t("smalls_sb", [128, nsm], F32)
        self.ident = t("ident", [128, 128], BF16)
        self.bones = t("bones", [128, 128], BF16)
        self.rmat = t("rmat", [128, 128], BF16)
        self.cst_f = t("cst_f", [128, 3, 128], F32)
        self.wgb = t("wgb", [16, L * 2 * 256], BF16)
        self.nbg = t("nbg", [128, L * 4], F32)
        self.qg = t("qg", [128, L], F32)
        self.aneg = t("aneg", [16, L], F32)
        self.lam = t("lam", [128, L], F32)
        self.lamtmp = t("lamtmp", [128, 4 * 64], F32)
        self.lam2 = t("lam2", [128, 4], F32)
        ar = self.arena = Arena(nc, "arena", 141 * 1024)
        ta = lambda name, shape, dt: ar.alloc(name, shape, dt)
        self.x_sb = ta("x_sb", [128, NS, D], F32)
        self.xn = ta("xn", [128, NS, D], BF16)
        self.junk = ta("junk", [128, D], BF16)
        self.ss = ta("ss", [128, NS], F32)
        self.rstd = ta("rstd", [128, NS], F32)
        self.hT = ta("hT", [128, 8, TT], BF16)
        self.h2T = ta("h2T", [128, 8, TT], BF16)
        self.o_sb = ta("o_sb", [128, NS, MIXW], BF16)
        self.oT = ta("oT", [128, 12, TT], BF16)
        self.hid = ta("hid", [128, 32, TT], BF16)
        self.rtmp = [ta(f"rtmp{i}", [128, TT], F32) for i in range(2)]
        self.wbuf = [t(f"wbuf{i}", [128, 6144], BF16) for i in range(2)]
        self.wi = 0
        self.stg = [ta(f"stg{i}", [128, TT], BF16) for i in range(4)]
        self.stgi = 0
        self.stgf = [ta(f"stgf{i}", [128, TT], F32) for i in range(2)]
        self.stgfi = 0
        self.stgt = [ta(f"stgt{i}", [128, NS, 512], BF16) for i in range(2)]
        self.stgti = 0
        self.fa = [ta(f"fa{i}", [128, TT], F32) for i in range(4)]
        self.sqb = ta("sqb", [128, TT], BF16)
        self.qnb = ta("qnb", [128, TT], BF16)
        self.cos_sb = ta("cos_sb", [128, TT], F32)
        self.sin_sb = ta("sin_sb", [128, TT], F32)
        self.lr_sb = [ta(f"lr_sb{i}", [16, TT], BF16) for i in range(2)]
        self.dt_sb = [ta(f"dt_sb{i}", [16, TT], F32) for i in range(3)]
        self.ps = [T(nc, f"ps{i}", [128, 512], F32, psum=True) for i in range(6)]
        self.pstt = [T(nc, f"pst{i}", [128, 1024], BF16, psum=True) for i in range(2)]
        self.psx = V(self.pstt[0][:, :].bitcast(F32), self.pstt[0].b)
        self.psy = V(self.pstt[1][:, :].bitcast(F32), self.pstt[1].b)
        sm = self.smax
        self.dense_peak = ar.off
        ar.reset()
        self.QB = min(512, sm)
        self.dqT = [ta("dqT0", [128, sm], BF16)]
        self.dkT = [ta("dkT0", [128, sm], BF16)]
        self.dva = [ta(f"dva{i}", [128, sm // 128, 132], BF16) for i in range(1)]
        self.pt = [ta(f"pt{i}", [128, self.QB], BF16) for i in range(3)]
        self.pti = 0
        self.dfin = ta("dfin", [128, 8], F32)
        self.dft = [ta(f"dft{i}", [128, 128], F32) for i in range(2)]
        self.djunk = ta("djunk", [128, 128], BF16)
        self.dost = [ta(f"dost{i}", [128, self.QB // 128, 128], BF16) for i in range(2)]
        self.dosti = 0
        self.sublnS = t("sublnS", [128, L * 128], F32)
        GT = self.GT = min(256, sm)
        NCH = GT // CH
        self.g_q = ta("g_q", [128, GT], BF16)
        self.g_k = ta("g_k", [128, GT], BF16)
        self.g_sp = ta("g_sp", [128, GT], F32)
        self.g_cs = ta("g_cs", [128, GT], F32)
        self.g_eb = ta("g_eb", [128, GT], F32)
        self.g_enb = ta("g_enb", [128, GT], F32)
        self.g_qt = ta("g_qt", [128, GT], BF16)
        self.g_kt = ta("g_kt", [128, GT], BF16)
        self.g_kh = ta("g_kh", [128, GT], BF16)
        self.g_ktok = ta("g_ktok", [64, NCH, 128], BF16)
        self.g_v = ta("g_v", [64, NCH, 256], BF16)
        self.g_g = ta("g_g", [64, NCH, 256], BF16)
        self.g_of = ta("g_of", [64, NCH, 256], F32)
        self.g_ost = ta("g_ost", [64, NCH, 256], BF16)
        self.g_am = [ta(f"g_am{i}", [64, 64], BF16) for i in range(2)]
        self.g_Sf = ta("g_Sf", [128, 128], F32)
        self.g_Sb = ta("g_Sb", [128, 128], BF16)
        self.g_o = [ta(f"g_o{i}", [64, 128], F32) for i in range(2)]
        self.g_oi = [ta(f"g_oi{i}", [64, 128], F32) for i in range(2)]
        self.g_fin = ta("g_fin", [64, 8], F32)
        self.g_junk = ta("g_junk", [64, 128], BF16)
        self.g_mask = t("g_mask", [128, GT], F32)
        self.tri = t("tri", [64, 2, 64], F32)
        self.c_in = [ta(f"c_in{i}", [128, GT + 4], BF16) for i in range(2)]
        self.c_acc = [ta(f"c_acc{i}", [128, GT], F32) for i in range(2)]
        self.c_cv = [ta(f"c_cv{i}", [128, GT], BF16) for i in range(2)]
        self.c_xt = [ta(f"c_xt{i}", [64, NCH, 128], BF16) for i in range(2)]
        self.c_par = 0
        self.s_la = ta("s_la", [16, GT], F32)
        self.s_dtt = ta("s_dtt", [16, GT], F32)
        self.s_cs = ta("s_cs", [16, GT], F32)
        self.s_rcs = ta("s_rcs", [16, GT], F32)
        self.s_ac = ta("s_ac", [16, GT], F32)
        self.s_seg = [ta(f"s_seg{i}", [128, 8, CH], F32) for i in range(2)]
        self.s_t1 = [ta(f"s_t1{i}", [64, 8, CH], F32) for i in range(2)]
        self.s_t2 = [ta(f"s_t2{i}", [64, 8, CH], F32) for i in range(2)]
        self.s_mt = [ta(f"s_mt{i}", [64, 8, CH], BF16) for i in range(2)]
        self.s_par = 0
        self.s_tok = ta("s_tok", [64, NCH, 32], F32)
        self.s_sm = [ta(f"s_sm{i}", [128, 64], F32) for i in range(2)]
        self.s_xt = ta("s_xt", [64, NCH, 640], BF16)
        self.s_xw = [ta(f"s_xw{i}", [64, 512], BF16) for i in range(2)]
        self.s_bt = ta("s_bt", [128, GT], BF16)
        self.s_ct = ta("s_ct", [128, GT], BF16)
        self.s_Sf = ta("s_Sf", [128, 256], F32)
        self.s_Sb = ta("s_Sb", [128, 256], BF16)
        self.s_yi = ta("s_yi", [64, 512], F32)
        self.s_yf = ta("s_yf", [64, NCH, 512], F32)
        self.s_z = ta("s_z", [64, NCH, 512], BF16)
        self.s_y = ta("s_y", [64, 512], F32)
        self.s_y2 = ta("s_y2", [64, 512], F32)
        self.s_ost = ta("s_ost", [64, NCH, 512], BF16)
        self.s_junk = ta("s_junk", [64, 512], BF16)
        self.mbias = t("mbias", [64, 2, CH], F32)
        self.hsel = t("hsel", [16, 2], F32)
        self.cvt_in = [V(self.wbuf[i][:, 0:2200].bitcast(F32), self.wbuf[i].b) for i in range(2)]
        self.cvt_out = [V(self.wbuf[i][:, 2200:3300], self.wbuf[i].b) for i in range(2)]

    def next_w(self):
        w = self.wbuf[self.wi % len(self.wbuf)]
        self.wi += 1
        return w

    def next_stg(self):
        w = self.stg[self.stgi % len(self.stg)]
        self.stgi += 1
        return w

    def prologue(self):
        L = self.L
        self.S.label = "prologue"
        self.dma("sp", self.smalls[:], self.smalls_d, [], [self.smalls])
        for i, k in enumerate(("c_ident", "c_bones", "c_rmat")):
            self.dma("sp", self.cst_f[:, i, :], self.cd[k], [], [self.cst_f])
        for i, tt in enumerate((self.ident, self.bones, self.rmat)):
            self.op("dve", "tensor_copy", [self.cst_f], [tt], tt[:], self.cst_f[:, i, :])
        self.op("dve", "tensor_copy", [self.smalls], [self.wgb], self.wgb[:], self.sm("wg")[0:16, :])
        self.op("dve", "tensor_scalar", [self.smalls], [self.nbg], self.nbg[:], self.sm("bg"), -1.0, None,
                op0=ALU.mult)
        self.op("dve", "tensor_scalar", [self.smalls], [self.qg], self.qg[:], self.sm("qn"), 0.125, None,
                op0=ALU.mult)
        self.op("act", "activation", [self.smalls], [self.aneg], self.aneg[:], self.sm("alog")[0:16, :], AF.Exp)
        self.op("dve", "tensor_scalar", [self.aneg], [self.aneg], self.aneg[:], self.aneg[:], -1.0, None,
                op0=ALU.mult)
        lql = self.sm("lql")
        for l in range(L):
            for j in range(2):
                a = lql[:, (l * 4 + 2 * j) * 64:(l * 4 + 2 * j + 1) * 64]
                b = lql[:, (l * 4 + 2 * j + 1) * 64:(l * 4 + 2 * j + 2) * 64]
                self.op("dve", "tensor_tensor", [self.smalls], [self.lamtmp], self.lamtmp[:, j * 64:(j + 1) * 64],
                        a, b, ALU.mult)
                self.op("dve", "tensor_reduce", [self.lamtmp], [self.lam2], self.lam2[:, j:j + 1],
                        self.lamtmp[:, j * 64:(j + 1) * 64], AX.X, ALU.add)
            self.op("act", "activation", [self.lam2], [self.lam2], self.lam2[:, 2:4], self.lam2[:, 0:2], AF.Exp)
            self.op("dve", "tensor_tensor", [self.lam2], [self.lam2], self.lam2[:, 0:1], self.lam2[:, 2:3],
                    self.lam2[:, 3:4], ALU.subtract)
            self.op("dve", "tensor_scalar", [self.lam2], [self.lam], self.lam[:, l:l + 1], self.lam2[:, 0:1],
                    float(lambda_init(l)), None, op0=ALU.add)
        for l in range(L):
            self.op("dve", "tensor_scalar", [self.smalls], [self.sublnS], self.sublnS[:, l * 128:(l + 1) * 128],
                    self.sm("subln")[:, l * 128:(l + 1) * 128], float(1.0 - lambda_init(l)), None, op0=ALU.mult)
        if self.mixers and len(self.mixers) < 3:
            self.op("dve", "memset", [], [self.o_sb], self.o_sb[:, :, :], 0.0)
            for t0 in range(0, self.smax, self.TT):
                self.dma("sp", self.s_o[t0:t0 + self.TT, :].rearrange("(s p) d -> p s d", p=128), self.o_sb[:, :, :],
                         [self.o_sb], [self.b_o])
        self.op("pool", "memset", [], [self.g_mask], self.g_mask[:, :], 1.0)
        self.op("pool", "memset", [self.g_mask], [self.g_mask],
                self.g_mask[:, :].rearrange("p (c t) -> p c t", t=CH)[:, :, 0:1], 0.0)
        self.dma("sp", self.tri[:, :, :], self.cd["c_tri"], [], [self.tri])
        self.op("dve", "tensor_scalar", [self.tri], [self.mbias], self.mbias[:, :, :], self.tri[:, :, :], 30000.0, -30000.0,
                op0=ALU.mult, op1=ALU.add)
        self.dma("sp", self.hsel[:, :], self.cd["c_hsel"], [], [self.hsel])
        i = 0
        for l in range(L):
            for k in WNAMES:
                r, c = WSHAPES[k]
                per = r * c // 128
                src = self.wf[k][l].rearrange("(p a) c -> p (a c)", p=128)
                dst = self.wb[k][l].rearrange("(p a) c -> p (a c)", p=128)
                cw = 1100 if k == "w_in" else 1024
                for j in range(per // cw):
                    ci, co = self.cvt_in[i % 2], self.cvt_out[i % 2]
                    self.dma("sp", ci[:, 0:cw], src[:, j * cw:(j + 1) * cw], [], [ci])
                    eng = ("dve", "act", "pool")[i % 3]
                    if eng == "act":
                        self.op("act", "copy", [ci], [co], co[:, 0:cw], ci[:, 0:cw])
                    else:
                        self.op(eng, "tensor_copy", [ci], [co], co[:, 0:cw], ci[:, 0:cw])
                    self.dma("sp", dst[:, j * cw:(j + 1) * cw], co[:, 0:cw], [co], [self.wb_buf[k][l]])
                    i += 1

    def norm_T(self, gname, l, dst):
        NS, TT = self.NS, self.TT
        self.S.label = "norm_" + gname
        for s in range(NS):
            self.op("act", "activation", [self.x_sb], [self.junk, self.ss], self.junk[:], self.x_sb[:, s, :],
                    AF.Square, accum_out=self.ss[:, s:s + 1])
        self.op("dve", "tensor_scalar", [self.ss], [self.rstd], self.rstd[:], self.ss[:], 1.0 / D, EPS,
                op0=ALU.mult, op1=ALU.add)
        self.op("act", "activation", [self.rstd], [self.rstd], self.rstd[:], self.rstd[:], AF.Ln)
        self.op("act", "activation", [self.rstd], [self.rstd], self.rstd[:], self.rstd[:], AF.Exp, scale=-0.5)
        for s in range(NS):
            if s % 2 == 0:
                self.op("dve", "tensor_scalar", [self.x_sb, self.rstd], [self.xn], self.xn[:, s, :],
                        self.x_sb[:, s, :], self.rstd[:, s:s + 1], None, op0=ALU.mult)
            else:
                self.op("act", "activation", [self.x_sb, self.rstd], [self.xn], self.xn[:, s, :],
                        self.x_sb[:, s, :], AF.Copy, scale=self.rstd[:, s:s + 1])
        g = self.sm(gname)
        for kc in range(8):
            pb = self.pstt[kc % 2]
            for s in range(NS):
                self.op("pe", "transpose", [self.xn, self.ident], [pb],
                        pb[:, s * 128: (s + 1) * 128], self.xn[:, s, kc * 128:(kc + 1) * 128],
                        self.ident[:], signal=(s == NS - 1))
            gk = g[:, l * 8 + kc: l * 8 + kc + 1]
            if kc % 2 == 0:
                self.op("dve", "tensor_scalar", [pb, self.smalls], [dst], dst[:, kc, :],
                        pb[:, 0:TT], gk, None, op0=ALU.mult)
            else:
                self.op("act", "activation", [pb, self.smalls], [dst], dst[:, kc, :],
                        pb[:, 0:TT], AF.Copy, scale=gk)

    def load_w(self, k, l, c0, ncols, r0=0, nrows=None):
        rows = WSHAPES[k][0] if nrows is None else nrows
        nch = rows // 128
        w = self.next_w()
        src = self.wb[k][l][r0:r0 + rows, c0:c0 + ncols].rearrange("(kc p) n -> p kc n", p=128)
        view = w[:, 0:nch * ncols].rearrange("p (kc n) -> p kc n", kc=nch)
        self.dma("sp", view, src, [self.wb_buf[k][l]], [w])
        return w, view

    def phase_a(self, l, t0):
        TT, NS = self.TT, self.NS
        self.S.label = "A"
        hT = self.hT
        fmi = [0]

        def fm_acc(w, wv, j, m=128):
            ps = self.ps[4 + fmi[0] % 2]
            fmi[0] += 1
            for kc in range(8):
                self.op("pe", "matmul", [w, hT], [ps], ps[0:m, 0:TT], wv[:, kc, j * 128:j * 128 + m], hT[:, kc, :],
                        start=(kc == 0), stop=(kc == 7), signal=(kc == 7))
            return ps

        evi = [0]

        def evac_copy(ps, dst_t, dst_ap, src_ap, scale=None):
            evi[0] += 1
            if evi[0] % 2 == 0:
                self.op("dve", "tensor_scalar", [ps], [dst_t], dst_ap, src_ap, 1.0 if scale is None else scale, None,
                        op0=ALU.mult)
            else:
                self.op("act", "activation", [ps], [dst_t], dst_ap, src_ap, AF.Copy,
                        scale=1.0 if scale is None else scale)

        if "gla" in self.mixers:
            w, wv = self.load_w("w_in", l, O_GQ, 512)
            for j in range(4):
                ps = fm_acc(w, wv, j)
                st = self.next_stg()
                evac_copy(ps, st, st[:, 0:TT], ps[:, 0:TT], scale=(0.125 if j < 2 else None))
                dst = (self.s_gq if j < 2 else self.s_gk)[j % 2][:, t0:t0 + TT]
                self.dma("pool", dst, st[:, 0:TT], [st], [self.b_gq if j < 2 else self.b_gk])
        w = self.next_w()
        wv = w[:, 0:8 * 48].rearrange("p (kc n) -> p kc n", kc=8)
        self.dma("sp", wv[:, :, 0:32], self.wb["w_in"][l][:, O_LR:O_LR + 32].rearrange("(kc p) n -> p kc n", p=128),
                 [self.wb_buf["w_in"][l]], [w])
        self.dma("sp", wv[:, :, 32:48], self.wb["w_in"][l][:, O_SDT:O_SDT + 16].rearrange("(kc p) n -> p kc n", p=128),
                 [self.wb_buf["w_in"][l]], [w])
        if "gla" in self.mixers:
            for d in range(2):
                ps = self.ps[4 + fmi[0] % 2]
                fmi[0] += 1
                for kc in range(8):
                    self.op("pe", "matmul", [w, hT], [ps], ps[0:16, 0:TT], wv[:, kc, d * 16:(d + 1) * 16], hT[:, kc, :],
                            start=(kc == 0), stop=(kc == 7), signal=(kc == 7))
                self.op("dve", "tensor_copy", [ps], [self.lr_sb[d]], self.lr_sb[d][:, 0:TT], ps[0:16, 0:TT])
            for d in range(2):
                for m in range(2):
                    ps = self.ps[4 + fmi[0] % 2]
                    fmi[0] += 1
                    wgo = ((l * 2 + d) * 256) + m * 128
                    self.op("pe", "matmul", [self.wgb, self.lr_sb[d]], [ps], ps[:, 0:TT], self.wgb[:, wgo:wgo + 128],
                            self.lr_sb[d][:, 0:TT], start=True, stop=True)
                    f = self.stgf[self.stgfi % 2]
                    self.stgfi += 1
                    bi = (l * 2 + d) * 2 + m
                    self.op("act", "activation", [ps, self.nbg], [f], f[:, 0:TT], ps[:, 0:TT], AF.Exp,
                            bias=self.nbg[:, bi:bi + 1], scale=-1.0)
                    self.op("act", "activation", [f], [f], f[:, 0:TT], f[:, 0:TT], AF.Ln, bias=1.0)
                    self.dma("pool", self.s_gb[d, m][:, t0:t0 + TT], f[:, 0:TT], [f], [self.b_gb])
        if "ssd" in self.mixers:
            ps = self.ps[4 + fmi[0] % 2]
            fmi[0] += 1
            for kc in range(8):
                self.op("pe", "matmul", [w, hT], [ps], ps[0:16, 0:TT], wv[:, kc, 32:48], hT[:, kc, :],
                        start=(kc == 0), stop=(kc == 7), signal=(kc == 7))
            d0, d1, d2 = self.dt_sb
            dtb = self.sm("dtb")
            self.op("act", "activation", [ps, self.smalls], [d0], d0[:, 0:TT], ps[0:16, 0:TT], AF.Exp,
                    bias=dtb[0:16, l:l + 1], scale=1.0)
            self.op("act", "activation", [d0], [d1], d1[:, 0:TT], d0[:, 0:TT], AF.Ln, bias=1.0)
            self.op("dve", "tensor_scalar", [d1, self.aneg], [d2], d2[:, 0:TT], d1[:, 0:TT], self.aneg[:, l:l + 1],
                    None, op0=ALU.mult)
            self.dma("pool", self.s_dt[0][:, t0:t0 + TT], d1[:, 0:TT], [d1], [self.b_dt])
            self.dma("pool", self.s_dt[1][:, t0:t0 + TT], d2[:, 0:TT], [d2], [self.b_dt])
            for (c0, nj, jb) in ((O_SX, 4, 0), (O_SX + 512, 2, 4)):
                w2, wv2 = self.load_w("w_in", l, c0, nj * 128)
                for j in range(nj):
                    ps = fm_acc(w2, wv2, j)
                    st = self.next_stg()
                    evac_copy(ps, st, st[:, 0:TT], ps[:, 0:TT])
                    self.dma("pool", self.s_sx[jb + j][:, t0:t0 + TT], st[:, 0:TT], [st], [self.b_sx])
        if "diff" in self.mixers:
            self.dma("sp", self.cos_sb[:, 0:TT], self.cd["c_cos"][:, t0:t0 + TT], [], [self.cos_sb])
            self.dma("sp", self.sin_sb[:, 0:TT], self.cd["c_sin"][:, t0:t0 + TT], [], [self.sin_sb])
            for qk in range(2):
                w2, wv2 = self.load_w("w_in", l, O_DQ if qk == 0 else O_DK, 512)
                gain = self.qg[:, l:l + 1] if qk == 0 else self.sm("kn")[:, l:l + 1]
                gain_t = self.qg if qk == 0 else self.smalls
                for j in range(4):
                    ps = fm_acc(w2, wv2, j)
                    p6 = self.ps[0]
                    f0, f1, f2, f3 = self.fa
                    self.op("act", "activation", [ps], [self.sqb], self.sqb[:, 0:TT], ps[:, 0:TT], AF.Square)
                    self.op("pe", "matmul", [self.bones, self.sqb], [p6], p6[:, 0:TT], self.bones[:], self.sqb[:, 0:TT],
                            start=True, stop=True)
                    self.op("dve", "tensor_scalar", [p6], [f0], f0[:, 0:TT], p6[:, 0:TT], EPS, None,
                            op0=ALU.add)
                    self.op("act", "activation", [f0], [f0], f0[:, 0:TT], f0[:, 0:TT], AF.Ln)
                    self.op("act", "activation", [f0], [f0], f0[:, 0:TT], f0[:, 0:TT], AF.Exp, scale=-0.5)
                    self.op("dve", "scalar_tensor_tensor", [ps, gain_t, f0], [f1], f1[:, 0:TT], ps[:, 0:TT], gain,
                            f0[:, 0:TT], op0=ALU.mult, op1=ALU.mult)
                    self.op("act", "copy", [f1], [self.qnb], self.qnb[:, 0:TT], f1[:, 0:TT])
                    self.op("pe", "matmul", [self.rmat, self.qnb], [p6], p6[:, 0:TT], self.rmat[:], self.qnb[:, 0:TT],
                            start=True, stop=True)
                    self.op("pool", "tensor_tensor", [f1, self.cos_sb], [f2], f2[:, 0:TT], f1[:, 0:TT],
                            self.cos_sb[:, 0:TT], ALU.mult)
                    self.op("dve", "tensor_tensor", [p6, self.sin_sb], [f3], f3[:, 0:TT], p6[:, 0:TT],
                            self.sin_sb[:, 0:TT], ALU.mult)
                    st = self.next_stg()
                    self.op("pool", "tensor_tensor", [f2, f3], [st], st[:, 0:TT], f2[:, 0:TT], f3[:, 0:TT], ALU.add)
                    self.dma("pool", (self.s_dq if qk == 0 else self.s_dk)[j][:, t0:t0 + TT], st[:, 0:TT], [st],
                             [self.b_dq if qk == 0 else self.b_dk])
        tmg = []
        if "gla" in self.mixers:
            tmg += [(O_GV, self.s_gv, self.b_gv, False), (O_GG, self.s_gg, self.b_gg, True)]
        if "diff" in self.mixers:
            tmg += [(O_DV, self.s_dv, self.b_dv, False)]
        if "ssd" in self.mixers:
            tmg += [(O_SZ, self.s_sz, self.b_sz, True)]
        for (c0, dst, dbuf, silu) in tmg:
            w2, wv2 = self.load_w("w_in", l, c0, 512)
            sg = self.stgt[self.stgti % 2]
            self.stgti += 1
            for s in range(NS):
                ps = self.ps[s % 4]
                for kc in range(8):
                    self.op("pe", "matmul", [w2, hT], [ps], ps[:, :], hT[:, kc, s * 128:(s + 1) * 128], wv2[:, kc, :],
                            start=(kc == 0), stop=(kc == 7), signal=(kc == 7))
                if silu:
                    self.op("act", "activation", [ps], [sg], sg[:, s, :], ps[:, :], AF.Silu)
                else:
                    evac_copy(ps, sg, sg[:, s, :], ps[:, :])
            self.dma("pool", dst[t0:t0 + TT, :].rearrange("(s p) c -> p s c", p=128), sg[:, :, :], [sg], [dbuf])

    def phase_c(self, l, xsrc, ydst, t0):
        TT, NS = self.TT, self.NS
        self.S.label = "C"
        x_sb = self.x_sb
        self.dma("sp", x_sb[:, :, :], xsrc[t0:t0 + TT, :].rearrange("(s p) d -> p s d", p=128), [self.b_y], [x_sb])
        if self.mixers:
            o_sb, oT = self.o_sb, self.oT
            self.dma("sp", o_sb[:, :, :], self.s_o[t0:t0 + TT, :].rearrange("(s p) d -> p s d", p=128),
                     [self.b_o], [o_sb])
            for fc in range(12):
                pb = self.pstt[fc % 2]
                for s in range(NS):
                    self.op("pe", "transpose", [o_sb, self.ident], [pb],
                            pb[:, s * 128: (s + 1) * 128], o_sb[:, s, fc * 128:(fc + 1) * 128],
                            self.ident[:], signal=(s == NS - 1))
                if fc % 2 == 0:
                    self.op("dve", "tensor_copy", [pb], [oT], oT[:, fc, :], pb[:, 0:TT])
                else:
                    self.op("act", "copy", [pb], [oT], oT[:, fc, :], pb[:, 0:TT])
            for n in range(2):
                w, wv = self.load_w("w_out", l, n * 512, 512)
                for s in range(NS):
                    ps = self.ps[s % 4]
                    for fc in range(12):
                        self.op("pe", "matmul", [w, oT], [ps], ps[:, :], oT[:, fc, s * 128:(s + 1) * 128], wv[:, fc, :],
                                start=(fc == 0), stop=(fc == 11), signal=(fc == 11))
                    self.op("dve", "tensor_tensor", [ps, x_sb], [x_sb], x_sb[:, s, n * 512:(n + 1) * 512], ps[:, :],
                            x_sb[:, s, n * 512:(n + 1) * 512], ALU.add)
        self.norm_T("g2", l, self.h2T)
        self.S.label = "C_mlp"
        h2T, hid = self.h2T, self.hid
        for fg in range(8):
            w, wv = self.load_w("w_mlp1", l, fg * 512, 512)
            for j in range(4):
                c = fg * 4 + j
                ps = self.ps[4 + c % 2]
                for kc in range(8):
                    self.op("pe", "matmul", [w, h2T], [ps], ps[:, 0:TT], wv[:, kc, j * 128:(j + 1) * 128], h2T[:, kc, :],
                            start=(kc == 0), stop=(kc == 7), signal=(kc == 7))
                r = self.rtmp[c % 2]
                self.op("act", "activation", [ps], [r], r[:, 0:TT], ps[:, 0:TT], AF.Relu)
                self.op("pool", "tensor_tensor", [r], [hid], hid[:, c, :], r[:, 0:TT], r[:, 0:TT], ALU.mult)
        for n in range(2):
            for g in range(4):
                w, wv = self.load_w("w_mlp2", l, n * 512, 512, r0=g * 1024, nrows=1024)
                for s in range(NS):
                    ps = self.ps[s % 4]
                    for j in range(8):
                        c = g * 8 + j
                        self.op("pe", "matmul", [w, hid], [ps], ps[:, :], hid[:, c, s * 128:(s + 1) * 128], wv[:, j, :],
                                start=(c == 0), stop=(c == 31), signal=(j == 7 and (s == NS - 1 or c == 31)))
            for s in range(NS):
                ps = self.ps[s % 4]
                self.op("dve", "tensor_tensor", [ps, x_sb], [x_sb], x_sb[:, s, n * 512:(n + 1) * 512], ps[:, :],
                        x_sb[:, s, n * 512:(n + 1) * 512], ALU.add)
        self.dma("pool", ydst[t0:t0 + TT, :].rearrange("(s p) d -> p s d", p=128), x_sb[:, :, :], [x_sb], [self.b_y])

    def build(self):
        L, TT = self.L, self.TT
        self.prologue()
        outs = []
        for (iname, oname, n, S) in self.groups:
            for i in range(n):
                xin = self.xin[iname][i]
                y = self.yout[oname][i]
                self.b_y = Buf("y")
                outs.append(self.b_y)
                for t0 in range(0, S, TT):
                    self.dma("sp", self.x_sb[:, :, :], xin[t0:t0 + TT, :].rearrange("(s p) d -> p s d", p=128),
                             [], [self.x_sb])
                    self.norm_T("g1", 0, self.hT)
                    self.phase_a(0, t0)
                for l in range(L):
                    self.S.barrier()
                    self.mixers_phase(l, S)
                    self.S.barrier()
                    for t0 in range(0, S, TT):
                        self.phase_c(l, xin if l == 0 else y, y, t0)
                        if l + 1 < L:
                            self.norm_T("g1", l + 1, self.hT)
                            self.phase_a(l + 1, t0)
        self.S.finish(outs + [self.b_o, self.b_dq, self.b_dk, self.b_dv, self.b_gq, self.b_gk, self.b_gv, self.b_gg,
                              self.b_gb, self.b_sx, self.b_sz, self.b_dt])
        self.S.replay()
        return self.nc

    def mixers_phase(self, l, S):
        import os
        if os.environ.get("K_SKIP_MIX"):
            return
        if "diff" in self.mixers:
            self.diff_mixer(l, S)
        if "gla" in self.mixers:
            self.gla_mixer(l, S)
        if "ssd" in self.mixers:
            self.ssd_mixer(l, S)

    def diff_mixer(self, l, S):
        self.S.label = "diff"
        QB = min(self.QB, S)
        NQ = QB // 128
        NK = S // 128
        self.op("pool", "memset", [], [self.dva[0]], self.dva[0][:, :, 128:129], 1.0)
        for h in range(4):
            qT, kT, va = self.dqT[0], self.dkT[0], self.dva[0]
            self.dma("sp", qT[:, 0:S], self.s_dq[h][:, 0:S], [self.b_dq], [qT])
            self.dma("sp", kT[:, 0:S], self.s_dk[h][:, 0:S], [self.b_dk], [kT])
            self.dma("sp", va[:, 0:NK, 0:128],
                     self.s_dv[0:S, h * 128:(h + 1) * 128].rearrange("(kt p) c -> p kt c", p=128), [self.b_dv], [va])
            for q0 in range(0, S, QB):
                for qs in range(NQ):
                    self.op("dve", "memset", [], [self.ps[qs]], self.ps[qs][:, :], 0.0)
                for kt in range(NK):
                    for c in range(2):
                        sc = self.ps[4 + (kt * 2 + c) % 2]
                        self.op("pe", "matmul", [kT, qT], [sc], sc[:, 0:QB], kT[c * 64:(c + 1) * 64, kt * 128:(kt + 1) * 128],
                                qT[c * 64:(c + 1) * 64, q0:q0 + QB], start=True, stop=True)
                        pt = self.pt[self.pti % 3]
                        self.pti += 1
                        self.op("act", "activation", [sc], [pt], pt[:, 0:QB], sc[:, 0:QB], AF.Exp)
                        for qs in range(NQ):
                            acc = self.ps[qs]
                            self.op("pe", "matmul", [pt, va], [acc], acc[:, c * 256:c * 256 + 129],
                                    pt[:, qs * 128:(qs + 1) * 128], va[:, kt, 0:129],
                                    start=False, stop=(kt == NK - 1), signal=(qs == NQ - 1), skip_group_check=True)
                ost = self.dost[self.dosti % 2]
                self.dosti += 1
                for qs in range(NQ):
                    acc = self.ps[qs]
                    fin = self.dfin
                    t0_, t1_ = self.dft
                    self.op("dve", "reciprocal", [acc], [fin], fin[:, 0:1], acc[:, 128:129])
                    self.op("dve", "reciprocal", [acc], [fin], fin[:, 1:2], acc[:, 384:385])
                    self.op("dve", "tensor_scalar", [fin, self.lam], [fin], fin[:, 2:3], fin[:, 1:2],
                            self.lam[:, l:l + 1], -1.0, op0=ALU.mult, op1=ALU.mult)
                    self.op("dve", "tensor_scalar", [acc, fin], [t0_], t0_[:], acc[:, 256:384], fin[:, 2:3], None,
                            op0=ALU.mult)
                    self.op("dve", "scalar_tensor_tensor", [acc, fin, t0_], [t1_], t1_[:], acc[:, 0:128], fin[:, 0:1],
                            t0_[:], op0=ALU.mult, op1=ALU.add)
                    self.op("act", "activation", [t1_], [self.djunk, fin], self.djunk[:], t1_[:], AF.Square,
                            accum_out=fin[:, 3:4])
                    self.op("dve", "tensor_scalar", [fin], [fin], fin[:, 4:5], fin[:, 3:4], 1.0 / 128, EPS,
                            op0=ALU.mult, op1=ALU.add)
                    self.op("act", "activation", [fin], [fin], fin[:, 6:7], fin[:, 4:5], AF.Ln)
                    self.op("act", "activation", [fin], [fin], fin[:, 5:6], fin[:, 6:7], AF.Exp, scale=-0.5)
                    self.op("dve", "scalar_tensor_tensor", [t1_, fin, self.sublnS], [ost], ost[:, qs, :], t1_[:],
                            fin[:, 5:6], self.sublnS[:, l * 128:(l + 1) * 128], op0=ALU.mult, op1=ALU.mult)
                self.dma("pool", self.s_o[q0:q0 + QB, 512 + h * 128:512 + (h + 1) * 128].rearrange("(s p) c -> p s c", p=128),
                         ost[:, 0:NQ, :], [ost], [self.b_o])

    def gla_mixer(self, l, S):
        self.S.label = "gla"
        GT = min(self.GT, S)
        NCH = GT // CH
        glan = self.sm("glan")
        import os
        ndir = int(os.environ.get("K_GLA_DIRS", "2"))
        nochunk = os.environ.get("K_GLA_NOCHUNK")
        for m in range(2):
            for d in range(ndir):
                self.op("dve", "memset", [], [self.g_Sf], self.g_Sf[:, :], 0.0)
                self.op("pool", "memset", [], [self.g_Sb], self.g_Sb[:, :], 0.0)
                tiles = list(range(0, S, GT))
                if d == 1:
                    tiles = tiles[::-1]
                for t0 in tiles:
                    q, k, sp, cs, eb, enb = self.g_q, self.g_k, self.g_sp, self.g_cs, self.g_eb, self.g_enb
                    qt, kt, kh, ktok, v = self.g_qt, self.g_kt, self.g_kh, self.g_ktok, self.g_v
                    self.dma("sp", q[:, 0:GT], self.s_gq[m][:, t0:t0 + GT], [self.b_gq], [q])
                    self.dma("sp", k[:, 0:GT], self.s_gk[m][:, t0:t0 + GT], [self.b_gk], [k])
                    self.dma("sp", sp[:, 0:GT], self.s_gb[d, m][:, t0:t0 + GT], [self.b_gb], [sp])
                    self.dma("sp", v[:, 0:NCH, :],
                             self.s_gv[t0:t0 + GT, m * 256:(m + 1) * 256].rearrange("(c p) f -> p c f", p=CH),
                             [self.b_gv], [v])
                    self.op("dve", "tensor_tensor_scan", [self.g_mask, sp], [cs], cs[:, 0:GT], self.g_mask[:, 0:GT],
                            sp[:, 0:GT], 0.0, ALU.mult, ALU.add)
                    cs3 = cs[:, 0:GT].rearrange("p (c t) -> p c t", t=CH)
                    if d == 1:
                        self.op("dve", "tensor_tensor", [sp, cs], [sp], sp[:, 0:GT], sp[:, 0:GT], cs[:, 0:GT],
                                ALU.subtract)
                        self.op("dve", "tensor_tensor", [sp, cs], [cs], cs3,
                                sp[:, 0:GT].rearrange("p (c t) -> p c t", t=CH),
                                cs3[:, :, CH - 1:CH].to_broadcast([128, NCH, CH]), ALU.add)
                    self.op("act", "activation", [cs], [eb], eb[:, 0:GT], cs[:, 0:GT], AF.Exp, scale=-1.0 / 16)
                    self.op("act", "activation", [cs], [enb], enb[:, 0:GT], cs[:, 0:GT], AF.Exp, scale=1.0 / 16)
                    self.op("dve", "tensor_tensor", [q, eb], [qt], qt[:, 0:GT], q[:, 0:GT], eb[:, 0:GT], ALU.mult)
                    self.op("pool", "tensor_tensor", [k, enb], [kt], kt[:, 0:GT], k[:, 0:GT], enb[:, 0:GT], ALU.mult)
                    eb3 = eb[:, 0:GT].rearrange("p (c t) -> p c t", t=CH)
                    edge = CH - 1 if d == 0 else 0
                    self.op("dve", "tensor_tensor", [kt, eb], [kh], kh[:, 0:GT].rearrange("p (c t) -> p c t", t=CH),
                            kt[:, 0:GT].rearrange("p (c t) -> p c t", t=CH),
                            eb3[:, :, edge:edge + 1].to_broadcast([128, NCH, CH]), ALU.mult)
                    pb = self.pstt[0]
                    for c in range(NCH):
                        self.op("pe", "transpose", [kh, self.ident], [pb], pb[0:64, c * 128:(c + 1) * 128],
                                kh[:, c * CH:(c + 1) * CH], self.ident[:], signal=(c == NCH - 1))
                    self.op("dve", "tensor_copy", [pb], [ktok], ktok[:, 0:NCH, :],
                            pb[0:64, 0:NCH * 128].rearrange("p (c f) -> p c f", f=128))
                    if d == 1:
                        self.dma("sp", self.g_of[:, 0:NCH, :],
                                 self.s_gof[t0:t0 + GT, m * 256:(m + 1) * 256].rearrange("(c p) f -> p c f", p=CH),
                                 [self.b_gof], [self.g_of])
                        self.dma("sp", self.g_g[:, 0:NCH, :],
                                 self.s_gg[t0:t0 + GT, m * 256:(m + 1) * 256].rearrange("(c p) f -> p c f", p=CH),
                                 [self.b_gg], [self.g_g])
                    chunks = list(range(NCH))
                    if d == 1:
                        chunks = chunks[::-1]
                    if nochunk:
                        chunks = []
                    for c in chunks:
                        cs_ = slice(c * CH, (c + 1) * CH)
                        for hh in range(int(os.environ.get("K_GLA_HH", "2"))):
                            p0 = hh * 64
                            pa = self.ps[4 + hh]
                            pav = pa[0:64, 0:64]
                            self.op("pe", "matmul", [kt, qt], [pa], pav, kt[p0:p0 + 64, cs_], qt[p0:p0 + 64, cs_],
                                    start=True, stop=True)
                            am = self.g_am[hh]
                            self.op("dve", "tensor_tensor", [pa, self.tri], [am], am[:, :], pav,
                                    self.tri[:, d, :], ALU.mult)
                            po = self.ps[hh]
                            vv = v[:, c, hh * 128:(hh + 1) * 128]
                            self.op("pe", "matmul", [am, v], [po], po[0:64, 0:128], am[:, :], vv, start=True, stop=True)
                            pin = self.ps[4 + hh]
                            pinv = pin[0:64, 128:256]
                            self.op("pe", "matmul", [qt, self.g_Sb], [pin], pinv, qt[p0:p0 + 64, cs_],
                                    self.g_Sb[p0:p0 + 64, :], start=True, stop=True)
                            oi = self.g_oi[hh]
                            self.op("act", "copy", [pin], [oi], oi[:, :], pinv)
                            pu = self.ps[2 + hh]
                            self.op("pe", "matmul", [ktok, v], [pu], pu[:, 0:128], ktok[:, c, :], vv, start=True, stop=True)
                            ecol = c * CH + edge
                            self.op("dve", "scalar_tensor_tensor", [self.g_Sf, eb, pu], [self.g_Sf],
                                    self.g_Sf[p0:p0 + 64, :], self.g_Sf[p0:p0 + 64, :], eb[p0:p0 + 64, ecol:ecol + 1],
                                    pu[p0:p0 + 64, 0:128], op0=ALU.mult, op1=ALU.add)
                            if d == 0:
                                self.op("dve", "tensor_tensor", [po, oi], [self.g_of], self.g_of[:, c, hh * 128:(hh + 1) * 128],
                                        po[0:64, 0:128], oi[:, :], ALU.add)
                            else:
                                o = self.g_o[hh]
                                fin = self.g_fin
                                f0 = hh * 4
                                self.op("dve", "tensor_tensor", [po, oi], [o], o[:, :], po[0:64, 0:128], oi[:, :], ALU.add)
                                self.op("pool", "tensor_tensor", [o, self.g_of], [o], o[:, :], o[:, :],
                                        self.g_of[:, c, hh * 128:(hh + 1) * 128], ALU.add)
                                self.op("act", "activation", [o], [self.g_junk, fin], self.g_junk[:, :], o[:, :], AF.Square,
                                        accum_out=fin[:, f0:f0 + 1])
                                self.op("dve", "tensor_scalar", [fin], [fin], fin[:, f0 + 1:f0 + 2], fin[:, f0:f0 + 1],
                                        1.0 / 128, EPS, op0=ALU.mult, op1=ALU.add)
                                self.op("act", "activation", [fin], [fin], fin[:, f0 + 2:f0 + 3], fin[:, f0 + 1:f0 + 2], AF.Ln)
                                self.op("act", "activation", [fin], [fin], fin[:, f0 + 3:f0 + 4], fin[:, f0 + 2:f0 + 3], AF.Exp,
                                        scale=-0.5)
                                self.op("dve", "scalar_tensor_tensor", [o, fin, self.smalls], [o], o[:, :], o[:, :],
                                        fin[:, f0 + 3:f0 + 4], glan[0:64, l * 128:(l + 1) * 128], op0=ALU.mult, op1=ALU.mult)
                                self.op("pool", "tensor_tensor", [o, self.g_g], [self.g_ost],
                                        self.g_ost[:, c, hh * 128:(hh + 1) * 128], o[:, :],
                                        self.g_g[:, c, hh * 128:(hh + 1) * 128], ALU.mult)
                        self.op("act", "copy", [self.g_Sf], [self.g_Sb], self.g_Sb[:, :], self.g_Sf[:, :])
                    if d == 0:
                        self.dma("pool", self.s_gof[t0:t0 + GT, m * 256:(m + 1) * 256].rearrange("(c p) f -> p c f", p=CH),
                                 self.g_of[:, 0:NCH, :], [self.g_of], [self.b_gof])
                    else:
                        self.dma("pool", self.s_o[t0:t0 + GT, m * 256:(m + 1) * 256].rearrange("(c p) f -> p c f", p=CH),
                                 self.g_ost[:, 0:NCH, :], [self.g_ost], [self.b_o])

    def ssd_mixer(self, l, S):
        GT = min(self.GT, S)
        NCH = GT // CH
        cw, cb = self.sm("convw"), self.sm("convb")
        self.S.label = "ssd_conv"
        for t0 in range(0, S, GT):
            lo, hi = max(t0 - 2, 0), min(t0 + GT + 2, S)
            for fc in range(6):
                kp = self.c_par % 2
                self.c_par += 1
                ci, acc, cv, cxt = self.c_in[kp], self.c_acc[kp], self.c_cv[kp], self.c_xt[kp]
                self.op("pool", "memset", [], [ci], ci[:, 0:2], 0.0)
                self.op("pool", "memset", [], [ci], ci[:, GT + 2:GT + 4], 0.0)
                self.dma("sp", ci[:, lo - (t0 - 2):hi - (t0 - 2)], self.s_sx[fc][:, lo:hi], [self.b_sx], [ci])
                wo = (l * 6 + fc) * 5
                self.op("dve", "tensor_scalar", [ci, self.smalls], [acc], acc[:, 0:GT], ci[:, 0:GT], cw[:, wo:wo + 1], None,
                        op0=ALU.mult)
                for k in range(1, 5):
                    self.op("dve", "scalar_tensor_tensor", [ci, self.smalls, acc], [acc], acc[:, 0:GT], ci[:, k:k + GT],
                            cw[:, wo + k:wo + k + 1], acc[:, 0:GT], op0=ALU.mult, op1=ALU.add)
                self.op("act", "activation", [acc, self.smalls], [cv], cv[:, 0:GT], acc[:, 0:GT], AF.Silu,
                        bias=cb[:, l * 6 + fc:l * 6 + fc + 1])
                if fc >= 4:
                    self.dma("pool", self.s_sbc[fc - 4][:, t0:t0 + GT], cv[:, 0:GT], [cv], [self.b_sbc])
                if fc <= 4:
                    pb = self.pstt[kp]
                    for c in range(NCH):
                        self.op("pe", "transpose", [cv, self.ident], [pb], pb[0:64, c * 128:(c + 1) * 128],
                                cv[:, c * CH:(c + 1) * CH], self.ident[:], signal=(c == NCH - 1))
                    self.op("act", "copy", [pb], [cxt], cxt[:, 0:NCH, :],
                            pb[0:64, 0:NCH * 128].rearrange("p (c f) -> p c f", f=128))
                    self.dma("pool", self.s_sxt[t0:t0 + GT, fc * 128:(fc + 1) * 128].rearrange("(c p) f -> p c f", p=CH),
                             cxt[:, 0:NCH, :], [cxt], [self.b_sxt])
        ssdD, ssdn = self.sm("ssdD"), self.sm("ssdn")
        self.S.label = "ssd_scan"
        idf = self.cst_f
        for d in range(2):
            r0 = d * 8
            self.op("dve", "memset", [], [self.s_Sf], self.s_Sf[:, :], 0.0)
            self.op("pool", "memset", [], [self.s_Sb], self.s_Sb[:, :], 0.0)
            tiles = list(range(0, S, GT))
            if d == 1:
                tiles = tiles[::-1]
            edge = CH - 1 if d == 0 else 0
            for t0 in tiles:
                la, dtt, cs, rcs, ac = self.s_la, self.s_dtt, self.s_cs, self.s_rcs, self.s_ac
                self.dma("sp", dtt[:, 0:GT], self.s_dt[0][:, t0:t0 + GT], [self.b_dt], [dtt])
                self.dma("sp", la[:, 0:GT], self.s_dt[1][:, t0:t0 + GT], [self.b_dt], [la])
                self.dma("sp", self.s_xt[:, 0:NCH, :], self.s_sxt[t0:t0 + GT, :].rearrange("(c p) f -> p c f", p=CH),
                         [self.b_sxt], [self.s_xt])
                self.dma("sp", self.s_bt[:, 0:GT], self.s_sbc[0][:, t0:t0 + GT], [self.b_sbc], [self.s_bt])
                self.dma("sp", self.s_ct[:, 0:GT], self.s_sbc[1][:, t0:t0 + GT], [self.b_sbc], [self.s_ct])
                self.op("dve", "tensor_tensor_scan", [self.g_mask, la], [cs], cs[:, 0:GT], self.g_mask[0:16, 0:GT],
                        la[:, 0:GT], 0.0, ALU.mult, ALU.add)
                cs3 = cs[:, 0:GT].rearrange("p (c t) -> p c t", t=CH)
                self.op("dve", "tensor_tensor", [la, cs], [rcs], rcs[:, 0:GT], la[:, 0:GT], cs[:, 0:GT], ALU.subtract)
                self.op("dve", "tensor_tensor", [rcs, cs], [rcs], rcs[:, 0:GT].rearrange("p (c t) -> p c t", t=CH),
                        rcs[:, 0:GT].rearrange("p (c t) -> p c t", t=CH),
                        cs3[:, :, CH - 1:CH].to_broadcast([16, NCH, CH]), ALU.add)
                self.op("dve", "tensor_scalar", [cs, self.hsel], [ac], ac[:, 0:GT], cs[:, 0:GT], self.hsel[:, 0:1], None,
                        op0=ALU.mult)
                self.op("dve", "scalar_tensor_tensor", [rcs, self.hsel, ac], [ac], ac[:, 0:GT], rcs[:, 0:GT],
                        self.hsel[:, 1:2], ac[:, 0:GT], op0=ALU.mult, op1=ALU.add)
                self.dma("pool", self.s_acd[:, t0:t0 + GT], ac[:, 0:GT], [ac], [self.b_acd])
                pq = self.ps[4]
                for c in range(NCH):
                    self.op("pe", "transpose", [ac, idf], [pq], pq[0:64, c * 32:c * 32 + 16], ac[:, c * CH:(c + 1) * CH],
                            idf[0:16, 0, 0:16], signal=False)
                    self.op("pe", "transpose", [dtt, idf], [pq], pq[0:64, c * 32 + 16:c * 32 + 32], dtt[:, c * CH:(c + 1) * CH],
                            idf[0:16, 0, 0:16], signal=(c == NCH - 1))
                self.op("dve", "tensor_copy", [pq], [self.s_tok], self.s_tok[:, 0:NCH, :],
                        pq[0:64, 0:NCH * 32].rearrange("p (c f) -> p c f", f=32))
                if d == 1:
                    self.dma("sp", self.s_yf[:, 0:NCH, :], self.s_yfd[t0:t0 + GT, :].rearrange("(c p) f -> p c f", p=CH),
                             [self.b_yfd], [self.s_yf])
                    self.dma("sp", self.s_z[:, 0:NCH, :], self.s_sz[t0:t0 + GT, :].rearrange("(c p) f -> p c f", p=CH),
                             [self.b_sz], [self.s_z])
                chunks = list(range(NCH))
                if d == 1:
                    chunks = chunks[::-1]
                xt = self.s_xt

                def stage1a(c, k):
                    cs_ = slice(c * CH, (c + 1) * CH)
                    seg, t1, t2, mt, sm_, xw = self.s_seg[k], self.s_t1[k], self.s_t2[k], self.s_mt[k], self.s_sm[k], self.s_xw[k]
                    self.dma("sp", seg[:, :, :], self.s_acd[r0:r0 + 8, t0 + c * CH:t0 + (c + 1) * CH].partition_broadcast(128),
                             [self.b_acd], [seg])
                    actok = self.s_tok[:, c, r0:r0 + 8]
                    self.op("dve", "tensor_tensor", [seg, self.s_tok], [t1], t1[:, :, :], seg[0:64, :, :],
                            actok.unsqueeze(2).to_broadcast([64, 8, CH]), ALU.subtract)
                    self.op("dve", "tensor_tensor", [seg, self.s_tok], [sm_], sm_[0:64, 8:16], seg[0:64, :, edge], actok,
                            ALU.subtract)
                    self.op("pool", "tensor_tensor", [t1, self.mbias], [t2], t2[:, :, :], t1[:, :, :],
                            self.mbias[:, d:d + 1, :].to_broadcast([64, 8, CH]), ALU.add)
                    self.op("act", "activation", [self.s_tok], [sm_], sm_[0:64, 0:8], actok, AF.Exp)
                    self.op("act", "activation", [sm_], [sm_], sm_[0:64, 16:24], sm_[0:64, 8:16], AF.Exp)
                    self.op("act", "activation", [seg], [sm_], sm_[:, 32:40], seg[:, :, edge], AF.Exp)
                    self.op("act", "activation", [t2], [t1], t1[:, :, :], t2[:, :, :], AF.Exp)
                    for g in range(2):
                        pcb = self.ps[4 + g]
                        self.op("pe", "matmul", [self.s_bt, self.s_ct], [pcb], pcb[0:64, 0:64],
                                self.s_bt[g * 64:(g + 1) * 64, cs_], self.s_ct[g * 64:(g + 1) * 64, cs_], start=True, stop=True)

                def stage1b(c, k):
                    cs_ = slice(c * CH, (c + 1) * CH)
                    seg, t1, t2, mt, sm_, xw = self.s_seg[k], self.s_t1[k], self.s_t2[k], self.s_mt[k], self.s_sm[k], self.s_xw[k]
                    actok = self.s_tok[:, c, r0:r0 + 8]
                    dttok = self.s_tok[:, c, 16 + r0:16 + r0 + 8]
                    self.op("dve", "tensor_tensor", [t1, self.s_tok], [t2], t2[:, :, :], t1[:, :, :],
                            dttok.unsqueeze(2).to_broadcast([64, 8, CH]), ALU.mult)
                    for g in range(2):
                        pcb = self.ps[4 + g]
                        self.op("dve", "tensor_tensor", [t2, pcb], [mt], mt[:, g * 4:(g + 1) * 4, :], t2[:, g * 4:(g + 1) * 4, :],
                                pcb[0:64, 0:64].unsqueeze(1).to_broadcast([64, 4, CH]), ALU.mult)
                    self.op("dve", "tensor_tensor", [sm_, self.s_tok], [sm_], sm_[0:64, 24:32], sm_[0:64, 16:24], dttok, ALU.mult)
                    py = self.ps[k]
                    for h in range(8):
                        self.op("pe", "matmul", [mt, xt], [py], py[0:64, h * 64:(h + 1) * 64], mt[:, h, :],
                                xt[:, c, h * 64:(h + 1) * 64], start=True, stop=True, signal=(h == 7))
                    self.op("dve", "tensor_tensor", [xt, sm_], [xw], xw[:, :].rearrange("p (h q) -> p h q", q=64),
                            xt[:, c, 0:512].rearrange("p (h q) -> p h q", q=64),
                            sm_[0:64, 24:32].unsqueeze(2).to_broadcast([64, 8, 64]), ALU.mult)
                    pu = self.ps[2 + k]
                    for g in range(2):
                        self.op("pe", "matmul", [xt, xw], [pu], pu[:, g * 256:(g + 1) * 256], xt[:, c, 512:640],
                                xw[:, g * 256:(g + 1) * 256], start=True, stop=True, signal=(g == 1))

                def stage2(c, k):
                    cs_ = slice(c * CH, (c + 1) * CH)
                    sm_ = self.s_sm[k]
                    py, pu = self.ps[k], self.ps[2 + k]
                    yi = self.s_yi
                    for g in range(2):
                        pi = self.psx if g == 0 else self.psy
                        self.op("pe", "matmul", [self.s_ct, self.s_Sb], [pi], pi[0:64, 0:256],
                                self.s_ct[g * 64:(g + 1) * 64, cs_], self.s_Sb[g * 64:(g + 1) * 64, :], start=True, stop=True)
                    for g in range(2):
                        pi = self.psx if g == 0 else self.psy
                        self.op("dve", "tensor_tensor", [pi, sm_], [yi],
                                yi[:, g * 256:(g + 1) * 256].rearrange("p (h q) -> p h q", q=64),
                                pi[0:64, 0:256].rearrange("p (h q) -> p h q", q=64),
                                sm_[0:64, g * 4:(g + 1) * 4].unsqueeze(2).to_broadcast([64, 4, 64]), ALU.mult)
                    for g in range(2):
                        p0 = g * 64
                        Sg = self.s_Sf[p0:p0 + 64, :].rearrange("p (h q) -> p h q", q=64)
                        self.op("dve", "tensor_tensor", [self.s_Sf, sm_], [self.s_Sf], Sg, Sg,
                                sm_[p0:p0 + 64, 32 + g * 4:32 + g * 4 + 4].unsqueeze(2).to_broadcast([64, 4, 64]), ALU.mult)
                        self.op("dve", "tensor_tensor", [self.s_Sf, pu], [self.s_Sf], self.s_Sf[p0:p0 + 64, :],
                                self.s_Sf[p0:p0 + 64, :], pu[p0:p0 + 64, g * 256:(g + 1) * 256], ALU.add)
                    self.op("act", "copy", [self.s_Sf], [self.s_Sb], self.s_Sb[:, :], self.s_Sf[:, :])
                    if d == 0:
                        self.op("dve", "tensor_tensor", [py, yi], [self.s_yf], self.s_yf[:, c, :], py[0:64, :], yi[:, :], ALU.add)
                    else:
                        y, y2, fin = self.s_y, self.s_y2, self.g_fin
                        self.op("dve", "tensor_tensor", [py, yi], [y], y[:, :], py[0:64, :], yi[:, :], ALU.add)
                        self.op("pool", "tensor_tensor", [y, self.s_yf], [y], y[:, :], y[:, :], self.s_yf[:, c, :], ALU.add)
                        self.op("dve", "tensor_tensor", [xt, self.smalls], [y2], y2[:, :].rearrange("p (h q) -> p h q", q=64),
                                xt[:, c, 0:512].rearrange("p (h q) -> p h q", q=64),
                                ssdD[0:64, l * 8:(l + 1) * 8].unsqueeze(2).to_broadcast([64, 8, 64]), ALU.mult)
                        self.op("pool", "tensor_tensor", [y, y2], [y], y[:, :], y[:, :], y2[:, :], ALU.add)
                        self.op("pool", "tensor_tensor", [y, self.s_z], [y], y[:, :], y[:, :], self.s_z[:, c, :], ALU.mult)
                        self.op("act", "activation", [y], [self.s_junk, fin], self.s_junk[:, :], y[:, :], AF.Square,
                                accum_out=fin[:, 0:1])
                        self.op("dve", "tensor_scalar", [fin], [fin], fin[:, 1:2], fin[:, 0:1], 1.0 / 512, EPS,
                                op0=ALU.mult, op1=ALU.add)
                        self.op("act", "activation", [fin], [fin], fin[:, 2:3], fin[:, 1:2], AF.Ln)
                        self.op("act", "activation", [fin], [fin], fin[:, 3:4], fin[:, 2:3], AF.Exp, scale=-0.5)
                        self.op("dve", "scalar_tensor_tensor", [y, fin, self.smalls], [self.s_ost], self.s_ost[:, c, :], y[:, :],
                                fin[:, 3:4], ssdn[0:64, l * 512:(l + 1) * 512], op0=ALU.mult, op1=ALU.mult)

                par0 = self.s_par
                stage1a(chunks[0], par0 % 2)
                stage1b(chunks[0], par0 % 2)
                for i, c in enumerate(chunks):
                    if i + 1 < len(chunks):
                        stage1a(chunks[i + 1], (par0 + i + 1) % 2)
                    stage2(c, (par0 + i) % 2)
                    if i + 1 < len(chunks):
                        stage1b(chunks[i + 1], (par0 + i + 1) % 2)
                self.s_par = par0 + len(chunks)
                if d == 0:
                    self.dma("pool", self.s_yfd[t0:t0 + GT, :].rearrange("(c p) f -> p c f", p=CH), self.s_yf[:, 0:NCH, :],
                             [self.s_yf], [self.b_yfd])
                else:
                    self.dma("pool", self.s_o[t0:t0 + GT, 1024:1536].rearrange("(c p) f -> p c f", p=CH),
                             self.s_ost[:, 0:NCH, :], [self.s_ost], [self.b_o])


_PROG = {}


def kernel(**inputs):
    inp = {k: np.asarray(v) for k, v in inputs.items()}
    xp, xs = inp["x_prompt"], inp["x_sample"]
    NCORE = 8
    npp, nps = xp.shape[0] // NCORE, xs.shape[0] // NCORE
    SP, SS = xp.shape[1], xs.shape[1]
    L = inp["w_in"].shape[0]
    key = (npp, SP, nps, SS, L)
    if key not in _PROG:
        prog = Prog([("x_p", "y_p", npp, SP), ("x_s", "y_s", nps, SS)], depth=L)
        _PROG[key] = prog.build()
    nc = _PROG[key]
    small, _ = pack_smalls({k: inp[k] for k in SMALL}, L)
    consts = host_consts(max(SP, SS))
    in_maps = []
    for c in range(NCORE):
        m = {"x_p": np.ascontiguousarray(xp[c * npp:(c + 1) * npp]),
             "x_s": np.ascontiguousarray(xs[c * nps:(c + 1) * nps]), "smalls": small}
        for k in WNAMES:
            m[k] = inp[k]
        m.update(consts)
        in_maps.append(m)
    res = run_bass_kernel_spmd(nc, in_maps, core_ids=list(range(NCORE)))
    yp = np.concatenate([r["y_p"] for r in res.results], 0).astype(np.float32)
    ys = np.concatenate([r["y_s"] for r in res.results], 0).astype(np.float32)
    return (yp, ys)
```

```python
import math
from contextlib import ExitStack

import numpy as np
import concourse.bass as bass
import concourse.mybir as mybir
from concourse.bass_utils import run_bass_kernel_spmd

F32 = mybir.dt.float32
BF16 = mybir.dt.bfloat16
AF = mybir.ActivationFunctionType
ALU = mybir.AluOpType
AX = mybir.AxisListType

D = 1024
DEPTH = 4
INW = 4400
MIXW = 1536
DFF = 4096
EPS = 1e-6
O_GQ, O_GK, O_GV, O_GG, O_LR = 0, 256, 512, 1024, 1536
O_DQ, O_DK, O_DV = 1568, 2080, 2592
O_SZ, O_SX, O_SDT = 3104, 3616, 4384
CH = 64


ALL_BUFS = []


class Buf:
    __slots__ = ("name", "w", "rs")

    def __init__(self, name=""):
        self.name = name
        self.w = None
        self.rs = []
        ALL_BUFS.append(self)


class Sched:
    CE = ("pe", "dve", "act", "pool")

    def __init__(self, nc, dma_ring=8):
        self.nc = nc
        self.streams = {e: [] for e in ("pe", "dve", "act", "pool", "sp")}
        self.cnt = {e: 0 for e in self.CE}
        self.waited = {e: {} for e in self.streams}
        self.dma_ring = dma_ring
        self.dma_i = {"sp": 0, "pool": 0, "act": 0}
        self.dma_last = {}
        self.ninst = 0
        self.label = ""
        self.labels = {e: [] for e in self.streams}
        self.annotate = False

    def _need(self, eng, tok):
        key, val = tok
        if self.waited[eng].get(key, 0) >= val:
            return None
        self.waited[eng][key] = val
        return tok

    def _track(self, tok, reads, writes):
        for b in writes:
            b.w = tok
            b.rs = []
        for b in reads:
            if b not in writes:
                b.rs.append(tok)
                if len(b.rs) > 24:
                    m = {}
                    for k, v in b.rs:
                        m[k] = max(m.get(k, 0), v)
                    b.rs = list(m.items())

    def _deps(self, reads, writes, extra):
        deps = list(extra)
        for b in reads:
            if b.w is not None:
                deps.append(b.w)
        for b in writes:
            if b.w is not None:
                deps.append(b.w)
            deps.extend(b.rs)
        return deps

    def emit(self, eng, fn, reads=(), writes=(), signal=True, extra=()):
        wm = {}
        for t in self._deps(reads, writes, extra):
            if t[0] == eng and (eng == "pe" or t[1] > self.cnt[eng]):
                continue
            t = self._need(eng, t)
            if t is not None:
                wm[t[0]] = max(wm.get(t[0], 0), t[1])
        if signal:
            self.cnt[eng] += 1
            tok = (eng, self.cnt[eng])
        else:
            tok = (eng, self.cnt[eng] + 1)
        self.streams[eng].append((tuple(wm.items()), fn, tok if signal else None))
        if self.annotate:
            self.labels[eng].append(self.label)
        self.ninst += 1
        self._track(tok, reads, writes)
        return tok

    def dma(self, q, fn, reads=(), writes=(), extra=()):
        i = self.dma_i[q]
        self.dma_i[q] += 1
        key = f"dma_{q}_{i % self.dma_ring}"
        val = 16 * (i // self.dma_ring + 1)
        deps = self._deps(reads, writes, extra)
        if i >= self.dma_ring:
            deps.append((key, val - 16))
        wm = {}
        for t in deps:
            t = self._need(q, t)
            if t is not None:
                wm[t[0]] = max(wm.get(t[0], 0), t[1])
        tok = (key, val)
        self.dma_last[key] = val
        self.streams[q].append((tuple(wm.items()), fn, tok))
        if self.annotate:
            self.labels[q].append(self.label)
        self.ninst += 1
        self._track(tok, reads, writes)
        return tok

    def barrier(self):
        toks = [(e, self.cnt[e]) for e in self.CE if self.cnt[e] > 0] + list(self.dma_last.items())
        for eng in self.streams:
            wm = {}
            for t in toks:
                if t[0] == eng:
                    continue
                t = self._need(eng, t)
                if t is not None:
                    wm[t[0]] = max(wm.get(t[0], 0), t[1])
            if wm:
                self.labels[eng].append("barrier")
                if eng == "sp":
                    self.streams[eng].append((tuple(wm.items()), lambda e: e.nop(), None))
                else:
                    self.cnt[eng] += 1
                    self.streams[eng].append((tuple(wm.items()), lambda e: e.nop(), (eng, self.cnt[eng])))
        for b in ALL_BUFS:
            b.w = None
            b.rs = []

    def finish(self, bufs):
        wm = {}
        for b in bufs:
            for t in ([b.w] if b.w is not None else []) + list(b.rs):
                t = self._need("sp", t)
                if t is not None:
                    wm[t[0]] = max(wm.get(t[0], 0), t[1])
        self.labels["sp"].append("finish")
        self.streams["sp"].append((tuple(wm.items()), lambda e: e.nop(), None))

    def replay(self):
        nc = self.nc
        keys = set()
        for st in self.streams.values():
            for waits, fn, tok in st:
                for k, v in waits:
                    keys.add(k)
                if tok is not None:
                    keys.add(tok[0])
        with ExitStack() as es:
            sems = {k: es.enter_context(nc.semaphore(k)) for k in sorted(keys)}
            block = es.enter_context(nc.Block())

            def run(engobj, st, lab=None):
                for i, (waits, fn, tok) in enumerate(st):
                    for k, v in waits:
                        engobj.wait_ge(sems[k], v)
                    ins = fn(engobj)
                    if lab is not None and i < len(lab):
                        ins.annotate(lab[i])
                    if tok is not None:
                        ins.then_inc(sems[tok[0]], 16 if tok[0].startswith("dma_") else 1)

            @block.sync
            def _(e):
                run(e, self.streams["sp"], self.labels["sp"] if self.annotate else None)

            @block.tensor
            def _(e):
                run(e, self.streams["pe"], self.labels["pe"] if self.annotate else None)

            @block.vector
            def _(e):
                run(e, self.streams["dve"], self.labels["dve"] if self.annotate else None)

            @block.scalar
            def _(e):
                run(e, self.streams["act"], self.labels["act"] if self.annotate else None)

            @block.gpsimd
            def _(e):
                run(e, self.streams["pool"], self.labels["pool"] if self.annotate else None)


class T:
    def __init__(self, nc, name, shape, dtype, psum=False):
        self.h = (nc.alloc_psum_tensor if psum else nc.alloc_sbuf_tensor)(name, list(shape), dtype)
        self.b = Buf(name)

    def __getitem__(self, k):
        return self.h[k]


class V(T):
    def __init__(self, ap, b):
        self.h = ap
        self.b = b


class Arena:
    def __init__(self, nc, name, nbytes):
        self.nc = nc
        self.t = nc.alloc_sbuf_tensor(name, [128, nbytes // 2], BF16)
        self.base = nc.lookup_mloc(self.t).addr
        self.size = nbytes
        self.off = 0
        self.peak = 0

    def reset(self):
        self.off = 0

    def alloc(self, name, shape, dtype):
        esz = 4 if dtype == F32 else 2
        n = int(np.prod(shape[1:]))
        nb = (n * esz + 63) // 64 * 64
        assert self.off + nb <= self.size, (name, self.off, nb, self.size)
        h = self.nc.alloc_sbuf_tensor_at(name, list(shape), dtype, offset=self.base + self.off)
        self.off += nb
        self.peak = max(self.peak, self.off)
        return V(h, Buf(name))


def lambda_init(l):
    return 0.8 - 0.6 * math.exp(-0.3 * l)


WNAMES = ["w_in", "w_out", "w_mlp1", "w_mlp2"]
WSHAPES = {"w_in": (D, INW), "w_out": (MIXW, D), "w_mlp1": (D, DFF), "w_mlp2": (DFF, D)}
SMALL = {"norm1": (D,), "gla_wg_f": (16, 256), "gla_bg_f": (256,), "gla_wg_b": (16, 256), "gla_bg_b": (256,),
         "gla_norm": (128,), "diff_qnorm": (64,), "diff_knorm": (64,), "diff_lq1": (64,), "diff_lk1": (64,),
         "diff_lq2": (64,), "diff_lk2": (64,), "diff_subln": (128,), "ssd_conv_w": (5, 768), "ssd_conv_b": (768,),
         "ssd_dt_bias_f": (8,), "ssd_dt_bias_b": (8,), "ssd_A_log_f": (8,), "ssd_A_log_b": (8,), "ssd_D": (8,),
         "ssd_norm": (512,), "norm2": (D,)}


def host_consts(smax):
    ident = np.eye(128, dtype=np.float32)
    bones = np.zeros((128, 128), np.float32)
    bones[:64, :64] = 1.0 / 64
    bones[64:, 64:] = 1.0 / 64
    rmat = np.zeros((128, 128), np.float32)
    for blk in (0, 64):
        for d in range(8):
            rmat[blk + d + 8, blk + d] = -1.0
            rmat[blk + d, blk + d + 8] = 1.0
    inv = (500000.0 ** (-np.arange(0, 16, 2, dtype=np.float32) / np.float32(16))).astype(np.float32)
    ang = (np.arange(smax, dtype=np.float32)[:, None] * inv[None, :]).astype(np.float32)
    cos = np.ones((128, smax), np.float32)
    sin = np.zeros((128, smax), np.float32)
    for blk in (0, 64):
        for d in range(16):
            cos[blk + d] = np.cos(ang[:, d % 8])
            sin[blk + d] = np.sin(ang[:, d % 8])
    tri = np.zeros((64, 2, 64), np.float32)
    jj, ii = np.meshgrid(np.arange(64), np.arange(64), indexing="ij")
    tri[:, 0, :] = (jj <= ii)
    tri[:, 1, :] = (jj >= ii)
    hsel = np.zeros((16, 2), np.float32)
    hsel[:8, 0] = 1.0
    hsel[8:, 1] = 1.0
    return {"c_ident": ident, "c_bones": bones, "c_rmat": rmat, "c_cos": cos, "c_sin": sin, "c_tri": tri,
            "c_hsel": hsel}


def pack_smalls(p, L):
    cols = {}
    parts = []
    off = 0

    def add(name, arr):
        nonlocal off
        arr = np.ascontiguousarray(arr, dtype=np.float32)
        a = np.zeros((128, int(np.prod(arr.shape[1:]))), np.float32)
        a[:arr.shape[0]] = arr.reshape(arr.shape[0], -1)
        cols[name] = (off, a.shape[1])
        parts.append(a)
        off += a.shape[1]

    add("g1", p["norm1"].reshape(L, 8, 128).transpose(2, 0, 1))
    add("g2", p["norm2"].reshape(L, 8, 128).transpose(2, 0, 1))
    add("wg", np.stack([p["gla_wg_f"], p["gla_wg_b"]], 1).transpose(2, 0, 1, 3))
    add("bg", np.stack([p["gla_bg_f"], p["gla_bg_b"]], 1).reshape(L, 2, 2, 128).transpose(3, 0, 1, 2))
    add("qn", np.tile(p["diff_qnorm"], (1, 2)).T)
    add("kn", np.tile(p["diff_knorm"], (1, 2)).T)
    lql = np.stack([p["diff_lq1"], p["diff_lk1"], p["diff_lq2"], p["diff_lk2"]], 1)
    add("lql", np.broadcast_to(lql[None], (128, L, 4, 64)))
    add("dtb", np.concatenate([p["ssd_dt_bias_f"], p["ssd_dt_bias_b"]], 1).T)
    add("alog", np.concatenate([p["ssd_A_log_f"], p["ssd_A_log_b"]], 1).T)
    add("convw", p["ssd_conv_w"].reshape(L, 5, 6, 128).transpose(3, 0, 2, 1))
    add("convb", p["ssd_conv_b"].reshape(L, 6, 128).transpose(2, 0, 1))
    add("glan", np.broadcast_to(p["gla_norm"][None], (128, L, 128)))
    add("subln", np.broadcast_to(p["diff_subln"][None], (128, L, 128)))
    add("ssdn", np.broadcast_to(p["ssd_norm"][None], (128, L, 512)))
    add("ssdD", np.broadcast_to(p["ssd_D"][None], (128, L, 8)))
    return np.concatenate(parts, 1), cols


def smalls_cols(L):
    dummy = {k: np.zeros((L,) + v, np.float32) for k, v in SMALL.items()}
    return pack_smalls(dummy, L)[1]


class Prog:
    def __init__(self, seq_groups, depth=DEPTH, TT=512, dbg=False, mixers=("gla", "diff", "ssd")):
        self.L = depth
        self.TT = TT
        self.dbg = dbg
        self.mixers = mixers
        self.groups = seq_groups
        self.smax = max(g[3] for g in seq_groups)
        nc = self.nc = bass.Bass("TRN2", target_bir_lowering=False)
        self.S = Sched(nc)
        import os
        self.S.annotate = bool(os.environ.get("K_ANNOTATE"))
        L = depth
        skind = "ExternalOutput" if dbg else "Internal"
        self.xin, self.yout = {}, {}
        for (iname, oname, n, S) in seq_groups:
            self.xin[iname] = nc.dram_tensor(iname, [n, S, D], F32, kind="ExternalInput").ap()
            self.yout[oname] = nc.dram_tensor(oname, [n, S, D], F32, kind="ExternalOutput").ap()
        self.wf = {k: nc.dram_tensor(k, [L] + list(WSHAPES[k]), F32, kind="ExternalInput").ap() for k in WNAMES}
        self.wb = {k: nc.dram_tensor("b_" + k, [L] + list(WSHAPES[k]), BF16, kind="Internal").ap() for k in WNAMES}
        self.wb_buf = {k: [Buf() for _ in range(L)] for k in WNAMES}
        self.scols = smalls_cols(L)
        nsm = sum(v[1] for v in self.scols.values())
        self.smalls_d = nc.dram_tensor("smalls", [128, nsm], F32, kind="ExternalInput").ap()
        self.cd = {k: nc.dram_tensor(k, [128, 128], F32, kind="ExternalInput").ap()
                   for k in ("c_ident", "c_bones", "c_rmat")}
        self.cd["c_tri"] = nc.dram_tensor("c_tri", [64, 2, 64], F32, kind="ExternalInput").ap()
        self.cd["c_hsel"] = nc.dram_tensor("c_hsel", [16, 2], F32, kind="ExternalInput").ap()
        self.cd["c_cos"] = nc.dram_tensor("c_cos", [128, self.smax], F32, kind="ExternalInput").ap()
        self.cd["c_sin"] = nc.dram_tensor("c_sin", [128, self.smax], F32, kind="ExternalInput").ap()
        sm = self.smax
        def scr(name, shape, dt):
            return nc.dram_tensor(name, list(shape), dt, kind=skind).ap(), Buf(name)
        self.s_dq, self.b_dq = scr("s_dq", [4, 128, sm], BF16)
        self.s_dk, self.b_dk = scr("s_dk", [4, 128, sm], BF16)
        self.s_dv, self.b_dv = scr("s_dv", [sm, 512], BF16)
        self.s_gq, self.b_gq = scr("s_gq", [2, 128, sm], BF16)
        self.s_gk, self.b_gk = scr("s_gk", [2, 128, sm], BF16)
        self.s_gv, self.b_gv = scr("s_gv", [sm, 512], BF16)
        self.s_gg, self.b_gg = scr("s_gg", [sm, 512], BF16)
        self.s_gb, self.b_gb = scr("s_gb", [2, 2, 128, sm], F32)
        self.s_sx, self.b_sx = scr("s_sx", [6, 128, sm], BF16)
        self.s_sz, self.b_sz = scr("s_sz", [sm, 512], BF16)
        self.s_dt, self.b_dt = scr("s_dt", [2, 16, sm], F32)
        self.s_o, self.b_o = scr("s_o", [sm, MIXW], BF16)
        self.s_gof, self.b_gof = scr("s_gof", [sm, 512], F32)
        self.s_sxt, self.b_sxt = scr("s_sxt", [sm, 640], BF16)
        self.s_sbc, self.b_sbc = scr("s_sbc", [2, 128, sm], BF16)
        self.s_acd, self.b_acd = scr("s_acd", [16, sm], F32)
        self.s_yfd, self.b_yfd = scr("s_yfd", [sm, 512], F32)
        self._alloc()

    def op(self, eng, name, reads, writes, *args, signal=True, **kw):
        return self.S.emit(eng, lambda e: getattr(e, name)(*args, **kw),
                           reads=[t.b if isinstance(t, T) else t for t in reads],
                           writes=[t.b if isinstance(t, T) else t for t in writes], signal=signal)

    def dma(self, q, out, in_, reads, writes, **kw):
        return self.S.dma(q, lambda e: e.dma_start(out=out, in_=in_, **kw),
                          reads=[t.b if isinstance(t, T) else t for t in reads],
                          writes=[t.b if isinstance(t, T) else t for t in writes])

    def sm(self, name):
        o, n = self.scols[name]
        return self.smalls[:, o:o + n]

    def _alloc(self):
        nc, L, TT = self.nc, self.L, self.TT
        NS = TT // 128
        self.NS = NS
        t = lambda name, shape, dt: T(nc, name, shape, dt)
        nsm = sum(v[1] for v in self.scols.values())
        self.smalls = t("smalls_sb", [128, nsm], F32)
        self.ident = t("ident", [128, 128], BF16)
        self.bones = t("bones", [128, 128], BF16)
        self.rmat = t("rmat", [128, 128], BF16)
        self.cst_f = t("cst_f", [128, 3, 128], F32)
        self.wgb = t("wgb", [16, L * 2 * 256], BF16)
        self.nbg = t("nbg", [128, L * 4], F32)
        self.qg = t("qg", [128, L], F32)
        self.aneg = t("aneg", [16, L], F32)
        self.lam = t("lam", [128, L], F32)
        self.lamtmp = t("lamtmp", [128, 4 * 64], F32)
        self.lam2 = t("lam2", [128, 4], F32)
        ar = self.arena = Arena(nc, "arena", 141 * 1024)
        ta = lambda name, shape, dt: ar.alloc(name, shape, dt)
        self.x_sb = ta("x_sb", [128, NS, D], F32)
        self.xn = ta("xn", [128, NS, D], BF16)
        self.junk = ta("junk", [128, D], BF16)
        self.ss = ta("ss", [128, NS], F32)
        self.rstd = ta("rstd", [128, NS], F32)
        self.hT = ta("hT", [128, 8, TT], BF16)
        self.h2T = ta("h2T", [128, 8, TT], BF16)
        self.o_sb = ta("o_sb", [128, NS, MIXW], BF16)
        self.oT = ta("oT", [128, 12, TT], BF16)
        self.hid = ta("hid", [128, 32, TT], BF16)
        self.rtmp = [ta(f"rtmp{i}", [128, TT], F32) for i in range(2)]
        self.wbuf = [t(f"wbuf{i}", [128, 6144], BF16) for i in range(2)]
        self.wi = 0
        self.stg = [ta(f"stg{i}", [128, TT], BF16) for i in range(4)]
        self.stgi = 0
        self.stgf = [ta(f"stgf{i}", [128, TT], F32) for i in range(2)]
        self.stgfi = 0
        self.stgt = [ta(f"stgt{i}", [128, NS, 512], BF16) for i in range(2)]
        self.stgti = 0
        self.fa = [ta(f"fa{i}", [128, TT], F32) for i in range(4)]
        self.sqb = ta("sqb", [128, TT], BF16)
        self.qnb = ta("qnb", [128, TT], BF16)
        self.cos_sb = ta("cos_sb", [128, TT], F32)
        self.sin_sb = ta("sin_sb", [128, TT], F32)
        self.lr_sb = [ta(f"lr_sb{i}", [16, TT], BF16) for i in range(2)]
        self.dt_sb = [ta(f"dt_sb{i}", [16, TT], F32) for i in range(3)]
        self.ps = [T(nc, f"ps{i}", [128, 512], F32, psum=True) for i in range(6)]
        self.pstt = [T(nc, f"pst{i}", [128, 1024], BF16, psum=True) for i in range(2)]
        self.psx = V(self.pstt[0][:, :].bitcast(F32), self.pstt[0].b)
        self.psy = V(self.pstt[1][:, :].bitcast(F32), self.pstt[1].b)
        sm = self.smax
        self.dense_peak = ar.off
        ar.reset()
        self.QB = min(512, sm)
        self.dqT = [ta("dqT0", [128, sm], BF16)]
        self.dkT = [ta("dkT0", [128, sm], BF16)]
        self.dva = [ta(f"dva{i}", [128, sm // 128, 132], BF16) for i in range(1)]
        self.pt = [ta(f"pt{i}", [128, self.QB], BF16) for i in range(3)]
        self.pti = 0
        self.dfin = ta("dfin", [128, 8], F32)
        self.dft = [ta(f"dft{i}", [128, 128], F32) for i in range(2)]
        self.djunk = ta("djunk", [128, 128], BF16)
        self.dost = [ta(f"dost{i}", [128, self.QB // 128, 128], BF16) for i in range(2)]
        self.dosti = 0
        self.sublnS = t("sublnS", [128, L * 128], F32)
        GT = self.GT = min(256, sm)
        NCH = GT // CH
        self.g_q = ta("g_q", [128, GT], BF16)
        self.g_k = ta("g_k", [128, GT], BF16)
        self.g_sp = ta("g_sp", [128, GT], F32)
        self.g_cs = ta("g_cs", [128, GT], F32)
        self.g_eb = ta("g_eb", [128, GT], F32)
        self.g_enb = ta("g_enb", [128, GT], F32)
        self.g_qt = ta("g_qt", [128, GT], BF16)
        self.g_kt = ta("g_kt", [128, GT], BF16)
        self.g_kh = ta("g_kh", [128, GT], BF16)
        self.g_ktok = ta("g_ktok", [64, NCH, 128], BF16)
        self.g_v = ta("g_v", [64, NCH, 256], BF16)
        self.g_g = ta("g_g", [64, NCH, 256], BF16)
        self.g_of = ta("g_of", [64, NCH, 256], F32)
        self.g_ost = ta("g_ost", [64, NCH, 256], BF16)
        self.g_am = [ta(f"g_am{i}", [64, 64], BF16) for i in range(2)]
        self.g_Sf = ta("g_Sf", [128, 128], F32)
        self.g_Sb = ta("g_Sb", [128, 128], BF16)
        self.g_o = [ta(f"g_o{i}", [64, 128], F32) for i in range(2)]
        self.g_oi = [ta(f"g_oi{i}", [64, 128], F32) for i in range(2)]
        self.g_fin = ta("g_fin", [64, 8], F32)
        self.g_junk = ta("g_junk", [64, 128], BF16)
        self.g_mask = t("g_mask", [128, GT], F32)
        self.tri = t("tri", [64, 2, 64], F32)
        self.c_in = [ta(f"c_in{i}", [128, GT + 4], BF16) for i in range(2)]
        self.c_acc = [ta(f"c_acc{i}", [128, GT], F32) for i in range(2)]
        self.c_cv = [ta(f"c_cv{i}", [128, GT], BF16) for i in range(2)]
        self.c_xt = [ta(f"c_xt{i}", [64, NCH, 128], BF16) for i in range(2)]
        self.c_par = 0
        self.s_la = ta("s_la", [16, GT], F32)
        self.s_dtt = ta("s_dtt", [16, GT], F32)
        self.s_cs = ta("s_cs", [16, GT], F32)
        self.s_rcs = ta("s_rcs", [16, GT], F32)
        self.s_ac = ta("s_ac", [16, GT], F32)
        self.s_seg = [ta(f"s_seg{i}", [128, 8, CH], F32) for i in range(2)]
        self.s_t1 = [ta(f"s_t1{i}", [64, 8, CH], F32) for i in range(2)]
        self.s_t2 = [ta(f"s_t2{i}", [64, 8, CH], F32) for i in range(2)]
        self.s_mt = [ta(f"s_mt{i}", [64, 8, CH], BF16) for i in range(2)]
        self.s_par = 0
        self.s_tok = ta("s_tok", [64, NCH, 32], F32)
        self.s_sm = [ta(f"s_sm{i}", [128, 64], F32) for i in range(2)]
        self.s_xt = ta("s_xt", [64, NCH, 640], BF16)
        self.s_xw = [ta(f"s_xw{i}", [64, 512], BF16) for i in range(2)]
        self.s_bt = ta("s_bt", [128, GT], BF16)
        self.s_ct = ta("s_ct", [128, GT], BF16)
        self.s_Sf = ta("s_Sf", [128, 256], F32)
        self.s_Sb = ta("s_Sb", [128, 256], BF16)
        self.s_yi = ta("s_yi", [64, 512], F32)
        self.s_yf = ta("s_yf", [64, NCH, 512], F32)
        self.s_z = ta("s_z", [64, NCH, 512], BF16)
        self.s_y = ta("s_y", [64, 512], F32)
        self.s_y2 = ta("s_y2", [64, 512], F32)
        self.s_ost = ta("s_ost", [64, NCH, 512], BF16)
        self.s_junk = ta("s_junk", [64, 512], BF16)
        self.mbias = t("mbias", [64, 2, CH], F32)
        self.hsel = t("hsel", [16, 2], F32)
        self.cvt_in = [V(self.wbuf[i][:, 0:2200].bitcast(F32), self.wbuf[i].b) for i in range(2)]
        self.cvt_out = [V(self.wbuf[i][:, 2200:3300], self.wbuf[i].b) for i in range(2)]

    def next_w(self):
        w = self.wbuf[self.wi % len(self.wbuf)]
        self.wi += 1
        return w

    def next_stg(self):
        w = self.stg[self.stgi % len(self.stg)]
        self.stgi += 1
        return w

    def prologue(self):
        L = self.L
        self.S.label = "prologue"
        self.dma("sp", self.smalls[:], self.smalls_d, [], [self.smalls])
        for i, k in enumerate(("c_ident", "c_bones", "c_rmat")):
            self.dma("sp", self.cst_f[:, i, :], self.cd[k], [], [self.cst_f])
        for i, tt in enumerate((self.ident, self.bones, self.rmat)):
            self.op("dve", "tensor_copy", [self.cst_f], [tt], tt[:], self.cst_f[:, i, :])
        self.op("dve", "tensor_copy", [self.smalls], [self.wgb], self.wgb[:], self.sm("wg")[0:16, :])
        self.op("dve", "tensor_scalar", [self.smalls], [self.nbg], self.nbg[:], self.sm("bg"), -1.0, None,
                op0=ALU.mult)
        self.op("dve", "tensor_scalar", [self.smalls], [self.qg], self.qg[:], self.sm("qn"), 0.125, None,
                op0=ALU.mult)
        self.op("act", "activation", [self.smalls], [self.aneg], self.aneg[:], self.sm("alog")[0:16, :], AF.Exp)
        self.op("dve", "tensor_scalar", [self.aneg], [self.aneg], self.aneg[:], self.aneg[:], -1.0, None,
                op0=ALU.mult)
        lql = self.sm("lql")
        for l in range(L):
            for j in range(2):
                a = lql[:, (l * 4 + 2 * j) * 64:(l * 4 + 2 * j + 1) * 64]
                b = lql[:, (l * 4 + 2 * j + 1) * 64:(l * 4 + 2 * j + 2) * 64]
                self.op("dve", "tensor_tensor", [self.smalls], [self.lamtmp], self.lamtmp[:, j * 64:(j + 1) * 64],
                        a, b, ALU.mult)
                self.op("dve", "tensor_reduce", [self.lamtmp], [self.lam2], self.lam2[:, j:j + 1],
                        self.lamtmp[:, j * 64:(j + 1) * 64], AX.X, ALU.add)
            self.op("act", "activation", [self.lam2], [self.lam2], self.lam2[:, 2:4], self.lam2[:, 0:2], AF.Exp)
            self.op("dve", "tensor_tensor", [self.lam2], [self.lam2], self.lam2[:, 0:1], self.lam2[:, 2:3],
                    self.lam2[:, 3:4], ALU.subtract)
            self.op("dve", "tensor_scalar", [self.lam2], [self.lam], self.lam[:, l:l + 1], self.lam2[:, 0:1],
                    float(lambda_init(l)), None, op0=ALU.add)
        for l in range(L):
            self.op("dve", "tensor_scalar", [self.smalls], [self.sublnS], self.sublnS[:, l * 128:(l + 1) * 128],
                    self.sm("subln")[:, l * 128:(l + 1) * 128], float(1.0 - lambda_init(l)), None, op0=ALU.mult)
        if self.mixers and len(self.mixers) < 3:
            self.op("dve", "memset", [], [self.o_sb], self.o_sb[:, :, :], 0.0)
            for t0 in range(0, self.smax, self.TT):
                self.dma("sp", self.s_o[t0:t0 + self.TT, :].rearrange("(s p) d -> p s d", p=128), self.o_sb[:, :, :],
                         [self.o_sb], [self.b_o])
        self.op("pool", "memset", [], [self.g_mask], self.g_mask[:, :], 1.0)
        self.op("pool", "memset", [self.g_mask], [self.g_mask],
                self.g_mask[:, :].rearrange("p (c t) -> p c t", t=CH)[:, :, 0:1], 0.0)
        self.dma("sp", self.tri[:, :, :], self.cd["c_tri"], [], [self.tri])
        self.op("dve", "tensor_scalar", [self.tri], [self.mbias], self.mbias[:, :, :], self.tri[:, :, :], 30000.0, -30000.0,
                op0=ALU.mult, op1=ALU.add)
        self.dma("sp", self.hsel[:, :], self.cd["c_hsel"], [], [self.hsel])
        i = 0
        for l in range(L):
            for k in WNAMES:
                r, c = WSHAPES[k]
                per = r * c // 128
                src = self.wf[k][l].rearrange("(p a) c -> p (a c)", p=128)
                dst = self.wb[k][l].rearrange("(p a) c -> p (a c)", p=128)
                cw = 1100 if k == "w_in" else 1024
                for j in range(per // cw):
                    ci, co = self.cvt_in[i % 2], self.cvt_out[i % 2]
                    self.dma("sp", ci[:, 0:cw], src[:, j * cw:(j + 1) * cw], [], [ci])
                    eng = ("dve", "act", "pool")[i % 3]
                    if eng == "act":
                        self.op("act", "copy", [ci], [co], co[:, 0:cw], ci[:, 0:cw])
                    else:
                        self.op(eng, "tensor_copy", [ci], [co], co[:, 0:cw], ci[:, 0:cw])
                    self.dma("sp", dst[:, j * cw:(j + 1) * cw], co[:, 0:cw], [co], [self.wb_buf[k][l]])
                    i += 1

    def norm_T(self, gname, l, dst):
        NS, TT = self.NS, self.TT
        self.S.label = "norm_" + gname
        for s in range(NS):
            self.op("act", "activation", [self.x_sb], [self.junk, self.ss], self.junk[:], self.x_sb[:, s, :],
                    AF.Square, accum_out=self.ss[:, s:s + 1])
        self.op("dve", "tensor_scalar", [self.ss], [self.rstd], self.rstd[:], self.ss[:], 1.0 / D, EPS,
                op0=ALU.mult, op1=ALU.add)
        self.op("act", "activation", [self.rstd], [self.rstd], self.rstd[:], self.rstd[:], AF.Ln)
        self.op("act", "activation", [self.rstd], [self.rstd], self.rstd[:], self.rstd[:], AF.Exp, scale=-0.5)
        for s in range(NS):
            if s % 2 == 0:
                self.op("dve", "tensor_scalar", [self.x_sb, self.rstd], [self.xn], self.xn[:, s, :],
                        self.x_sb[:, s, :], self.rstd[:, s:s + 1], None, op0=ALU.mult)
            else:
                self.op("act", "activation", [self.x_sb, self.rstd], [self.xn], self.xn[:, s, :],
                        self.x_sb[:, s, :], AF.Copy, scale=self.rstd[:, s:s + 1])
        g = self.sm(gname)
        for kc in range(8):
            pb = self.pstt[kc % 2]
            for s in range(NS):
                self.op("pe", "transpose", [self.xn, self.ident], [pb],
                        pb[:, s * 128: (s + 1) * 128], self.xn[:, s, kc * 128:(kc + 1) * 128],
                        self.ident[:], signal=(s == NS - 1))
            gk = g[:, l * 8 + kc: l * 8 + kc + 1]
            if kc % 2 == 0:
                self.op("dve", "tensor_scalar", [pb, self.smalls], [dst], dst[:, kc, :],
                        pb[:, 0:TT], gk, None, op0=ALU.mult)
            else:
                self.op("act", "activation", [pb, self.smalls], [dst], dst[:, kc, :],
                        pb[:, 0:TT], AF.Copy, scale=gk)

    def load_w(self, k, l, c0, ncols, r0=0, nrows=None):
        rows = WSHAPES[k][0] if nrows is None else nrows
        nch = rows // 128
        w = self.next_w()
        src = self.wb[k][l][r0:r0 + rows, c0:c0 + ncols].rearrange("(kc p) n -> p kc n", p=128)
        view = w[:, 0:nch * ncols].rearrange("p (kc n) -> p kc n", kc=nch)
        self.dma("sp", view, src, [self.wb_buf[k][l]], [w])
        return w, view

    def phase_a(self, l, t0):
        TT, NS = self.TT, self.NS
        self.S.label = "A"
        hT = self.hT
        fmi = [0]

        def fm_acc(w, wv, j, m=128):
            ps = self.ps[4 + fmi[0] % 2]
            fmi[0] += 1
            for kc in range(8):
                self.op("pe", "matmul", [w, hT], [ps], ps[0:m, 0:TT], wv[:, kc, j * 128:j * 128 + m], hT[:, kc, :],
                        start=(kc == 0), stop=(kc == 7), signal=(kc == 7))
            return ps

        evi = [0]

        def evac_copy(ps, dst_t, dst_ap, src_ap, scale=None):
            evi[0] += 1
            if evi[0] % 2 == 0:
                self.op("dve", "tensor_scalar", [ps], [dst_t], dst_ap, src_ap, 1.0 if scale is None else scale, None,
                        op0=ALU.mult)
            else:
                self.op("act", "activation", [ps], [dst_t], dst_ap, src_ap, AF.Copy,
                        scale=1.0 if scale is None else scale)

        if "gla" in self.mixers:
            w, wv = self.load_w("w_in", l, O_GQ, 512)
            for j in range(4):
                ps = fm_acc(w, wv, j)
                st = self.next_stg()
                evac_copy(ps, st, st[:, 0:TT], ps[:, 0:TT], scale=(0.125 if j < 2 else None))
                dst = (self.s_gq if j < 2 else self.s_gk)[j % 2][:, t0:t0 + TT]
                self.dma("pool", dst, st[:, 0:TT], [st], [Buf()])
        w = self.next_w()
        wv = w[:, 0:8 * 48].rearrange("p (kc n) -> p kc n", kc=8)
        self.dma("sp", wv[:, :, 0:32], self.wb["w_in"][l][:, O_LR:O_LR + 32].rearrange("(kc p) n -> p kc n", p=128),
                 [self.wb_buf["w_in"][l]], [w])
        self.dma("sp", wv[:, :, 32:48], self.wb["w_in"][l][:, O_SDT:O_SDT + 16].rearrange("(kc p) n -> p kc n", p=128),
                 [self.wb_buf["w_in"][l]], [w])
        if "gla" in self.mixers:
            for d in range(2):
                ps = self.ps[4 + fmi[0] % 2]
                fmi[0] += 1
                for kc in range(8):
                    self.op("pe", "matmul", [w, hT], [ps], ps[0:16, 0:TT], wv[:, kc, d * 16:(d + 1) * 16], hT[:, kc, :],
                            start=(kc == 0), stop=(kc == 7), signal=(kc == 7))
                self.op("dve", "tensor_copy", [ps], [self.lr_sb[d]], self.lr_sb[d][:, 0:TT], ps[0:16, 0:TT])
            for d in range(2):
                for m in range(2):
                    ps = self.ps[4 + fmi[0] % 2]
                    fmi[0] += 1
                    wgo = ((l * 2 + d) * 256) + m * 128
                    self.op("pe", "matmul", [self.wgb, self.lr_sb[d]], [ps], ps[:, 0:TT], self.wgb[:, wgo:wgo + 128],
                            self.lr_sb[d][:, 0:TT], start=True, stop=True)
                    f = self.stgf[self.stgfi % 2]
                    self.stgfi += 1
                    bi = (l * 2 + d) * 2 + m
                    self.op("act", "activation", [ps, self.nbg], [f], f[:, 0:TT], ps[:, 0:TT], AF.Exp,
                            bias=self.nbg[:, bi:bi + 1], scale=-1.0)
                    self.op("act", "activation", [f], [f], f[:, 0:TT], f[:, 0:TT], AF.Ln, bias=1.0)
                    self.dma("pool", self.s_gb[d, m][:, t0:t0 + TT], f[:, 0:TT], [f], [Buf()])
        if "ssd" in self.mixers:
            ps = self.ps[4 + fmi[0] % 2]
            fmi[0] += 1
            for kc in range(8):
                self.op("pe", "matmul", [w, hT], [ps], ps[0:16, 0:TT], wv[:, kc, 32:48], hT[:, kc, :],
                        start=(kc == 0), stop=(kc == 7), signal=(kc == 7))
            d0, d1, d2 = self.dt_sb
            dtb = self.sm("dtb")
            self.op("act", "activation", [ps, self.smalls], [d0], d0[:, 0:TT], ps[0:16, 0:TT], AF.Exp,
                    bias=dtb[0:16, l:l + 1], scale=1.0)
            self.op("act", "activation", [d0], [d1], d1[:, 0:TT], d0[:, 0:TT], AF.Ln, bias=1.0)
            self.op("dve", "tensor_scalar", [d1, self.aneg], [d2], d2[:, 0:TT], d1[:, 0:TT], self.aneg[:, l:l + 1],
                    None, op0=ALU.mult)
            self.dma("pool", self.s_dt[0][:, t0:t0 + TT], d1[:, 0:TT], [d1], [Buf()])
            self.dma("pool", self.s_dt[1][:, t0:t0 + TT], d2[:, 0:TT], [d2], [Buf()])
            for (c0, nj, jb) in ((O_SX, 4, 0), (O_SX + 512, 2, 4)):
                w2, wv2 = self.load_w("w_in", l, c0, nj * 128)
                for j in range(nj):
                    ps = fm_acc(w2, wv2, j)
                    st = self.next_stg()
                    evac_copy(ps, st, st[:, 0:TT], ps[:, 0:TT])
                    self.dma("pool", self.s_sx[jb + j][:, t0:t0 + TT], st[:, 0:TT], [st], [Buf()])
        if "diff" in self.mixers:
            self.dma("sp", self.cos_sb[:, 0:TT], self.cd["c_cos"][:, t0:t0 + TT], [], [self.cos_sb])
            self.dma("sp", self.sin_sb[:, 0:TT], self.cd["c_sin"][:, t0:t0 + TT], [], [self.sin_sb])
            for qk in range(2):
                w2, wv2 = self.load_w("w_in", l, O_DQ if qk == 0 else O_DK, 512)
                gain = self.qg[:, l:l + 1] if qk == 0 else self.sm("kn")[:, l:l + 1]
                gain_t = self.qg if qk == 0 else self.smalls
                for j in range(4):
                    ps = fm_acc(w2, wv2, j)
                    p6 = self.ps[0]
                    f0, f1, f2, f3 = self.fa
                    self.op("act", "activation", [ps], [self.sqb], self.sqb[:, 0:TT], ps[:, 0:TT], AF.Square)
                    self.op("pe", "matmul", [self.bones, self.sqb], [p6], p6[:, 0:TT], self.bones[:], self.sqb[:, 0:TT],
                            start=True, stop=True)
                    self.op("dve", "tensor_scalar", [p6], [f0], f0[:, 0:TT], p6[:, 0:TT], EPS, None,
                            op0=ALU.add)
                    self.op("act", "activation", [f0], [f0], f0[:, 0:TT], f0[:, 0:TT], AF.Ln)
                    self.op("act", "activation", [f0], [f0], f0[:, 0:TT], f0[:, 0:TT], AF.Exp, scale=-0.5)
                    self.op("dve", "scalar_tensor_tensor", [ps, gain_t, f0], [f1], f1[:, 0:TT], ps[:, 0:TT], gain,
                            f0[:, 0:TT], op0=ALU.mult, op1=ALU.mult)
                    self.op("act", "copy", [f1], [self.qnb], self.qnb[:, 0:TT], f1[:, 0:TT])
                    self.op("pe", "matmul", [self.rmat, self.qnb], [p6], p6[:, 0:TT], self.rmat[:], self.qnb[:, 0:TT],
                            start=True, stop=True)
                    self.op("pool", "tensor_tensor", [f1, self.cos_sb], [f2], f2[:, 0:TT], f1[:, 0:TT],
                            self.cos_sb[:, 0:TT], ALU.mult)
                    self.op("dve", "tensor_tensor", [p6, self.sin_sb], [f3], f3[:, 0:TT], p6[:, 0:TT],
                            self.sin_sb[:, 0:TT], ALU.mult)
                    st = self.next_stg()
                    self.op("pool", "tensor_tensor", [f2, f3], [st], st[:, 0:TT], f2[:, 0:TT], f3[:, 0:TT], ALU.add)
                    self.dma("pool", (self.s_dq if qk == 0 else self.s_dk)[j][:, t0:t0 + TT], st[:, 0:TT], [st],
                             [Buf()])
        tmg = []
        if "gla" in self.mixers:
            tmg += [(O_GV, self.s_gv, self.b_gv, False), (O_GG, self.s_gg, self.b_gg, True)]
        if "diff" in self.mixers:
            tmg += [(O_DV, self.s_dv, self.b_dv, False)]
        if "ssd" in self.mixers:
            tmg += [(O_SZ, self.s_sz, self.b_sz, True)]
        for (c0, dst, dbuf, silu) in tmg:
            w2, wv2 = self.load_w("w_in", l, c0, 512)
            sg = self.stgt[self.stgti % 2]
            self.stgti += 1
            for s in range(NS):
                ps = self.ps[s % 4]
                for kc in range(8):
                    self.op("pe", "matmul", [w2, hT], [ps], ps[:, :], hT[:, kc, s * 128:(s + 1) * 128], wv2[:, kc, :],
                            start=(kc == 0), stop=(kc == 7), signal=(kc == 7))
                if silu:
                    self.op("act", "activation", [ps], [sg], sg[:, s, :], ps[:, :], AF.Silu)
                else:
                    evac_copy(ps, sg, sg[:, s, :], ps[:, :])
            self.dma("pool", dst[t0:t0 + TT, :].rearrange("(s p) c -> p s c", p=128), sg[:, :, :], [sg], [Buf()])

    def phase_c(self, l, xsrc, ydst, t0):
        TT, NS = self.TT, self.NS
        self.S.label = "C"
        x_sb = self.x_sb
        self.dma("sp", x_sb[:, :, :], xsrc[t0:t0 + TT, :].rearrange("(s p) d -> p s d", p=128), [self.b_y], [x_sb])
        if self.mixers:
            o_sb, oT = self.o_sb, self.oT
            self.dma("sp", o_sb[:, :, :], self.s_o[t0:t0 + TT, :].rearrange("(s p) d -> p s d", p=128),
                     [self.b_o], [o_sb])
            for fc in range(12):
                pb = self.pstt[fc % 2]
                for s in range(NS):
                    self.op("pe", "transpose", [o_sb, self.ident], [pb],
                            pb[:, s * 128: (s + 1) * 128], o_sb[:, s, fc * 128:(fc + 1) * 128],
                            self.ident[:], signal=(s == NS - 1))
                if fc % 2 == 0:
                    self.op("dve", "tensor_copy", [pb], [oT], oT[:, fc, :], pb[:, 0:TT])
                else:
                    self.op("act", "copy", [pb], [oT], oT[:, fc, :], pb[:, 0:TT])
            for n in range(2):
                w, wv = self.load_w("w_out", l, n * 512, 512)
                for s in range(NS):
                    ps = self.ps[s % 4]
                    for fc in range(12):
                        self.op("pe", "matmul", [w, oT], [ps], ps[:, :], oT[:, fc, s * 128:(s + 1) * 128], wv[:, fc, :],
                                start=(fc == 0), stop=(fc == 11), signal=(fc == 11))
                    self.op("dve", "tensor_tensor", [ps, x_sb], [x_sb], x_sb[:, s, n * 512:(n + 1) * 512], ps[:, :],
                            x_sb[:, s, n * 512:(n + 1) * 512], ALU.add)
        self.norm_T("g2", l, self.h2T)
        self.S.label = "C_mlp"
        h2T, hid = self.h2T, self.hid
        for fg in range(8):
            w, wv = self.load_w("w_mlp1", l, fg * 512, 512)
            for j in range(4):
                c = fg * 4 + j
                ps = self.ps[4 + c % 2]
                for kc in range(8):
                    self.op("pe", "matmul", [w, h2T], [ps], ps[:, 0:TT], wv[:, kc, j * 128:(j + 1) * 128], h2T[:, kc, :],
                            start=(kc == 0), stop=(kc == 7), signal=(kc == 7))
                r = self.rtmp[c % 2]
                self.op("act", "activation", [ps], [r], r[:, 0:TT], ps[:, 0:TT], AF.Relu)
                self.op("pool", "tensor_tensor", [r], [hid], hid[:, c, :], r[:, 0:TT], r[:, 0:TT], ALU.mult)
        for n in range(2):
            for g in range(4):
                w, wv = self.load_w("w_mlp2", l, n * 512, 512, r0=g * 1024, nrows=1024)
                for s in range(NS):
                    ps = self.ps[s % 4]
                    for j in range(8):
                        c = g * 8 + j
                        self.op("pe", "matmul", [w, hid], [ps], ps[:, :], hid[:, c, s * 128:(s + 1) * 128], wv[:, j, :],
                                start=(c == 0), stop=(c == 31), signal=(j == 7 and (s == NS - 1 or c == 31)))
            for s in range(NS):
                ps = self.ps[s % 4]
                self.op("dve", "tensor_tensor", [ps, x_sb], [x_sb], x_sb[:, s, n * 512:(n + 1) * 512], ps[:, :],
                        x_sb[:, s, n * 512:(n + 1) * 512], ALU.add)
        self.dma("pool", ydst[t0:t0 + TT, :].rearrange("(s p) d -> p s d", p=128), x_sb[:, :, :], [x_sb], [Buf()])

    def build(self):
        L, TT = self.L, self.TT
        self.prologue()
        outs = []
        for (iname, oname, n, S) in self.groups:
            for i in range(n):
                xin = self.xin[iname][i]
                y = self.yout[oname][i]
                self.b_y = Buf("y")
                outs.append(self.b_y)
                for t0 in range(0, S, TT):
                    self.dma("sp", self.x_sb[:, :, :], xin[t0:t0 + TT, :].rearrange("(s p) d -> p s d", p=128),
                             [], [self.x_sb])
                    self.norm_T("g1", 0, self.hT)
                    self.phase_a(0, t0)
                for l in range(L):
                    self.S.barrier()
                    self.mixers_phase(l, S)
                    self.S.barrier()
                    for t0 in range(0, S, TT):
                        self.phase_c(l, xin if l == 0 else y, y, t0)
                        if l + 1 < L:
                            self.norm_T("g1", l + 1, self.hT)
                            self.phase_a(l + 1, t0)
        self.S.barrier()
        self.S.replay()
        return self.nc

    def mixers_phase(self, l, S):
        import os
        if os.environ.get("K_SKIP_MIX"):
            return
        if "diff" in self.mixers:
            self.diff_mixer(l, S)
        if "gla" in self.mixers:
            self.gla_mixer(l, S)
        if "ssd" in self.mixers:
            self.ssd_mixer(l, S)

    def diff_mixer(self, l, S):
        self.S.label = "diff"
        QB = min(self.QB, S)
        NQ = QB // 128
        NK = S // 128
        self.op("pool", "memset", [], [self.dva[0]], self.dva[0][:, :, 128:129], 1.0)
        for h in range(4):
            qT, kT, va = self.dqT[0], self.dkT[0], self.dva[0]
            self.dma("sp", qT[:, 0:S], self.s_dq[h][:, 0:S], [self.b_dq], [qT])
            self.dma("sp", kT[:, 0:S], self.s_dk[h][:, 0:S], [self.b_dk], [kT])
            self.dma("sp", va[:, 0:NK, 0:128],
                     self.s_dv[0:S, h * 128:(h + 1) * 128].rearrange("(kt p) c -> p kt c", p=128), [self.b_dv], [va])
            for q0 in range(0, S, QB):
                for qs in range(NQ):
                    self.op("dve", "memset", [], [self.ps[qs]], self.ps[qs][:, :], 0.0)
                units = [(kt, c) for kt in range(NK) for c in range(2)]

                def score(u):
                    kt, c = units[u]
                    sc = self.ps[4 + u % 2]
                    self.op("pe", "matmul", [kT, qT], [sc], sc[:, 0:QB], kT[c * 64:(c + 1) * 64, kt * 128:(kt + 1) * 128],
                            qT[c * 64:(c + 1) * 64, q0:q0 + QB], start=True, stop=True)

                def exp_pv(u):
                    kt, c = units[u]
                    sc = self.ps[4 + u % 2]
                    pt = self.pt[self.pti % 3]
                    self.pti += 1
                    self.op("act", "activation", [sc], [pt], pt[:, 0:QB], sc[:, 0:QB], AF.Exp)
                    return pt

                def pv(u, pt):
                    kt, c = units[u]
                    for qs in range(NQ):
                        acc = self.ps[qs]
                        self.op("pe", "matmul", [pt, va], [acc], acc[:, c * 256:c * 256 + 129],
                                pt[:, qs * 128:(qs + 1) * 128], va[:, kt, 0:129],
                                start=False, stop=(kt == NK - 1), signal=(qs == NQ - 1), skip_group_check=True)

                score(0)
                for u in range(len(units)):
                    pt = exp_pv(u)
                    if u + 1 < len(units):
                        score(u + 1)
                    pv(u, pt)
                ost = self.dost[self.dosti % 2]
                self.dosti += 1
                for qs in range(NQ):
                    acc = self.ps[qs]
                    fin = self.dfin
                    t0_, t1_ = self.dft
                    self.op("dve", "reciprocal", [acc], [fin], fin[:, 0:1], acc[:, 128:129])
                    self.op("dve", "reciprocal", [acc], [fin], fin[:, 1:2], acc[:, 384:385])
                    self.op("dve", "tensor_scalar", [fin, self.lam], [fin], fin[:, 2:3], fin[:, 1:2],
                            self.lam[:, l:l + 1], -1.0, op0=ALU.mult, op1=ALU.mult)
                    self.op("dve", "tensor_scalar", [acc, fin], [t0_], t0_[:], acc[:, 256:384], fin[:, 2:3], None,
                            op0=ALU.mult)
                    self.op("dve", "scalar_tensor_tensor", [acc, fin, t0_], [t1_], t1_[:], acc[:, 0:128], fin[:, 0:1],
                            t0_[:], op0=ALU.mult, op1=ALU.add)
                    self.op("act", "activation", [t1_], [self.djunk, fin], self.djunk[:], t1_[:], AF.Square,
                            accum_out=fin[:, 3:4])
                    self.op("dve", "tensor_scalar", [fin], [fin], fin[:, 4:5], fin[:, 3:4], 1.0 / 128, EPS,
                            op0=ALU.mult, op1=ALU.add)
                    self.op("act", "activation", [fin], [fin], fin[:, 6:7], fin[:, 4:5], AF.Ln)
                    self.op("act", "activation", [fin], [fin], fin[:, 5:6], fin[:, 6:7], AF.Exp, scale=-0.5)
                    self.op("dve", "scalar_tensor_tensor", [t1_, fin, self.sublnS], [ost], ost[:, qs, :], t1_[:],
                            fin[:, 5:6], self.sublnS[:, l * 128:(l + 1) * 128], op0=ALU.mult, op1=ALU.mult)
                self.dma("pool", self.s_o[q0:q0 + QB, 512 + h * 128:512 + (h + 1) * 128].rearrange("(s p) c -> p s c", p=128),
                         ost[:, 0:NQ, :], [ost], [Buf()])

    def gla_mixer(self, l, S):
        self.S.label = "gla"
        GT = min(self.GT, S)
        NCH = GT // CH
        glan = self.sm("glan")
        import os
        ndir = int(os.environ.get("K_GLA_DIRS", "2"))
        nochunk = os.environ.get("K_GLA_NOCHUNK")
        for m in range(2):
            for d in range(ndir):
                self.op("dve", "memset", [], [self.g_Sf], self.g_Sf[:, :], 0.0)
                self.op("pool", "memset", [], [self.g_Sb], self.g_Sb[:, :], 0.0)
                tiles = list(range(0, S, GT))
                if d == 1:
                    tiles = tiles[::-1]
                for t0 in tiles:
                    q, k, sp, cs, eb, enb = self.g_q, self.g_k, self.g_sp, self.g_cs, self.g_eb, self.g_enb
                    qt, kt, kh, ktok, v = self.g_qt, self.g_kt, self.g_kh, self.g_ktok, self.g_v
                    self.dma("sp", q[:, 0:GT], self.s_gq[m][:, t0:t0 + GT], [self.b_gq], [q])
                    self.dma("sp", k[:, 0:GT], self.s_gk[m][:, t0:t0 + GT], [self.b_gk], [k])
                    self.dma("sp", sp[:, 0:GT], self.s_gb[d, m][:, t0:t0 + GT], [self.b_gb], [sp])
                    self.dma("sp", v[:, 0:NCH, :],
                             self.s_gv[t0:t0 + GT, m * 256:(m + 1) * 256].rearrange("(c p) f -> p c f", p=CH),
                             [self.b_gv], [v])
                    self.op("dve", "tensor_tensor_scan", [self.g_mask, sp], [cs], cs[:, 0:GT], self.g_mask[:, 0:GT],
                            sp[:, 0:GT], 0.0, ALU.mult, ALU.add)
                    cs3 = cs[:, 0:GT].rearrange("p (c t) -> p c t", t=CH)
                    if d == 1:
                        self.op("dve", "tensor_tensor", [sp, cs], [sp], sp[:, 0:GT], sp[:, 0:GT], cs[:, 0:GT],
                                ALU.subtract)
                        self.op("dve", "tensor_tensor", [sp, cs], [cs], cs3,
                                sp[:, 0:GT].rearrange("p (c t) -> p c t", t=CH),
                                cs3[:, :, CH - 1:CH].to_broadcast([128, NCH, CH]), ALU.add)
                    self.op("act", "activation", [cs], [eb], eb[:, 0:GT], cs[:, 0:GT], AF.Exp, scale=-1.0 / 16)
                    self.op("act", "activation", [cs], [enb], enb[:, 0:GT], cs[:, 0:GT], AF.Exp, scale=1.0 / 16)
                    self.op("dve", "tensor_tensor", [q, eb], [qt], qt[:, 0:GT], q[:, 0:GT], eb[:, 0:GT], ALU.mult)
                    self.op("pool", "tensor_tensor", [k, enb], [kt], kt[:, 0:GT], k[:, 0:GT], enb[:, 0:GT], ALU.mult)
                    eb3 = eb[:, 0:GT].rearrange("p (c t) -> p c t", t=CH)
                    edge = CH - 1 if d == 0 else 0
                    self.op("dve", "tensor_tensor", [kt, eb], [kh], kh[:, 0:GT].rearrange("p (c t) -> p c t", t=CH),
                            kt[:, 0:GT].rearrange("p (c t) -> p c t", t=CH),
                            eb3[:, :, edge:edge + 1].to_broadcast([128, NCH, CH]), ALU.mult)
                    pb = self.pstt[0]
                    for c in range(NCH):
                        self.op("pe", "transpose", [kh, self.ident], [pb], pb[0:64, c * 128:(c + 1) * 128],
                                kh[:, c * CH:(c + 1) * CH], self.ident[:], signal=(c == NCH - 1))
                    self.op("dve", "tensor_copy", [pb], [ktok], ktok[:, 0:NCH, :],
                            pb[0:64, 0:NCH * 128].rearrange("p (c f) -> p c f", f=128))
                    if d == 1:
                        self.dma("sp", self.g_of[:, 0:NCH, :],
                                 self.s_gof[t0:t0 + GT, m * 256:(m + 1) * 256].rearrange("(c p) f -> p c f", p=CH),
                                 [self.b_gof], [self.g_of])
                        self.dma("sp", self.g_g[:, 0:NCH, :],
                                 self.s_gg[t0:t0 + GT, m * 256:(m + 1) * 256].rearrange("(c p) f -> p c f", p=CH),
                                 [self.b_gg], [self.g_g])
                    chunks = list(range(NCH))
                    if d == 1:
                        chunks = chunks[::-1]
                    if nochunk:
                        chunks = []
                    for c in chunks:
                        cs_ = slice(c * CH, (c + 1) * CH)
                        ecol = c * CH + edge
                        for hh in range(2):
                            p0 = hh * 64
                            pa = self.ps[4 + hh]
                            self.op("pe", "matmul", [kt, qt], [pa], pa[0:64, 0:64], kt[p0:p0 + 64, cs_], qt[p0:p0 + 64, cs_],
                                    start=True, stop=True)
                        for hh in range(2):
                            pu = self.ps[2 + hh]
                            self.op("pe", "matmul", [ktok, v], [pu], pu[:, 0:128], ktok[:, c, :],
                                    v[:, c, hh * 128:(hh + 1) * 128], start=True, stop=True)
                        for hh in range(2):
                            pa = self.ps[4 + hh]
                            am = self.g_am[hh]
                            self.op("dve", "tensor_tensor", [pa, self.tri], [am], am[:, :], pa[0:64, 0:64],
                                    self.tri[:, d, :], ALU.mult)
                        for hh in range(2):
                            p0 = hh * 64
                            po = self.ps[hh]
                            self.op("pe", "matmul", [self.g_am[hh], v], [po], po[0:64, 0:128], self.g_am[hh][:, :],
                                    v[:, c, hh * 128:(hh + 1) * 128], start=True, stop=True)
                            pin = self.ps[4 + hh]
                            self.op("pe", "matmul", [qt, self.g_Sb], [pin], pin[0:64, 128:256], qt[p0:p0 + 64, cs_],
                                    self.g_Sb[p0:p0 + 64, :], start=True, stop=True)
                        for hh in range(2):
                            pin = self.ps[4 + hh]
                            oi = self.g_oi[hh]
                            self.op("act", "copy", [pin], [oi], oi[:, :], pin[0:64, 128:256])
                        for hh in range(2):
                            p0 = hh * 64
                            pu = self.ps[2 + hh]
                            self.op("dve", "scalar_tensor_tensor", [self.g_Sf, eb, pu], [self.g_Sf],
                                    self.g_Sf[p0:p0 + 64, :], self.g_Sf[p0:p0 + 64, :], eb[p0:p0 + 64, ecol:ecol + 1],
                                    pu[p0:p0 + 64, 0:128], op0=ALU.mult, op1=ALU.add)
                        self.op("act", "copy", [self.g_Sf], [self.g_Sb], self.g_Sb[:, :], self.g_Sf[:, :])
                        for hh in range(2):
                            po = self.ps[hh]
                            oi = self.g_oi[hh]
                            if d == 0:
                                self.op("dve", "tensor_tensor", [po, oi], [self.g_of], self.g_of[:, c, hh * 128:(hh + 1) * 128],
                                        po[0:64, 0:128], oi[:, :], ALU.add)
                            else:
                                o = self.g_o[hh]
                                self.op("dve", "tensor_tensor", [po, oi], [o], o[:, :], po[0:64, 0:128], oi[:, :], ALU.add)
                                self.op("pool", "tensor_tensor", [o, self.g_of], [o], o[:, :], o[:, :],
                                        self.g_of[:, c, hh * 128:(hh + 1) * 128], ALU.add)
                        if d == 1:
                            fin = self.g_fin
                            for hh in range(2):
                                o = self.g_o[hh]
                                f0 = hh * 4
                                self.op("act", "activation", [o], [self.g_junk, fin], self.g_junk[:, :], o[:, :], AF.Square,
                                        accum_out=fin[:, f0:f0 + 1])
                            for hh in range(2):
                                f0 = hh * 4
                                self.op("dve", "tensor_scalar", [fin], [fin], fin[:, f0 + 1:f0 + 2], fin[:, f0:f0 + 1],
                                        1.0 / 128, EPS, op0=ALU.mult, op1=ALU.add)
                            for hh in range(2):
                                f0 = hh * 4
                                self.op("act", "activation", [fin], [fin], fin[:, f0 + 2:f0 + 3], fin[:, f0 + 1:f0 + 2], AF.Ln)
                            for hh in range(2):
                                f0 = hh * 4
                                self.op("act", "activation", [fin], [fin], fin[:, f0 + 3:f0 + 4], fin[:, f0 + 2:f0 + 3], AF.Exp,
                                        scale=-0.5)
                            for hh in range(2):
                                o = self.g_o[hh]
                                f0 = hh * 4
                                self.op("dve", "scalar_tensor_tensor", [o, fin, self.smalls], [o], o[:, :], o[:, :],
                                        fin[:, f0 + 3:f0 + 4], glan[0:64, l * 128:(l + 1) * 128], op0=ALU.mult, op1=ALU.mult)
                                self.op("pool", "tensor_tensor", [o, self.g_g], [self.g_ost],
                                        self.g_ost[:, c, hh * 128:(hh + 1) * 128], o[:, :],
                                        self.g_g[:, c, hh * 128:(hh + 1) * 128], ALU.mult)
                    if d == 0:
                        self.dma("pool", self.s_gof[t0:t0 + GT, m * 256:(m + 1) * 256].rearrange("(c p) f -> p c f", p=CH),
                                 self.g_of[:, 0:NCH, :], [self.g_of], [self.b_gof])
                    else:
                        self.dma("pool", self.s_o[t0:t0 + GT, m * 256:(m + 1) * 256].rearrange("(c p) f -> p c f", p=CH),
                                 self.g_ost[:, 0:NCH, :], [self.g_ost], [Buf()])

    def ssd_mixer(self, l, S):
        GT = min(self.GT, S)
        NCH = GT // CH
        cw, cb = self.sm("convw"), self.sm("convb")
        self.S.label = "ssd_conv"
        for t0 in range(0, S, GT):
            lo, hi = max(t0 - 2, 0), min(t0 + GT + 2, S)
            for fc in range(6):
                kp = self.c_par % 2
                self.c_par += 1
                ci, acc, cv, cxt = self.c_in[kp], self.c_acc[kp], self.c_cv[kp], self.c_xt[kp]
                self.op("pool", "memset", [], [ci], ci[:, 0:2], 0.0)
                self.op("pool", "memset", [], [ci], ci[:, GT + 2:GT + 4], 0.0)
                self.dma("sp", ci[:, lo - (t0 - 2):hi - (t0 - 2)], self.s_sx[fc][:, lo:hi], [self.b_sx], [ci])
                wo = (l * 6 + fc) * 5
                self.op("dve", "tensor_scalar", [ci, self.smalls], [acc], acc[:, 0:GT], ci[:, 0:GT], cw[:, wo:wo + 1], None,
                        op0=ALU.mult)
                for k in range(1, 5):
                    self.op("dve", "scalar_tensor_tensor", [ci, self.smalls, acc], [acc], acc[:, 0:GT], ci[:, k:k + GT],
                            cw[:, wo + k:wo + k + 1], acc[:, 0:GT], op0=ALU.mult, op1=ALU.add)
                self.op("act", "activation", [acc, self.smalls], [cv], cv[:, 0:GT], acc[:, 0:GT], AF.Silu,
                        bias=cb[:, l * 6 + fc:l * 6 + fc + 1])
                if fc >= 4:
                    self.dma("pool", self.s_sbc[fc - 4][:, t0:t0 + GT], cv[:, 0:GT], [cv], [self.b_sbc])
                if fc <= 4:
                    pb = self.pstt[kp]
                    for c in range(NCH):
                        self.op("pe", "transpose", [cv, self.ident], [pb], pb[0:64, c * 128:(c + 1) * 128],
                                cv[:, c * CH:(c + 1) * CH], self.ident[:], signal=(c == NCH - 1))
                    self.op("act", "copy", [pb], [cxt], cxt[:, 0:NCH, :],
                            pb[0:64, 0:NCH * 128].rearrange("p (c f) -> p c f", f=128))
                    self.dma("pool", self.s_sxt[t0:t0 + GT, fc * 128:(fc + 1) * 128].rearrange("(c p) f -> p c f", p=CH),
                             cxt[:, 0:NCH, :], [cxt], [self.b_sxt])
        ssdD, ssdn = self.sm("ssdD"), self.sm("ssdn")
        self.S.label = "ssd_scan"
        idf = self.cst_f
        for d in range(2):
            r0 = d * 8
            self.op("dve", "memset", [], [self.s_Sf], self.s_Sf[:, :], 0.0)
            self.op("pool", "memset", [], [self.s_Sb], self.s_Sb[:, :], 0.0)
            tiles = list(range(0, S, GT))
            if d == 1:
                tiles = tiles[::-1]
            edge = CH - 1 if d == 0 else 0
            for t0 in tiles:
                la, dtt, cs, rcs, ac = self.s_la, self.s_dtt, self.s_cs, self.s_rcs, self.s_ac
                self.dma("sp", dtt[:, 0:GT], self.s_dt[0][:, t0:t0 + GT], [self.b_dt], [dtt])
                self.dma("sp", la[:, 0:GT], self.s_dt[1][:, t0:t0 + GT], [self.b_dt], [la])
                self.dma("sp", self.s_xt[:, 0:NCH, :], self.s_sxt[t0:t0 + GT, :].rearrange("(c p) f -> p c f", p=CH),
                         [self.b_sxt], [self.s_xt])
                self.dma("sp", self.s_bt[:, 0:GT], self.s_sbc[0][:, t0:t0 + GT], [self.b_sbc], [self.s_bt])
                self.dma("sp", self.s_ct[:, 0:GT], self.s_sbc[1][:, t0:t0 + GT], [self.b_sbc], [self.s_ct])
                self.op("dve", "tensor_tensor_scan", [self.g_mask, la], [cs], cs[:, 0:GT], self.g_mask[0:16, 0:GT],
                        la[:, 0:GT], 0.0, ALU.mult, ALU.add)
                cs3 = cs[:, 0:GT].rearrange("p (c t) -> p c t", t=CH)
                self.op("dve", "tensor_tensor", [la, cs], [rcs], rcs[:, 0:GT], la[:, 0:GT], cs[:, 0:GT], ALU.subtract)
                self.op("dve", "tensor_tensor", [rcs, cs], [rcs], rcs[:, 0:GT].rearrange("p (c t) -> p c t", t=CH),
                        rcs[:, 0:GT].rearrange("p (c t) -> p c t", t=CH),
                        cs3[:, :, CH - 1:CH].to_broadcast([16, NCH, CH]), ALU.add)
                self.op("dve", "tensor_scalar", [cs, self.hsel], [ac], ac[:, 0:GT], cs[:, 0:GT], self.hsel[:, 0:1], None,
                        op0=ALU.mult)
                self.op("dve", "scalar_tensor_tensor", [rcs, self.hsel, ac], [ac], ac[:, 0:GT], rcs[:, 0:GT],
                        self.hsel[:, 1:2], ac[:, 0:GT], op0=ALU.mult, op1=ALU.add)
                self.dma("pool", self.s_acd[:, t0:t0 + GT], ac[:, 0:GT], [ac], [self.b_acd])
                pq = self.ps[4]
                for c in range(NCH):
                    self.op("pe", "transpose", [ac, idf], [pq], pq[0:64, c * 32:c * 32 + 16], ac[:, c * CH:(c + 1) * CH],
                            idf[0:16, 0, 0:16], signal=False)
                    self.op("pe", "transpose", [dtt, idf], [pq], pq[0:64, c * 32 + 16:c * 32 + 32], dtt[:, c * CH:(c + 1) * CH],
                            idf[0:16, 0, 0:16], signal=(c == NCH - 1))
                self.op("dve", "tensor_copy", [pq], [self.s_tok], self.s_tok[:, 0:NCH, :],
                        pq[0:64, 0:NCH * 32].rearrange("p (c f) -> p c f", f=32))
                if d == 1:
                    self.dma("sp", self.s_yf[:, 0:NCH, :], self.s_yfd[t0:t0 + GT, :].rearrange("(c p) f -> p c f", p=CH),
                             [self.b_yfd], [self.s_yf])
                    self.dma("sp", self.s_z[:, 0:NCH, :], self.s_sz[t0:t0 + GT, :].rearrange("(c p) f -> p c f", p=CH),
                             [self.b_sz], [self.s_z])
                chunks = list(range(NCH))
                if d == 1:
                    chunks = chunks[::-1]
                xt = self.s_xt

                def stage1a(c, k):
                    cs_ = slice(c * CH, (c + 1) * CH)
                    seg, t1, t2, mt, sm_, xw = self.s_seg[k], self.s_t1[k], self.s_t2[k], self.s_mt[k], self.s_sm[k], self.s_xw[k]
                    self.dma("sp", seg[:, :, :], self.s_acd[r0:r0 + 8, t0 + c * CH:t0 + (c + 1) * CH].partition_broadcast(128),
                             [self.b_acd], [seg])
                    actok = self.s_tok[:, c, r0:r0 + 8]
                    self.op("dve", "tensor_tensor", [seg, self.s_tok], [t1], t1[:, :, :], seg[0:64, :, :],
                            actok.unsqueeze(2).to_broadcast([64, 8, CH]), ALU.subtract)
                    self.op("dve", "tensor_tensor", [seg, self.s_tok], [sm_], sm_[0:64, 8:16], seg[0:64, :, edge], actok,
                            ALU.subtract)
                    self.op("pool", "tensor_tensor", [t1, self.mbias], [t2], t2[:, :, :], t1[:, :, :],
                            self.mbias[:, d:d + 1, :].to_broadcast([64, 8, CH]), ALU.add)
                    self.op("act", "activation", [self.s_tok], [sm_], sm_[0:64, 0:8], actok, AF.Exp)
                    self.op("act", "activation", [sm_], [sm_], sm_[0:64, 16:24], sm_[0:64, 8:16], AF.Exp)
                    self.op("act", "activation", [seg], [sm_], sm_[:, 32:40], seg[:, :, edge], AF.Exp)
                    self.op("act", "activation", [t2], [t1], t1[:, :, :], t2[:, :, :], AF.Exp)
                    for g in range(2):
                        pcb = self.ps[4 + g]
                        self.op("pe", "matmul", [self.s_bt, self.s_ct], [pcb], pcb[0:64, 0:64],
                                self.s_bt[g * 64:(g + 1) * 64, cs_], self.s_ct[g * 64:(g + 1) * 64, cs_], start=True, stop=True)

                def stage1b(c, k):
                    cs_ = slice(c * CH, (c + 1) * CH)
                    seg, t1, t2, mt, sm_, xw = self.s_seg[k], self.s_t1[k], self.s_t2[k], self.s_mt[k], self.s_sm[k], self.s_xw[k]
                    actok = self.s_tok[:, c, r0:r0 + 8]
                    dttok = self.s_tok[:, c, 16 + r0:16 + r0 + 8]
                    self.op("dve", "tensor_tensor", [t1, self.s_tok], [t2], t2[:, :, :], t1[:, :, :],
                            dttok.unsqueeze(2).to_broadcast([64, 8, CH]), ALU.mult)
                    for g in range(2):
                        pcb = self.ps[4 + g]
                        self.op("dve", "tensor_tensor", [t2, pcb], [mt], mt[:, g * 4:(g + 1) * 4, :], t2[:, g * 4:(g + 1) * 4, :],
                                pcb[0:64, 0:64].unsqueeze(1).to_broadcast([64, 4, CH]), ALU.mult)
                    self.op("dve", "tensor_tensor", [sm_, self.s_tok], [sm_], sm_[0:64, 24:32], sm_[0:64, 16:24], dttok, ALU.mult)
                    py = self.ps[k]
                    for h in range(8):
                        self.op("pe", "matmul", [mt, xt], [py], py[0:64, h * 64:(h + 1) * 64], mt[:, h, :],
                                xt[:, c, h * 64:(h + 1) * 64], start=True, stop=True, signal=(h == 7))
                    self.op("dve", "tensor_tensor", [xt, sm_], [xw], xw[:, :].rearrange("p (h q) -> p h q", q=64),
                            xt[:, c, 0:512].rearrange("p (h q) -> p h q", q=64),
                            sm_[0:64, 24:32].unsqueeze(2).to_broadcast([64, 8, 64]), ALU.mult)
                    pu = self.ps[2 + k]
                    for g in range(2):
                        self.op("pe", "matmul", [xt, xw], [pu], pu[:, g * 256:(g + 1) * 256], xt[:, c, 512:640],
                                xw[:, g * 256:(g + 1) * 256], start=True, stop=True, signal=(g == 1))

                def stage2(c, k):
                    cs_ = slice(c * CH, (c + 1) * CH)
                    sm_ = self.s_sm[k]
                    py, pu = self.ps[k], self.ps[2 + k]
                    yi = self.s_yi
                    for g in range(2):
                        pi = self.psx if g == 0 else self.psy
                        self.op("pe", "matmul", [self.s_ct, self.s_Sb], [pi], pi[0:64, 0:256],
                                self.s_ct[g * 64:(g + 1) * 64, cs_], self.s_Sb[g * 64:(g + 1) * 64, :], start=True, stop=True)
                    for g in range(2):
                        pi = self.psx if g == 0 else self.psy
                        self.op("dve", "tensor_tensor", [pi, sm_], [yi],
                                yi[:, g * 256:(g + 1) * 256].rearrange("p (h q) -> p h q", q=64),
                                pi[0:64, 0:256].rearrange("p (h q) -> p h q", q=64),
                                sm_[0:64, g * 4:(g + 1) * 4].unsqueeze(2).to_broadcast([64, 4, 64]), ALU.mult)
                    for g in range(2):
                        p0 = g * 64
                        Sg = self.s_Sf[p0:p0 + 64, :].rearrange("p (h q) -> p h q", q=64)
                        self.op("dve", "tensor_tensor", [self.s_Sf, sm_], [self.s_Sf], Sg, Sg,
                                sm_[p0:p0 + 64, 32 + g * 4:32 + g * 4 + 4].unsqueeze(2).to_broadcast([64, 4, 64]), ALU.mult)
                        self.op("dve", "tensor_tensor", [self.s_Sf, pu], [self.s_Sf], self.s_Sf[p0:p0 + 64, :],
                                self.s_Sf[p0:p0 + 64, :], pu[p0:p0 + 64, g * 256:(g + 1) * 256], ALU.add)
                    self.op("act", "copy", [self.s_Sf], [self.s_Sb], self.s_Sb[:, :], self.s_Sf[:, :])
                    if d == 0:
                        self.op("dve", "tensor_tensor", [py, yi], [self.s_yf], self.s_yf[:, c, :], py[0:64, :], yi[:, :], ALU.add)
                    else:
                        y, y2, fin = self.s_y, self.s_y2, self.g_fin
                        self.op("dve", "tensor_tensor", [py, yi], [y], y[:, :], py[0:64, :], yi[:, :], ALU.add)
                        self.op("pool", "tensor_tensor", [y, self.s_yf], [y], y[:, :], y[:, :], self.s_yf[:, c, :], ALU.add)
                        self.op("dve", "tensor_tensor", [xt, self.smalls], [y2], y2[:, :].rearrange("p (h q) -> p h q", q=64),
                                xt[:, c, 0:512].rearrange("p (h q) -> p h q", q=64),
                                ssdD[0:64, l * 8:(l + 1) * 8].unsqueeze(2).to_broadcast([64, 8, 64]), ALU.mult)
                        self.op("pool", "tensor_tensor", [y, y2], [y], y[:, :], y[:, :], y2[:, :], ALU.add)
                        self.op("pool", "tensor_tensor", [y, self.s_z], [y], y[:, :], y[:, :], self.s_z[:, c, :], ALU.mult)
                        self.op("act", "activation", [y], [self.s_junk, fin], self.s_junk[:, :], y[:, :], AF.Square,
                                accum_out=fin[:, 0:1])
                        self.op("dve", "tensor_scalar", [fin], [fin], fin[:, 1:2], fin[:, 0:1], 1.0 / 512, EPS,
                                op0=ALU.mult, op1=ALU.add)
                        self.op("act", "activation", [fin], [fin], fin[:, 2:3], fin[:, 1:2], AF.Ln)
                        self.op("act", "activation", [fin], [fin], fin[:, 3:4], fin[:, 2:3], AF.Exp, scale=-0.5)
                        self.op("dve", "scalar_tensor_tensor", [y, fin, self.smalls], [self.s_ost], self.s_ost[:, c, :], y[:, :],
                                fin[:, 3:4], ssdn[0:64, l * 512:(l + 1) * 512], op0=ALU.mult, op1=ALU.mult)

                par0 = self.s_par
                stage1a(chunks[0], par0 % 2)
                stage1b(chunks[0], par0 % 2)
                for i, c in enumerate(chunks):
                    if i + 1 < len(chunks):
                        stage1a(chunks[i + 1], (par0 + i + 1) % 2)
                    stage2(c, (par0 + i) % 2)
                    if i + 1 < len(chunks):
                        stage1b(chunks[i + 1], (par0 + i + 1) % 2)
                self.s_par = par0 + len(chunks)
                if d == 0:
                    self.dma("pool", self.s_yfd[t0:t0 + GT, :].rearrange("(c p) f -> p c f", p=CH), self.s_yf[:, 0:NCH, :],
                             [self.s_yf], [self.b_yfd])
                else:
                    self.dma("pool", self.s_o[t0:t0 + GT, 1024:1536].rearrange("(c p) f -> p c f", p=CH),
                             self.s_ost[:, 0:NCH, :], [self.s_ost], [Buf()])


_PROG = {}


def kernel(**inputs):
    inp = {k: np.asarray(v) for k, v in inputs.items()}
    xp, xs = inp["x_prompt"], inp["x_sample"]
    NCORE = 8
    npp, nps = xp.shape[0] // NCORE, xs.shape[0] // NCORE
    SP, SS = xp.shape[1], xs.shape[1]
    L = inp["w_in"].shape[0]
    key = (npp, SP, nps, SS, L)
    if key not in _PROG:
        prog = Prog([("x_p", "y_p", npp, SP), ("x_s", "y_s", nps, SS)], depth=L)
        _PROG[key] = prog.build()
    nc = _PROG[key]
    small, _ = pack_smalls({k: inp[k] for k in SMALL}, L)
    consts = host_consts(max(SP, SS))
    in_maps = []
    for c in range(NCORE):
        m = {"x_p": np.ascontiguousarray(xp[c * npp:(c + 1) * npp]),
             "x_s": np.ascontiguousarray(xs[c * nps:(c + 1) * nps]), "smalls": small}
        for k in WNAMES:
            m[k] = inp[k]
        m.update(consts)
        in_maps.append(m)
    res = run_bass_kernel_spmd(nc, in_maps, core_ids=list(range(NCORE)))
    yp = np.concatenate([r["y_p"] for r in res.results], 0).astype(np.float32)
    ys = np.concatenate([r["y_s"] for r in res.results], 0).astype(np.float32)
    return (yp, ys)
```
